# Optimizing a Trainium2 kernel written in Bass

```python
import math
import jax, jax.numpy as jnp
from jax import lax
import numpy as np

D_MODEL = 2048
BATCH = 4
SEQ = 8192
DEPTH = 4
DEC_BATCH = 8
DEC_SEQ = 32
PAST_LEN = 1024

CHUNK = 64
N_MIXERS = 4
N_A = (DEPTH + 3) // 4
N_B = (DEPTH + 2) // 4
N_C = (DEPTH + 1) // 4
N_D = DEPTH // 4

A_BLOCK = 128
A_GROUPS = 8
A_GROUP_DIM = D_MODEL // A_GROUPS
B_HEADS = 8
B_DV = D_MODEL // B_HEADS
B_DQK = B_DV // 2
C_HEADS = 8
C_DK = D_MODEL // C_HEADS
C_DV = 2 * C_DK
ROPE_BASE = 10000.0
D_HEADS = 16
D_HEAD_DIM = D_MODEL // D_HEADS
Q_BLOCK = 128
FFN_HIDDEN = ((8 * D_MODEL + 3 * 256 - 1) // (3 * 256)) * 256
ALPHA = (2.0 * DEPTH) ** 0.25
BETA = (8.0 * DEPTH) ** -0.25
LN_EPS = 1e-5

kernel_name = 'hybrid_streaming_encoder_step'


def layer_norm(x, g, b):
    xf = x.astype(jnp.float32)
    mu = jnp.mean(xf, axis=-1, keepdims=True)
    var = jnp.mean(jnp.square(xf - mu), axis=-1, keepdims=True)
    y = (xf - mu) * lax.rsqrt(var + LN_EPS)
    return (y * g.astype(jnp.float32) + b.astype(jnp.float32)).astype(x.dtype)


def head_rms_norm(h, g):
    y = h * lax.rsqrt(jnp.mean(h * h, axis=-1, keepdims=True) + LN_EPS)
    return y.reshape(*h.shape[:-2], -1) * g.astype(jnp.float32)


def head_layer_norm(h, g, b):
    mu = jnp.mean(h, axis=-1, keepdims=True)
    var = jnp.mean(jnp.square(h - mu), axis=-1, keepdims=True)
    y = ((h - mu) * lax.rsqrt(var + LN_EPS)).reshape(*h.shape[:-2], -1)
    return y * g.astype(jnp.float32) + b.astype(jnp.float32)


def rotary(x, pos):
    half = x.shape[-1] // 2
    inv = ROPE_BASE ** (-jnp.arange(half, dtype=jnp.float32) / half)
    ang = pos.astype(jnp.float32)[:, None] * inv[None, :]
    cos = jnp.cos(ang)[None, :, None, :]
    sin = jnp.sin(ang)[None, :, None, :]
    x1, x2 = x[..., :half], x[..., half:]
    return jnp.concatenate([x1 * cos - x2 * sin, x1 * sin + x2 * cos], axis=-1)


def to_chunks(a, block):
    b, t = a.shape[:2]
    return jnp.moveaxis(a.reshape(b, t // block, block, *a.shape[2:]), 1, 0)


def from_chunks(a):
    nc, b, l = a.shape[:3]
    return jnp.moveaxis(a, 0, 1).reshape(b, nc * l, *a.shape[3:])


def post_norm(x, sub, g, b):
    return layer_norm(ALPHA * x + sub, g, b)


def swiglu(x, w_in, w_out):
    gate, up = jnp.split(x @ w_in, 2, axis=-1)
    return (jax.nn.silu(gate) * up) @ w_out


def gmlp_mixer(x, w_in, b_in, vn_g, vn_b, w_s, b_s, w_out, block):
    bsz, t, _ = x.shape
    z = jax.nn.gelu(x @ w_in + b_in)
    u, v = jnp.split(z, 2, axis=-1)
    v = layer_norm(v, vn_g, vn_b)
    pos = jnp.arange(block)
    mask = (pos[None, :] // CHUNK) <= (pos[:, None] // CHUNK)
    ws = jnp.where(mask[None], w_s[:, :block, :block], 0.0).astype(x.dtype)
    vb = v.reshape(bsz, t // block, block, A_GROUPS, A_GROUP_DIM)
    s = jnp.einsum('gpq,bcqge->bcpge', ws, vb) + jnp.transpose(b_s[:, :block])[None, None, :, :, None]
    y = u * s.reshape(bsz, t, D_MODEL)
    return y @ w_out, v


def mlstm_mixer(x, w_in, b_gates, norm_g, w_out, c0, n0, m0, block):
    f32 = jnp.float32
    bsz, t, _ = x.shape
    hq = B_HEADS * B_DQK
    q, k, v, og, gates = jnp.split(x @ w_in, [hq, 2 * hq, 2 * hq + D_MODEL, 2 * hq + 2 * D_MODEL], axis=-1)
    q = q.reshape(bsz, t, B_HEADS, B_DQK).astype(f32) * (B_DQK ** -0.5)
    k = k.reshape(bsz, t, B_HEADS, B_DQK).astype(f32)
    v = v.reshape(bsz, t, B_HEADS, B_DV).astype(f32)
    gates = gates.astype(f32) + b_gates.astype(f32)
    log_i, f_pre = jnp.split(gates, 2, axis=-1)
    log_f = jax.nn.log_sigmoid(f_pre)
    xs = tuple(to_chunks(a, block) for a in (q, k, v, log_i, log_f))
    causal = jnp.tril(jnp.ones((block, block), dtype=bool))

    def step(carry, inp):
        c, n, m = carry
        qc, kc, vc, ic, fc = inp
        bcum = jnp.cumsum(fc, axis=1)
        dmat = bcum[:, :, None, :] - bcum[:, None, :, :] + ic[:, None, :, :]
        dmat = jnp.where(causal[None, :, :, None], dmat, -jnp.inf)
        inter = bcum + m[:, None, :]
        m_t = jnp.maximum(inter, jnp.max(dmat, axis=2))
        w_intra = jnp.exp(dmat - m_t[:, :, None, :])
        w_inter = jnp.exp(inter - m_t)
        scores = jnp.einsum('bthd,bshd->btsh', qc, kc) * w_intra
        num = jnp.einsum('btsh,bshe->bthe', scores, vc) + w_inter[..., None] * jnp.einsum('bhed,bthd->bthe', c, qc)
        den = jnp.sum(scores, axis=2) + w_inter * jnp.einsum('bhd,bthd->bth', n, qc)
        h = num / jnp.maximum(jnp.abs(den), jnp.exp(-m_t))[..., None]
        m_new = m_t[:, -1]
        b_last = bcum[:, -1]
        w_c = jnp.exp(b_last + m - m_new)
        w_s = jnp.exp(b_last[:, None, :] - bcum + ic - m_new[:, None, :])
        c_new = w_c[..., None, None] * c + jnp.einsum('bsh,bshe,bshd->bhed', w_s, vc, kc)
        n_new = w_c[..., None] * n + jnp.einsum('bsh,bshd->bhd', w_s, kc)
        return (c_new, n_new, m_new), h

    (c, n, m), hs = lax.scan(step, (c0.astype(f32), n0.astype(f32), m0.astype(f32)), xs)
    h = from_chunks(hs)
    h = head_rms_norm(h, norm_g) * jax.nn.sigmoid(og.astype(f32))
    return h.astype(x.dtype) @ w_out, c, n, m


def retention_mixer(x, w_in, gn_g, gn_b, w_out, s0, pos0, block):
    f32 = jnp.float32
    bsz, t, _ = x.shape
    hk = C_HEADS * C_DK
    q, k, v, g = jnp.split(x @ w_in, [hk, 2 * hk, 2 * hk + C_HEADS * C_DV], axis=-1)
    pos = pos0 + jnp.arange(t)
    q = rotary(q.reshape(bsz, t, C_HEADS, C_DK).astype(f32), pos)
    k = rotary(k.reshape(bsz, t, C_HEADS, C_DK).astype(f32), pos) * (C_DK ** -0.5)
    v = v.reshape(bsz, t, C_HEADS, C_DV).astype(f32)
    log_gamma = jnp.log1p(-(2.0 ** (-5.0 - jnp.arange(C_HEADS, dtype=f32))))
    idx = jnp.arange(block, dtype=f32)
    causal = idx[:, None] >= idx[None, :]
    decay_intra = jnp.where(causal[:, :, None], jnp.exp((idx[:, None] - idx[None, :])[:, :, None] * log_gamma), 0.0)
    decay_q = jnp.exp((idx + 1.0)[:, None] * log_gamma)
    decay_k = jnp.exp((block - 1.0 - idx)[:, None] * log_gamma)
    decay_s = jnp.exp(block * log_gamma)
    xs = tuple(to_chunks(a, block) for a in (q, k, v))

    def step(s, inp):
        qc, kc, vc = inp
        scores = jnp.einsum('bthd,bshd->btsh', qc, kc) * decay_intra
        o = jnp.einsum('btsh,bshe->bthe', scores, vc) + jnp.einsum('bthd,bhde->bthe', qc, s) * decay_q[None, :, :, None]
        s_new = decay_s[None, :, None, None] * s + jnp.einsum('bshd,sh,bshe->bhde', kc, decay_k, vc)
        return s_new, o

    s_fin, outs = lax.scan(step, s0.astype(f32), xs)
    o = head_layer_norm(from_chunks(outs), gn_g, gn_b)
    y = (jax.nn.silu(g.astype(f32)) * o).astype(x.dtype) @ w_out
    return y, s_fin


def fox_project(x, w_in, b_f):
    f32 = jnp.float32
    bsz, t, _ = x.shape
    q, k, v, f_pre = jnp.split(x @ w_in, [D_MODEL, 2 * D_MODEL, 3 * D_MODEL], axis=-1)
    q = q.reshape(bsz, t, D_HEADS, D_HEAD_DIM).astype(f32) * (D_HEAD_DIM ** -0.5)
    k = k.reshape(bsz, t, D_HEADS, D_HEAD_DIM).astype(f32)
    v = v.reshape(bsz, t, D_HEADS, D_HEAD_DIM).astype(f32)
    logf = jax.nn.log_sigmoid(f_pre.astype(f32) + b_f.astype(f32))
    return q, k, v, logf


def fox_prompt(x, w_in, b_f, w_out):
    bsz, t, _ = x.shape
    q, k, v, logf = fox_project(x, w_in, b_f)
    c_t = jnp.transpose(jnp.cumsum(logf, axis=1), (0, 2, 1))
    kpos = jnp.arange(t)

    def one_block(blk):
        start = blk * Q_BLOCK
        qb = lax.dynamic_slice_in_dim(q, start, Q_BLOCK, axis=1)
        cq = lax.dynamic_slice_in_dim(c_t, start, Q_BLOCK, axis=2)
        qpos = start + jnp.arange(Q_BLOCK)
        logits = jnp.einsum('bqhd,bkhd->bhqk', qb, k) + cq[..., :, None] - c_t[..., None, :]
        logits = jnp.where(kpos[None, :] <= qpos[:, None], logits, -jnp.inf)
        p = jax.nn.softmax(logits, axis=-1)
        return jnp.einsum('bhqk,bkhd->bqhd', p, v)

    o = lax.map(one_block, jnp.arange(t // Q_BLOCK))
    o = jnp.moveaxis(o, 0, 1).reshape(bsz, t, D_MODEL)
    return o.astype(x.dtype) @ w_out, k, v, logf


def fox_sample(x, w_in, b_f, w_out, k_cache, v_cache, logf_cache):
    f32 = jnp.float32
    bsz, t, _ = x.shape
    past = k_cache.shape[1]
    q, k, v, logf = fox_project(x, w_in, b_f)
    k_all = jnp.concatenate([k_cache.astype(f32), k], axis=1)
    v_all = jnp.concatenate([v_cache.astype(f32), v], axis=1)
    c_all = jnp.transpose(jnp.cumsum(jnp.concatenate([logf_cache.astype(f32), logf], axis=1), axis=1), (0, 2, 1))
    logits = jnp.einsum('bqhd,bkhd->bhqk', q, k_all) + c_all[..., past:, None] - c_all[..., None, :]
    kpos = jnp.arange(past + t)
    qpos = past + jnp.arange(t)
    logits = jnp.where(kpos[None, :] <= qpos[:, None], logits, -jnp.inf)
    p = jax.nn.softmax(logits, axis=-1)
    o = jnp.einsum('bhqk,bkhd->bqhd', p, v_all).reshape(bsz, t, D_MODEL)
    return o.astype(x.dtype) @ w_out, k, v, logf


def setup_inputs(seed: int = 0) -> dict:
    key = jax.random.key(seed)
    ks = iter(jax.random.split(key, 48))
    f32 = jnp.float32
    D = D_MODEL

    def nrm(shape, scale):
        return jax.random.normal(next(ks), shape, f32) * scale

    inputs = {}
    inputs['x_prompt'] = nrm((BATCH, SEQ, D), 1.0)
    inputs['x_sample'] = nrm((DEC_BATCH, DEC_SEQ, D), 1.0)
    inputs['state_b_C'] = nrm((N_B, DEC_BATCH, B_HEADS, B_DV, B_DQK), 0.5)
    inputs['state_b_n'] = nrm((N_B, DEC_BATCH, B_HEADS, B_DQK), 0.5)
    inputs['state_b_m'] = nrm((N_B, DEC_BATCH, B_HEADS), 1.0)
    inputs['state_c_S'] = nrm((N_C, DEC_BATCH, C_HEADS, C_DK, C_DV), 1.0)
    inputs['cache_d_k'] = nrm((N_D, DEC_BATCH, PAST_LEN, D_HEADS, D_HEAD_DIM), 1.0)
    inputs['cache_d_v'] = nrm((N_D, DEC_BATCH, PAST_LEN, D_HEADS, D_HEAD_DIM), 1.0)
    inputs['cache_d_logf'] = jax.nn.log_sigmoid(nrm((N_D, DEC_BATCH, PAST_LEN, D_HEADS), 1.0) + 3.0)
    inputs['a_w_in'] = nrm((N_A, D, 2 * D), D ** -0.5)
    inputs['a_b_in'] = nrm((N_A, 2 * D), 0.02)
    inputs['a_vn_g'] = 1.0 + nrm((N_A, D), 0.02)
    inputs['a_vn_b'] = nrm((N_A, D), 0.02)
    inputs['a_w_s'] = nrm((N_A, A_GROUPS, A_BLOCK, A_BLOCK), A_BLOCK ** -0.5)
    inputs['a_b_s'] = 1.0 + nrm((N_A, A_GROUPS, A_BLOCK), 0.02)
    inputs['a_w_out'] = nrm((N_A, D, D), BETA * D ** -0.5)
    inputs['b_w_in'] = nrm((N_B, D, 2 * B_HEADS * B_DQK + 2 * D + 2 * B_HEADS), D ** -0.5)
    inputs['b_b_gates'] = jnp.concatenate([nrm((N_B, B_HEADS), 0.01), jnp.linspace(3.0, 6.0, B_HEADS, dtype=f32)[None, :] + nrm((N_B, B_HEADS), 0.01)], axis=-1)
    inputs['b_norm_g'] = 1.0 + nrm((N_B, D), 0.02)
    inputs['b_w_out'] = nrm((N_B, D, D), BETA * D ** -0.5)
    inputs['c_w_in'] = nrm((N_C, D, 2 * C_HEADS * C_DK + 2 * C_HEADS * C_DV), D ** -0.5)
    inputs['c_gn_g'] = 1.0 + nrm((N_C, C_HEADS * C_DV), 0.02)
    inputs['c_gn_b'] = nrm((N_C, C_HEADS * C_DV), 0.02)
    inputs['c_w_out'] = nrm((N_C, C_HEADS * C_DV, D), BETA * (C_HEADS * C_DV) ** -0.5)
    inputs['d_w_in'] = nrm((N_D, D, 3 * D + D_HEADS), D ** -0.5)
    inputs['d_b_f'] = jnp.linspace(1.0, 4.0, D_HEADS, dtype=f32)[None, :] + nrm((N_D, D_HEADS), 0.01)
    inputs['d_w_out'] = nrm((N_D, D, D), BETA * D ** -0.5)
    inputs['ffn_w_in'] = nrm((DEPTH, D, 2 * FFN_HIDDEN), D ** -0.5)
    inputs['ffn_w_out'] = nrm((DEPTH, FFN_HIDDEN, D), BETA * FFN_HIDDEN ** -0.5)
    inputs['ln1_g'] = 1.0 + nrm((DEPTH, D), 0.02)
    inputs['ln1_b'] = nrm((DEPTH, D), 0.02)
    inputs['ln2_g'] = 1.0 + nrm((DEPTH, D), 0.02)
    inputs['ln2_b'] = nrm((DEPTH, D), 0.02)
    return inputs


def reference(x_prompt, x_sample, state_b_C, state_b_n, state_b_m, state_c_S, cache_d_k, cache_d_v, cache_d_logf,
              a_w_in, a_b_in, a_vn_g, a_vn_b, a_w_s, a_b_s, a_w_out,
              b_w_in, b_b_gates, b_norm_g, b_w_out,
              c_w_in, c_gn_g, c_gn_b, c_w_out,
              d_w_in, d_b_f, d_w_out,
              ffn_w_in, ffn_w_out, ln1_g, ln1_b, ln2_g, ln2_b):
    f32 = jnp.float32
    xp, xs = x_prompt, x_sample
    bp, tp = xp.shape[0], xp.shape[1]
    ts = xs.shape[1]
    a_vs = []
    b_cp, b_np, b_mp, b_cs, b_ns, b_ms = [], [], [], [], [], []
    c_sp, c_ss = [], []
    d_kp, d_vp, d_fp, d_ks, d_vs, d_fs = [], [], [], [], [], []
    for i in range(DEPTH):
        kind, j = i % N_MIXERS, i // N_MIXERS
        if kind == 0:
            mp, _ = gmlp_mixer(xp, a_w_in[j], a_b_in[j], a_vn_g[j], a_vn_b[j], a_w_s[j], a_b_s[j], a_w_out[j], A_BLOCK)
            ms, v_rows = gmlp_mixer(xs, a_w_in[j], a_b_in[j], a_vn_g[j], a_vn_b[j], a_w_s[j], a_b_s[j], a_w_out[j], ts)
            a_vs.append(v_rows)
        elif kind == 1:
            zc = jnp.zeros((bp, B_HEADS, B_DV, B_DQK), f32)
            zn = jnp.zeros((bp, B_HEADS, B_DQK), f32)
            zm = jnp.zeros((bp, B_HEADS), f32)
            mp, cp, np_, mp_state = mlstm_mixer(xp, b_w_in[j], b_b_gates[j], b_norm_g[j], b_w_out[j], zc, zn, zm, CHUNK)
            ms, cs, ns, ms_state = mlstm_mixer(xs, b_w_in[j], b_b_gates[j], b_norm_g[j], b_w_out[j], state_b_C[j], state_b_n[j], state_b_m[j], ts)
            b_cp.append(cp); b_np.append(np_); b_mp.append(mp_state)
            b_cs.append(cs); b_ns.append(ns); b_ms.append(ms_state)
        elif kind == 2:
            zs = jnp.zeros((bp, C_HEADS, C_DK, C_DV), f32)
            mp, sp = retention_mixer(xp, c_w_in[j], c_gn_g[j], c_gn_b[j], c_w_out[j], zs, 0, CHUNK)
            ms, ss = retention_mixer(xs, c_w_in[j], c_gn_g[j], c_gn_b[j], c_w_out[j], state_c_S[j], PAST_LEN, ts)
            c_sp.append(sp); c_ss.append(ss)
        else:
            mp, kp, vp, fp = fox_prompt(xp, d_w_in[j], d_b_f[j], d_w_out[j])
            ms, kn, vn, fn = fox_sample(xs, d_w_in[j], d_b_f[j], d_w_out[j], cache_d_k[j], cache_d_v[j], cache_d_logf[j])
            d_kp.append(kp); d_vp.append(vp); d_fp.append(fp)
            d_ks.append(kn); d_vs.append(vn); d_fs.append(fn)
        xp = post_norm(xp, mp, ln1_g[i], ln1_b[i])
        xs = post_norm(xs, ms, ln1_g[i], ln1_b[i])
        xp = post_norm(xp, swiglu(xp, ffn_w_in[i], ffn_w_out[i]), ln2_g[i], ln2_b[i])
        xs = post_norm(xs, swiglu(xs, ffn_w_in[i], ffn_w_out[i]), ln2_g[i], ln2_b[i])
    return (xp, xs, jnp.stack(a_vs),
            jnp.stack(b_cp), jnp.stack(b_np), jnp.stack(b_mp),
            jnp.stack(b_cs), jnp.stack(b_ns), jnp.stack(b_ms),
            jnp.stack(c_sp), jnp.stack(c_ss),
            jnp.stack(d_kp), jnp.stack(d_vp), jnp.stack(d_fp),
            jnp.stack(d_ks), jnp.stack(d_vs), jnp.stack(d_fs))
```

```python
import math
from contextlib import ExitStack

import numpy as np
import concourse.bass as bass
import concourse.mybir as mybir
from concourse.bass_utils import run_bass_kernel_spmd

F32 = mybir.dt.float32
BF16 = mybir.dt.bfloat16
AF = mybir.ActivationFunctionType
ALU = mybir.AluOpType

D = 2048
KC = D // 128
FFN_H = 5632
DEPTH = 4
ALPHA = (2.0 * DEPTH) ** 0.25
LN_EPS = 1e-5
PAST = 1024


class Buf:
    __slots__ = ("w", "r", "t")

    def __init__(self, t=None):
        self.w = None
        self.r = {}
        self.t = t


class PEProxy:
    def __init__(self, eng):
        self._e = eng
        self.last_stop = True

    def matmul(self, *a, **k):
        self.last_stop = bool(k.get("stop", True))
        return self._e.matmul(*a, **k)

    def transpose(self, *a, **k):
        self.last_stop = True
        return self._e.transpose(*a, **k)

    def wait_ge(self, *a, **k):
        return self._e.wait_ge(*a, **k)


class Ctx:
    def __init__(self, nc, es):
        self.nc = nc
        self.es = es
        self.E = {"pe": PEProxy(nc.tensor), "act": nc.scalar, "dve": nc.vector, "pool": nc.gpsimd, "sp": nc.sync}
        self.sems = {}
        self.cnt = {}
        for e in self.E:
            self.sems[e] = es.enter_context(nc.semaphore("s_" + e))
            self.cnt[e] = 0
        self.dq = {}
        for q, n in (("sp", 16), ("pool", 8), ("act", 4)):
            names = []
            for i in range(n):
                nm = "d_%s%d" % (q, i)
                self.sems[nm] = es.enter_context(nc.semaphore(nm))
                self.cnt[nm] = 0
                names.append(nm)
            self.dq[q] = [names, 0]
        self.known = {e: {} for e in self.E}
        self.n_ins = 0

    def sb(self, name, shape, dt, es=None):
        self.n_sb = getattr(self, "n_sb", 0) + 1
        t = (es or self.es).enter_context(self.nc.sbuf_tensor("%s_%d" % (name, self.n_sb), list(shape), dt))
        return Buf(t)

    def _wait(self, e, deps):
        kn = self.known[e]
        for s, v in deps.items():
            if s == "pe" and e == "pe":
                continue
            if kn.get(s, 0) >= v:
                continue
            self.E[e].wait_ge(self.sems[s], v)
            kn[s] = v
            self.n_ins += 1

    @staticmethod
    def _deps(reads, writes):
        deps = {}
        for b in reads:
            if b.w is not None and deps.get(b.w[0], 0) < b.w[1]:
                deps[b.w[0]] = b.w[1]
        for b in writes:
            if b.w is not None and deps.get(b.w[0], 0) < b.w[1]:
                deps[b.w[0]] = b.w[1]
            for s, v in b.r.items():
                if deps.get(s, 0) < v:
                    deps[s] = v
        return deps

    @staticmethod
    def _record(tok, reads, writes):
        s, v = tok
        for b in reads:
            if b.r.get(s, 0) < v:
                b.r[s] = v
        for b in writes:
            b.w = tok
            b.r = {}

    def op(self, e, fn, reads=(), writes=()):
        deps = self._deps(reads, writes)
        self._wait(e, deps)
        ins = fn(self.E[e])
        self.n_ins += 1
        if e == "pe" and not self.E["pe"].last_stop:
            self._record((e, self.cnt[e] + 1), reads, writes)
            return
        self.cnt[e] += 1
        ins.then_inc(self.sems[e], 1)
        self._record((e, self.cnt[e]), reads, writes)

    def dma(self, q, out, in_, reads=(), writes=()):
        names, idx = self.dq[q]
        nm = names[idx % len(names)]
        self.dq[q][1] = idx + 1
        deps = self._deps(reads, writes)
        if self.cnt[nm] > 0 and deps.get(nm, 0) < self.cnt[nm]:
            deps[nm] = self.cnt[nm]
        self._wait(q, deps)
        ins = self.E[q].dma_start(out=out, in_=in_)
        self.cnt[nm] += 16
        ins.then_inc(self.sems[nm], 16)
        self.n_ins += 1
        self._record((nm, self.cnt[nm]), reads, writes)

    def barrier(self):
        allc = {s: v for s, v in self.cnt.items() if v > 0}
        for e in self.E:
            self._wait(e, allc)

    def finish(self):
        allc = {s: v for s, v in self.cnt.items() if v > 0}
        self._wait("sp", allc)


def bcast_rows(ap_row, nparts):
    return ap_row.partition_broadcast(nparts)


class Seg:
    def __init__(self, name, T, TT):
        self.name = name
        self.T = T
        self.TT = TT
        self.P = min(128, TT)
        self.NS = TT // self.P
        self.ntiles = T // TT


class Prog:
    def __init__(self, cfg):
        self.cfg = cfg
        self.TP = cfg["TP"]
        self.TS = cfg.get("TS", 32)
        self.layers = cfg.get("layers", [0, 1, 2, 3])
        self.do_ffn = cfg.get("ffn", True)
        self.nc = bass.Bass("TRN2", target_bir_lowering=False)
        self.dram = {}

    def din(self, name, shape, dt=F32):
        t = self.nc.dram_tensor(name, list(shape), dt, kind="ExternalInput")
        self.dram[name] = t
        return t

    def dout(self, name, shape, dt=F32):
        t = self.nc.dram_tensor(name, list(shape), dt, kind="ExternalOutput")
        self.dram[name] = t
        return t

    def dscr(self, name, shape, dt):
        t = self.nc.dram_tensor(name, list(shape), dt)
        self.dram[name] = t
        return t

    def convert_weight(self, c, src, dst, K, N):
        rows = max(128, (1 << 20) // N // 128 * 128)
        b = Buf()
        r0 = 0
        while r0 < K:
            r1 = min(K, r0 + rows)
            c.dma("pool", dst.ap()[r0:r1, :], src.ap()[r0:r1, :], writes=(b,))
            r0 = r1
        return b

    def ln_epilogue(self, c, S, es, zb, z_ap, gbc, bbc, xn_bufs, out_tm_ap, out_T, t0, P, ident, psT, tcnt):
        st, mv, rs, nmr, xn, xTo = xn_bufs
        for q in range(4):
            c.op("dve", lambda e, q=q: e.bn_stats(out=st.t[:P, q, :], in_=z_ap[:, q * 512:(q + 1) * 512]),
                 reads=(zb,), writes=(st,))
        c.op("dve", lambda e: e.bn_aggr(out=mv.t[:P, :], in_=st.t[:P].rearrange("p a b -> p (a b)")),
             reads=(st,), writes=(mv,))
        c.op("act", lambda e: e.activation(out=rs.t[:P, :], in_=mv.t[:P, 1:2], func=AF.Sqrt, bias=self.eps_t.t[:P, :],
                                           scale=1.0), reads=(mv, self.eps_t), writes=(rs,))
        c.op("dve", lambda e: e.reciprocal(out=rs.t[:P, :], in_=rs.t[:P, :]), reads=(rs,), writes=(rs,))
        c.op("dve", lambda e: e.scalar_tensor_tensor(out=nmr.t[:P, :], in0=mv.t[:P, 0:1], scalar=-1.0,
                                                     in1=rs.t[:P, :], op0=ALU.mult, op1=ALU.mult),
             reads=(mv, rs), writes=(nmr,))
        c.op("act", lambda e: e.activation(out=xn.t[:P, :], in_=z_ap, func=AF.Identity, bias=nmr.t[:P, 0:1],
                                           scale=rs.t[:P, 0:1]), reads=(zb, rs, nmr), writes=(xn,))
        c.op("dve", lambda e: e.tensor_tensor(out=xn.t[:P, :], in0=xn.t[:P, :], in1=gbc.t[:P, :], op=ALU.mult),
             reads=(xn, gbc), writes=(xn,))
        c.op("pool", lambda e: e.tensor_tensor(out=xn.t[:P, :], in0=xn.t[:P, :], in1=bbc.t[:P, :], op=ALU.add),
             reads=(xn, bbc), writes=(xn,))
        c.dma("sp", out_tm_ap, xn.t[:P, :], reads=(xn,))
        if out_T is None:
            return
        for g4 in range(4):
            pb = psT[tcnt[0] % len(psT)]
            tcnt[0] += 1
            for j in range(4):
                kc = g4 * 4 + j
                c.op("pe", lambda e, kc=kc, j=j, pb=pb: e.transpose(out=pb.t[:, j * P:(j + 1) * P],
                                                                      in_=xn.t[:P, kc * 128:(kc + 1) * 128],
                                                                      identity=ident.t[:P, :P]),
                     reads=(xn, ident), writes=(pb,))
            eng = "act" if g4 % 2 == 0 else "dve"
            if eng == "act":
                c.op("act", lambda e, g4=g4, pb=pb: e.copy(out=xTo.t[:, g4 * 4:(g4 + 1) * 4, :P],
                                                            in_=pb.t[:, :4 * P].rearrange("p (a b) -> p a b", b=P)),
                     reads=(pb,), writes=(xTo,))
            else:
                c.op("dve", lambda e, g4=g4, pb=pb: e.tensor_copy(out=xTo.t[:, g4 * 4:(g4 + 1) * 4, :P],
                                                                   in_=pb.t[:, :4 * P].rearrange("p (a b) -> p a b", b=P)),
                     reads=(pb,), writes=(xTo,))
        c.dma("sp", out_T.ap().rearrange("(kc p) t -> p kc t", p=128)[:, :, t0:t0 + P], xTo.t[:, :, :P], reads=(xTo,))

    def out_proj_ln(self, c, S, es, hT, nkc, wout_b, wout_ready, xres, gbc, bbc, lnb, x_out, xT_out, tile, ident,
                    ps_o, psT, wo_slots, cnts):
        P, NS, TT = S.P, S.NS, S.TT
        t0 = tile * TT
        OG = 256
        for og in range(D // OG):
            wo = wo_slots[cnts["wo"] % 2]
            cnts["wo"] += 1
            c.dma("sp", wo.t[:, :nkc, :],
                  wout_b.ap().rearrange("(kc p) n -> p kc n", p=128)[:, :, og * OG:(og + 1) * OG],
                  reads=(wout_ready,), writes=(wo,))
            for s in range(NS):
                pb = ps_o[cnts["po"] % len(ps_o)]
                cnts["po"] += 1
                for k in range(nkc):
                    c.op("pe", lambda e, k=k, s=s, pb=pb, wo=wo: e.matmul(pb.t[:P, :OG], lhsT=hT.t[:, k, s * P:(s + 1) * P],
                                                                          rhs=wo.t[:, k, :], start=(k == 0),
                                                                          stop=(k == nkc - 1)),
                         reads=(hT, wo), writes=(pb,))
                c.op("dve", lambda e, s=s, og=og, pb=pb: e.scalar_tensor_tensor(
                    out=xres.t[:P, s, og * OG:(og + 1) * OG], in0=xres.t[:P, s, og * OG:(og + 1) * OG], scalar=ALPHA,
                    in1=pb.t[:P, :OG], op0=ALU.mult, op1=ALU.add), reads=(xres, pb), writes=(xres,))
        for s in range(NS):
            self.ln_epilogue(c, S, es, xres, xres.t[:P, s, :], gbc, bbc, lnb,
                             x_out.ap()[t0 + s * P:t0 + (s + 1) * P, :], xT_out, t0 + s * P, P, ident, psT,
                             cnts["tc"])

    def ffn_stage(self, c, S, li, x_in, xT_in, x_out, xT_out, W, ident, psb):
        nc = self.nc
        P, NS, TT = S.P, S.NS, S.TT
        with ExitStack() as es:
            xT = c.sb("f_xT", [128, KC, TT], BF16, es)
            hT = c.sb("f_hT", [128, FFN_H // 128, TT], BF16, es)
            xres = c.sb("f_xres", [P, NS, D], F32, es)
            wi = [c.sb("f_wi%d" % i, [128, 2, KC, 256], BF16, es) for i in range(2)]
            wo = [c.sb("f_wo%d" % i, [128, FFN_H // 128, 256], BF16, es) for i in range(2)]
            sg = [c.sb("f_sg%d" % i, [128, TT], F32, es) for i in range(2)]
            gbc = c.sb("f_g", [P, D], F32, es)
            bbc = c.sb("f_b", [P, D], F32, es)
            lnb = (c.sb("f_st", [P, 4, 6], F32, es), c.sb("f_mv", [P, 2], F32, es), c.sb("f_rs", [P, 1], F32, es),
                   c.sb("f_nmr", [P, 1], F32, es), c.sb("f_xn", [P, D], F32, es), c.sb("f_xTo", [128, KC, P], BF16, es))
            c.dma("sp", gbc.t[:, :], W["ln2_g"].ap()[li:li + 1, :].partition_broadcast(P), writes=(gbc,))
            c.dma("sp", bbc.t[:, :], W["ln2_b"].ap()[li:li + 1, :].partition_broadcast(P), writes=(bbc,))
            win_b, win_r = W["ffn_w_in_b"][li]
            wout_b, wout_r = W["ffn_w_out_b"][li]
            cnts = {"wo": 0, "po": 0, "tc": [0], "wi": 0, "pg": 0}
            ps_g = psb[0:2]
            ps_u = psb[2:4]
            ps_o = psb[4:6]
            psT = psb[6:8]
            NHB = FFN_H // 256
            for tile in range(S.ntiles):
                t0 = tile * TT
                c.dma("sp", xT.t[:, :, :], xT_in.ap().rearrange("(kc p) t -> p kc t", p=128)[:, :, t0:t0 + TT],
                      writes=(xT,))
                c.dma("sp", xres.t[:, :, :], x_in.ap()[t0:t0 + TT, :].rearrange("(s p) d -> p s d", p=P),
                      writes=(xres,))
                for j in range(NHB):
                    w = wi[cnts["wi"] % 2]
                    cnts["wi"] += 1
                    for gu in range(2):
                        c0 = gu * FFN_H + j * 256
                        c.dma("sp", w.t[:, gu, :, :],
                              win_b.ap().rearrange("(kc p) n -> p kc n", p=128)[:, :, c0:c0 + 256],
                              reads=(win_r,), writes=(w,))
                    for half in range(2):
                        hc = j * 2 + half
                        pg = ps_g[cnts["pg"] % 2]
                        pu = ps_u[cnts["pg"] % 2]
                        sgb = sg[cnts["pg"] % 2]
                        cnts["pg"] += 1
                        for gu, pb in ((0, pg), (1, pu)):
                            for k in range(KC):
                                c.op("pe", lambda e, k=k, gu=gu, pb=pb, w=w, half=half: e.matmul(
                                    pb.t[:, :TT], lhsT=w.t[:, gu, k, half * 128:(half + 1) * 128], rhs=xT.t[:, k, :],
                                    start=(k == 0), stop=(k == KC - 1)), reads=(w, xT), writes=(pb,))
                        c.op("act", lambda e, pg=pg, sgb=sgb: e.activation(out=sgb.t[:, :], in_=pg.t[:, :TT], func=AF.Silu),
                             reads=(pg,), writes=(sgb,))
                        c.op("dve", lambda e, pu=pu, sgb=sgb, hc=hc: e.tensor_tensor(out=hT.t[:, hc, :], in0=sgb.t[:, :],
                                                                                       in1=pu.t[:, :TT], op=ALU.mult),
                             reads=(pu, sgb), writes=(hT,))
                self.out_proj_ln(c, S, es, hT, FFN_H // 128, wout_b, wout_r, xres, gbc, bbc, lnb, x_out, xT_out, tile,
                                 ident, ps_o, psT, wo, cnts)
            c.barrier()


    def ln_bufs(self, c, es, P, pre):
        return (c.sb(pre + "_st", [P, 4, 6], F32, es), c.sb(pre + "_mv", [P, 2], F32, es), c.sb(pre + "_rs", [P, 1], F32, es),
                c.sb(pre + "_nmr", [P, 1], F32, es), c.sb(pre + "_xn", [P, D], F32, es), c.sb(pre + "_xTo", [128, KC, P], BF16, es))

    def load_bc(self, c, es, name, row_ap, P, n):
        b = c.sb(name, [P, n], F32, es)
        c.dma("sp", b.t[:, :], row_ap.partition_broadcast(P), writes=(b,))
        return b

    def row_stats(self, c, src, src_ap, P, n, st, mv, rs, nmr):
        nq = max(1, n // 512)
        w = n // nq
        for q in range(nq):
            c.op("dve", lambda e, q=q: e.bn_stats(out=st.t[:P, q, :], in_=src_ap[:, q * w:(q + 1) * w]),
                 reads=(src,), writes=(st,))
        c.op("dve", lambda e: e.bn_aggr(out=mv.t[:P, :], in_=st.t[:P, :nq, :].rearrange("p a b -> p (a b)")),
             reads=(st,), writes=(mv,))
        c.op("act", lambda e: e.activation(out=rs.t[:P, :], in_=mv.t[:P, 1:2], func=AF.Sqrt, bias=self.eps_t.t[:P, :],
                                           scale=1.0), reads=(mv, self.eps_t), writes=(rs,))
        c.op("dve", lambda e: e.reciprocal(out=rs.t[:P, :], in_=rs.t[:P, :]), reads=(rs,), writes=(rs,))
        c.op("dve", lambda e: e.scalar_tensor_tensor(out=nmr.t[:P, :], in0=mv.t[:P, 0:1], scalar=-1.0,
                                                     in1=rs.t[:P, :], op0=ALU.mult, op1=ALU.mult),
             reads=(mv, rs), writes=(nmr,))

    def a_stage(self, c, S, li, j, x_in, xT_in, x_out, xT_out, W, ident, psb, v_out=None):
        nc = self.nc
        P, NS, TT = S.P, S.NS, S.TT
        BL = P
        with ExitStack() as es:
            xT = c.sb("a_xT", [128, KC, TT], BF16, es)
            xres = c.sb("a_xres", [P, NS, D], F32, es)
            uT = c.sb("a_uT", [128, KC, TT], BF16, es)
            yT = c.sb("a_yT", [128, KC, TT], BF16, es)
            vln = c.sb("a_vln", [P, NS, D], BF16, es)
            ws_ = [c.sb("a_w%d" % i, [128, KC, 256], BF16, es) for i in range(2)]
            wo = [c.sb("a_wo%d" % i, [128, KC, 256], BF16, es) for i in range(2)]
            tt = [c.sb("a_t%d" % i, [128, TT], F32, es) for i in range(2)]
            lnb = self.ln_bufs(c, es, P, "a")
            st, mv, rs, nmr = lnb[0], lnb[1], lnb[2], lnb[3]
            gbc = self.load_bc(c, es, "a_g", W["ln1_g"].ap()[li:li + 1, :], P, D)
            bbc = self.load_bc(c, es, "a_b", W["ln1_b"].ap()[li:li + 1, :], P, D)
            vg = self.load_bc(c, es, "a_vg", W["a_vn_g"].ap()[j:j + 1, :], P, D)
            vb = self.load_bc(c, es, "a_vb", W["a_vn_b"].ap()[j:j + 1, :], P, D)
            binv = self.load_bc(c, es, "a_binv", W["a_b_in"].ap()[j:j + 1, D:2 * D], P, D)
            bs = self.load_bc(c, es, "a_bs", W["a_b_s"].ap()[j:j + 1].rearrange("a g p -> a (g p)"), 128, 8 * 128)
            binu = c.sb("a_binu", [128, KC], F32, es)
            with nc.allow_non_contiguous_dma(reason="tiny bias column load"):
                c.dma("sp", binu.t[:, :], W["a_b_in"].ap()[j, 0:D].rearrange("(kc p) -> p kc", p=128), writes=(binu,))
            wsf = c.sb("a_wsf", [128, 8, 128], F32, es)
            wsT = c.sb("a_wsT", [128, 8, 128], BF16, es)
            c.dma("sp", wsf.t[:, :, :], W["a_w_s"].ap()[j].rearrange("g p q -> p g q"), writes=(wsf,))
            for g in range(8):
                pb = psb[g % 8]
                c.op("pe", lambda e, g=g, pb=pb: e.transpose(out=pb.t[:, :128], in_=wsf.t[:, g, :], identity=ident.t[:, :]),
                     reads=(wsf, ident), writes=(pb,))
                c.op("dve", lambda e, g=g, pb=pb: e.tensor_copy(out=wsT.t[:, g, :], in_=pb.t[:, :128]), reads=(pb,),
                     writes=(wsT,))
            if BL > 64:
                c.op("pool", lambda e: e.memset(wsT.t[64:128, :, 0:64], 0.0), writes=(wsT,))
            win_b, win_r = W["a_w_in_b"][j]
            wout_b, wout_r = W["a_w_out_b"][j]
            cnts = {"wo": 0, "po": 0, "tc": [0], "w": 0, "pu": 0}
            ps_u = psb[0:2]
            ps_v = psb[2:4]
            ps_o = psb[4:6]
            psT = psb[6:8]
            for tile in range(S.ntiles):
                t0 = tile * TT
                c.dma("sp", xT.t[:, :, :], xT_in.ap().rearrange("(kc p) t -> p kc t", p=128)[:, :, t0:t0 + TT],
                      writes=(xT,))
                for jb in range(D // 256):
                    w = ws_[cnts["w"] % 2]
                    cnts["w"] += 1
                    c.dma("sp", w.t[:, :, :], win_b.ap().rearrange("(kc p) n -> p kc n", p=128)[:, :, jb * 256:(jb + 1) * 256],
                          reads=(win_r,), writes=(w,))
                    for half in range(2):
                        ec = jb * 2 + half
                        pb = ps_u[cnts["pu"] % 2]
                        cnts["pu"] += 1
                        for k in range(KC):
                            c.op("pe", lambda e, k=k, pb=pb, w=w, half=half: e.matmul(
                                pb.t[:, :TT], lhsT=w.t[:, k, half * 128:(half + 1) * 128], rhs=xT.t[:, k, :],
                                start=(k == 0), stop=(k == KC - 1)), reads=(w, xT), writes=(pb,))
                        c.op("act", lambda e, pb=pb, ec=ec: e.activation(out=uT.t[:, ec, :], in_=pb.t[:, :TT],
                                                                         func=AF.Gelu_apprx_tanh, bias=binu.t[:, ec:ec + 1],
                                                                         scale=1.0), reads=(pb, binu), writes=(uT,))
                for og in range(D // 256):
                    w = ws_[cnts["w"] % 2]
                    cnts["w"] += 1
                    c.dma("sp", w.t[:, :, :],
                          win_b.ap().rearrange("(kc p) n -> p kc n", p=128)[:, :, D + og * 256:D + (og + 1) * 256],
                          reads=(win_r,), writes=(w,))
                    for s_ in range(NS):
                        pb = ps_v[cnts["pu"] % 2]
                        cnts["pu"] += 1
                        for k in range(KC):
                            c.op("pe", lambda e, k=k, pb=pb, w=w, s_=s_: e.matmul(
                                pb.t[:P, :256], lhsT=xT.t[:, k, s_ * P:(s_ + 1) * P], rhs=w.t[:, k, :],
                                start=(k == 0), stop=(k == KC - 1)), reads=(w, xT), writes=(pb,))
                        c.op("dve", lambda e, pb=pb, s_=s_, og=og: e.tensor_tensor(
                            out=xres.t[:P, s_, og * 256:(og + 1) * 256], in0=pb.t[:P, :256],
                            in1=binv.t[:P, og * 256:(og + 1) * 256], op=ALU.add), reads=(pb, binv), writes=(xres,))
                for s_ in range(NS):
                    va = xres.t[:P, s_, :]
                    c.op("act", lambda e, va=va: e.activation(out=va, in_=va, func=AF.Gelu_apprx_tanh), reads=(xres,),
                         writes=(xres,))
                    self.row_stats(c, xres, va, P, D, st, mv, rs, nmr)
                    c.op("act", lambda e, va=va: e.activation(out=va, in_=va, func=AF.Identity, bias=nmr.t[:P, 0:1],
                                                               scale=rs.t[:P, 0:1]), reads=(xres, rs, nmr), writes=(xres,))
                    c.op("dve", lambda e, va=va: e.tensor_tensor(out=va, in0=va, in1=vg.t[:P, :], op=ALU.mult),
                         reads=(xres, vg), writes=(xres,))
                    if v_out is None:
                        c.op("pool", lambda e, va=va, s_=s_: e.tensor_tensor(out=vln.t[:P, s_, :], in0=va, in1=vb.t[:P, :],
                                                                              op=ALU.add), reads=(xres, vb), writes=(vln,))
                    else:
                        c.op("pool", lambda e, va=va: e.tensor_tensor(out=va, in0=va, in1=vb.t[:P, :], op=ALU.add),
                             reads=(xres, vb), writes=(xres,))
                        c.dma("sp", v_out.ap()[t0 + s_ * P:t0 + (s_ + 1) * P, :], va, reads=(xres,))
                        c.op("dve", lambda e, va=va, s_=s_: e.tensor_copy(out=vln.t[:P, s_, :], in_=va), reads=(xres,),
                             writes=(vln,))
                for ec in range(KC):
                    g = ec // 2
                    pb = ps_u[cnts["pu"] % 2]
                    tb = tt[cnts["pu"] % 2]
                    cnts["pu"] += 1
                    for s_ in range(NS):
                        c.op("pe", lambda e, pb=pb, s_=s_, ec=ec, g=g: e.matmul(
                            pb.t[:, s_ * BL:(s_ + 1) * BL], lhsT=vln.t[:BL, s_, ec * 128:(ec + 1) * 128],
                            rhs=wsT.t[:BL, g, :BL], start=True, stop=True), reads=(vln, wsT), writes=(pb,))
                    for s_ in range(NS):
                        c.op("dve", lambda e, pb=pb, tb=tb, s_=s_, g=g: e.tensor_tensor(
                            out=tb.t[:, s_ * BL:(s_ + 1) * BL], in0=pb.t[:, s_ * BL:(s_ + 1) * BL],
                            in1=bs.t[:, g * 128:g * 128 + BL], op=ALU.add), reads=(pb, bs), writes=(tb,))
                    c.op("pool", lambda e, tb=tb, ec=ec: e.tensor_tensor(out=yT.t[:, ec, :], in0=tb.t[:, :TT],
                                                                          in1=uT.t[:, ec, :], op=ALU.mult),
                         reads=(tb, uT), writes=(yT,))
                c.dma("sp", xres.t[:, :, :], x_in.ap()[t0:t0 + TT, :].rearrange("(s p) d -> p s d", p=P),
                      writes=(xres,))
                self.out_proj_ln(c, S, es, yT, KC, wout_b, wout_r, xres, gbc, bbc, lnb, x_out, xT_out, tile, ident,
                                 ps_o, psT, wo, cnts)
            c.barrier()


    def d_stage(self, c, S, key, li, j, x_in, xT_in, x_out, xT_out, W, ident, psb):
        nc = self.nc
        P, NS, TT, T = S.P, S.NS, S.TT, S.T
        NB = T // P
        smp = key == "s"
        qT_d = self.dscr("d_qT_" + key, [D, T], BF16)
        kT_d = self.dscr("d_kT_" + key, [D, T], BF16)
        v_d = self.dscr("d_v_" + key, [T, D], BF16)
        oT_d = self.dscr("d_oT_" + key, [D, T], BF16)
        crow_d = self.dscr("d_crow_" + key, [16, T], F32)
        k_out, v_out, f_out = self.d_outs[key]
        win_b, win_r = W["d_w_in_b"][j]
        wout_b, wout_r = W["d_w_out_b"][j]
        C = self.consts
        with ExitStack() as es0:
            negc = c.sb("d_negc", [P, NB, 16], F32, es0)
            C = dict(C)
            C["negm"] = c.sb("c_negm", [128, 4, 512], F32, es0)
            c.dma("sp", C["negm"].t[:, :, :], self.cin["c_negm"].ap(), writes=(C["negm"],))
            if smp:
                negcc = c.sb("d_negcc", [128, 8, 16], F32, es0)
            with ExitStack() as es:
                xT = c.sb("d_xT", [128, KC, TT], BF16, es)
                ws_ = [c.sb("d_w%d" % i, [128, KC, 256], BF16, es) for i in range(2)]
                stg = [c.sb("d_stg%d" % i, [128, TT], BF16, es) for i in range(2)]
                tmf = c.sb("d_tmf", [P, NS, D], F32, es)
                vb16 = c.sb("d_vb16", [P, NS, D], BF16, es)
                wf = c.sb("d_wf", [128, KC, 16], BF16, es)
                bfb = self.load_bc(c, es, "d_bf", W["d_b_f"].ap()[j:j + 1, :], 128, 16)
                carry = c.sb("d_carry", [128, 16], F32, es)
                ft = c.sb("d_ft", [128, 16], F32, es)
                lf = c.sb("d_lf", [128, 16], F32, es)
                cs = c.sb("d_cs", [128, 16], F32, es)
                crow = c.sb("d_crow", [16, TT], F32, es)
                c.dma("sp", wf.t[:, :, :], win_b.ap().rearrange("(kc p) n -> p kc n", p=128)[:, :, 3 * D:3 * D + 16],
                      reads=(win_r,), writes=(wf,))
                c.op("pool", lambda e: e.memset(carry.t[:, :], 0.0), writes=(carry,))
                pcnt = [0]

                def cumsum_block(lf_ap, lfb, n, negdst_ap, negdst, crow_ap=None):
                    pa = psb[6]
                    pt = psb[7]
                    c.op("pe", lambda e: e.matmul(pa.t[:n, :16], lhsT=C["tri"].t[:n, :n], rhs=lf_ap, start=True, stop=True),
                         reads=(C["tri"], lfb), writes=(pa,))
                    c.op("pe", lambda e: e.matmul(pt.t[:128, :16], lhsT=C["ones_f"].t[:n, :128], rhs=lf_ap, start=True,
                                                  stop=True), reads=(C["ones_f"], lfb), writes=(pt,))
                    c.op("dve", lambda e: e.tensor_tensor(out=cs.t[:n, :], in0=pa.t[:n, :16], in1=carry.t[:n, :], op=ALU.add),
                         reads=(pa, carry), writes=(cs,))
                    c.op("dve", lambda e: e.tensor_scalar(out=negdst_ap, in0=cs.t[:n, :], scalar1=-1.0, scalar2=None,
                                                          op0=ALU.mult), reads=(cs,), writes=(negdst,))
                    c.op("dve", lambda e: e.tensor_tensor(out=carry.t[:, :], in0=pt.t[:128, :16], in1=carry.t[:, :],
                                                          op=ALU.add), reads=(pt, carry), writes=(carry,))
                    if crow_ap is not None:
                        c.op("pe", lambda e: e.transpose(out=pa.t[:16, 32:32 + n], in_=cs.t[:n, :16], identity=ident.t[:n, :n]),
                             reads=(cs, ident), writes=(pa,))
                        c.op("act", lambda e: e.copy(out=crow_ap, in_=pa.t[:16, 32:32 + n]), reads=(pa,), writes=(crow,))

                if smp:
                    lfc = c.sb("d_lfc", [128, 8, 16], F32, es)
                    c.dma("sp", lfc.t[:, :, :], W["cache_d_logf"].ap().rearrange("(b p) h -> p b h", p=128), writes=(lfc,))
                    for b in range(8):
                        cumsum_block(lfc.t[:, b, :], lfc, 128, negcc.t[:, b, :], negcc)
                    for b in range(8):
                        c.op("dve", lambda e, b=b: e.tensor_tensor(out=negcc.t[:, b, :], in0=negcc.t[:, b, :],
                                                                   in1=carry.t[:, :], op=ALU.add), reads=(negcc, carry),
                             writes=(negcc,))
                    c.op("pool", lambda e: e.memset(carry.t[:, :], 0.0), writes=(carry,))
                cw = 0
                for tile in range(S.ntiles):
                    t0 = tile * TT
                    c.dma("sp", xT.t[:, :, :], xT_in.ap().rearrange("(kc p) t -> p kc t", p=128)[:, :, t0:t0 + TT],
                          writes=(xT,))
                    for part, dst in ((0, qT_d), (1, kT_d)):
                        for jb in range(D // 256):
                            w = ws_[cw % 2]
                            cw += 1
                            c.dma("sp", w.t[:, :, :], win_b.ap().rearrange("(kc p) n -> p kc n", p=128)[
                                :, :, part * D + jb * 256:part * D + (jb + 1) * 256], reads=(win_r,), writes=(w,))
                            for half in range(2):
                                hc = jb * 2 + half
                                pb = psb[pcnt[0] % 2]
                                sg_ = stg[pcnt[0] % 2]
                                pcnt[0] += 1
                                for k in range(KC):
                                    c.op("pe", lambda e, k=k, pb=pb, w=w, half=half: e.matmul(
                                        pb.t[:, :TT], lhsT=w.t[:, k, half * 128:(half + 1) * 128], rhs=xT.t[:, k, :],
                                        start=(k == 0), stop=(k == KC - 1)), reads=(w, xT), writes=(pb,))
                                sc = (128.0 ** -0.5) if part == 0 else 1.0
                                c.op("act", lambda e, pb=pb, sg_=sg_, sc=sc: e.activation(out=sg_.t[:, :], in_=pb.t[:, :TT],
                                                                                         func=AF.Copy, scale=sc),
                                     reads=(pb,), writes=(sg_,))
                                c.dma("sp", dst.ap()[hc * 128:(hc + 1) * 128, t0:t0 + TT], sg_.t[:, :], reads=(sg_,))
                    for part, out_d in ((1, k_out), (2, v_out)):
                        for og in range(D // 256):
                            w = ws_[cw % 2]
                            cw += 1
                            c.dma("sp", w.t[:, :, :], win_b.ap().rearrange("(kc p) n -> p kc n", p=128)[
                                :, :, part * D + og * 256:part * D + (og + 1) * 256], reads=(win_r,), writes=(w,))
                            for s_ in range(NS):
                                pb = psb[2 + pcnt[0] % 2]
                                pcnt[0] += 1
                                for k in range(KC):
                                    c.op("pe", lambda e, k=k, pb=pb, w=w, s_=s_: e.matmul(
                                        pb.t[:P, :256], lhsT=xT.t[:, k, s_ * P:(s_ + 1) * P], rhs=w.t[:, k, :],
                                        start=(k == 0), stop=(k == KC - 1)), reads=(w, xT), writes=(pb,))
                                if pcnt[0] % 2:
                                    c.op("act", lambda e, pb=pb, s_=s_, og=og: e.copy(
                                        out=tmf.t[:P, s_, og * 256:(og + 1) * 256], in_=pb.t[:P, :256]), reads=(pb,),
                                        writes=(tmf,))
                                else:
                                    c.op("dve", lambda e, pb=pb, s_=s_, og=og: e.tensor_copy(
                                        out=tmf.t[:P, s_, og * 256:(og + 1) * 256], in_=pb.t[:P, :256]), reads=(pb,),
                                        writes=(tmf,))
                        c.dma("sp", out_d.ap()[t0:t0 + TT, :].rearrange("(s p) d -> p s d", p=P), tmf.t[:, :, :],
                              reads=(tmf,))
                        if part == 2:
                            c.op("pool", lambda e: e.tensor_copy(out=vb16.t[:, :, :], in_=tmf.t[:, :, :]), reads=(tmf,),
                                 writes=(vb16,))
                            c.dma("sp", v_d.ap()[t0:t0 + TT, :].rearrange("(s p) d -> p s d", p=P), vb16.t[:, :, :],
                                  reads=(vb16,))
                    for s_ in range(NS):
                        pb = psb[4 + s_ % 2]
                        for k in range(KC):
                            c.op("pe", lambda e, k=k, pb=pb, s_=s_: e.matmul(
                                pb.t[:P, :16], lhsT=xT.t[:, k, s_ * P:(s_ + 1) * P], rhs=wf.t[:, k, :],
                                start=(k == 0), stop=(k == KC - 1)), reads=(wf, xT), writes=(pb,))
                        c.op("dve", lambda e, pb=pb: e.tensor_tensor(out=ft.t[:P, :], in0=pb.t[:P, :16], in1=bfb.t[:P, :],
                                                                     op=ALU.add), reads=(pb, bfb), writes=(ft,))
                        c.op("act", lambda e: e.activation(out=ft.t[:P, :], in_=ft.t[:P, :], func=AF.Exp, scale=-1.0),
                             reads=(ft,), writes=(ft,))
                        c.op("act", lambda e: e.activation(out=ft.t[:P, :], in_=ft.t[:P, :], func=AF.Ln,
                                                           bias=C["one"].t[:P, :], scale=1.0), reads=(ft, C["one"]),
                             writes=(ft,))
                        c.op("dve", lambda e: e.tensor_scalar(out=lf.t[:P, :], in0=ft.t[:P, :], scalar1=-1.0, scalar2=None,
                                                              op0=ALU.mult), reads=(ft,), writes=(lf,))
                        r0 = t0 + s_ * P
                        c.dma("sp", f_out.ap()[r0:r0 + P, :], lf.t[:P, :], reads=(lf,))
                        cumsum_block(lf.t[:P, :], lf, P, negc.t[:P, tile * NS + s_, :], negc,
                                     crow.t[:16, s_ * P:(s_ + 1) * P])
                    c.dma("sp", crow_d.ap()[:, t0:t0 + TT], crow.t[:16, :TT], reads=(crow,))
                c.barrier()
            with ExitStack() as es:
                TK = T + (PAST if smp else 0)
                kT_h = c.sb("d_kTh", [128, TK], BF16, es)
                qT_h = c.sb("d_qTh", [128, T], BF16, es)
                V_h = c.sb("d_Vh", [P, NB, 128], BF16, es)
                oT_h = c.sb("d_oTh", [128, T], BF16, es)
                tb = [c.sb("d_t%d" % i, [128, TT], F32, es) for i in range(2)]
                Eb = [c.sb("d_E%d" % i, [128, TT], BF16, es) for i in range(2)]
                cq = [c.sb("d_cq%d" % i, [128, TT], F32, es) for i in range(2)]
                rden = c.sb("d_rden", [128, TT], F32, es)
                if smp:
                    ckf = c.sb("d_ckf", [128, 8, 128], F32, es)
                    V_c = c.sb("d_Vc", [128, 8, 128], BF16, es)
                it = 0
                qcnt = 0
                for h in range(16):
                    hs = slice(h * 128, (h + 1) * 128)
                    c.dma("sp", qT_h.t[:, :], qT_d.ap()[hs, :], writes=(qT_h,))
                    c.dma("sp", V_h.t[:, :, :], v_d.ap()[:, hs].rearrange("(b p) d -> p b d", p=P), writes=(V_h,))
                    if smp:
                        c.dma("sp", kT_h.t[:, PAST:PAST + T], kT_d.ap()[hs, :], writes=(kT_h,))
                        c.dma("sp", ckf.t[:, :, :], W["cache_d_k"].ap()[:, hs].rearrange("(b p) d -> p b d", p=128),
                              writes=(ckf,))
                        c.dma("pool", V_c.t[:, :, :], W["cache_d_v"].ap()[:, hs].rearrange("(b p) d -> p b d", p=128),
                              writes=(V_c,))
                        for b in range(8):
                            pb = psb[6 + b % 2]
                            c.op("pe", lambda e, b=b, pb=pb: e.transpose(out=pb.t[:, :128], in_=ckf.t[:, b, :],
                                                                         identity=ident.t[:, :]), reads=(ckf, ident),
                                 writes=(pb,))
                            c.op("act", lambda e, b=b, pb=pb: e.copy(out=kT_h.t[:, b * 128:(b + 1) * 128], in_=pb.t[:, :128]),
                                 reads=(pb,), writes=(kT_h,))
                    else:
                        c.dma("sp", kT_h.t[:, :], kT_d.ap()[hs, :], writes=(kT_h,))
                    for qt in range(S.ntiles):
                        q0 = qt * TT
                        cqb = cq[qcnt % 2]
                        po = psb[2 + qcnt % 2]
                        pd = psb[4 + qcnt % 2]
                        qcnt += 1
                        c.dma("sp", cqb.t[:, :], crow_d.ap()[h:h + 1, q0:q0 + TT].partition_broadcast(128), writes=(cqb,))
                        blocks = []
                        if smp:
                            for b in range(8):
                                blocks.append((kT_h.t[:, b * 128:(b + 1) * 128], V_c.t[:, b, :], negcc.t[:, b, h:h + 1], 128,
                                               None, V_c, negcc))
                            blocks.append((kT_h.t[:, PAST:PAST + T], V_h.t[:P, 0, :], negc.t[:P, 0, h:h + 1], P, 0, V_h, negc))
                        else:
                            for kb in range((q0 + TT) // 128):
                                jm = kb - q0 // 128
                                blocks.append((kT_h.t[:, kb * 128:(kb + 1) * 128], V_h.t[:, kb, :], negc.t[:, kb, h:h + 1],
                                               128, jm if jm >= 0 else None, V_h, negc))
                        nb = len(blocks)
                        for bi, (k_ap, v_ap, nc_ap, bl, jm, vbuf, ncbuf) in enumerate(blocks):
                            pS = psb[it % 2]
                            t_ = tb[it % 2]
                            E_ = Eb[it % 2]
                            it += 1
                            c.op("pe", lambda e, pS=pS, k_ap=k_ap, bl=bl, q0=q0: e.matmul(
                                pS.t[:bl, :TT], lhsT=k_ap, rhs=qT_h.t[:, q0:q0 + TT], start=True, stop=True),
                                reads=(kT_h, qT_h), writes=(pS,))
                            c.op("dve", lambda e, pS=pS, t_=t_, bl=bl, cqb=cqb: e.tensor_tensor(
                                out=t_.t[:bl, :], in0=pS.t[:bl, :TT], in1=cqb.t[:bl, :], op=ALU.add), reads=(pS, cqb),
                                writes=(t_,))
                            if jm is not None:
                                c.op("pool", lambda e, t_=t_, bl=bl, jm=jm: e.tensor_tensor(
                                    out=t_.t[:bl, :], in0=t_.t[:bl, :], in1=C["negm"].t[:bl, jm, :TT], op=ALU.add),
                                    reads=(t_, C["negm"]), writes=(t_,))
                            c.op("act", lambda e, t_=t_, E_=E_, bl=bl, nc_ap=nc_ap: e.activation(
                                out=E_.t[:bl, :], in_=t_.t[:bl, :], func=AF.Exp, bias=nc_ap, scale=1.0),
                                reads=(t_, ncbuf), writes=(E_,))
                            c.op("pe", lambda e, po=po, v_ap=v_ap, E_=E_, bl=bl, bi=bi, nb=nb: e.matmul(
                                po.t[:, :TT], lhsT=v_ap, rhs=E_.t[:bl, :], start=(bi == 0), stop=(bi == nb - 1)),
                                reads=(vbuf, E_), writes=(po,))
                            c.op("pe", lambda e, pd=pd, E_=E_, bl=bl, bi=bi, nb=nb: e.matmul(
                                pd.t[:, :TT], lhsT=C["ones_b"].t[:bl, :], rhs=E_.t[:bl, :], start=(bi == 0),
                                stop=(bi == nb - 1)), reads=(C["ones_b"], E_), writes=(pd,))
                        c.op("dve", lambda e, pd=pd: e.reciprocal(out=rden.t[:, :], in_=pd.t[:, :TT]), reads=(pd,),
                             writes=(rden,))
                        c.op("dve", lambda e, po=po, q0=q0: e.tensor_tensor(out=oT_h.t[:, q0:q0 + TT], in0=po.t[:, :TT],
                                                                           in1=rden.t[:, :], op=ALU.mult), reads=(po, rden),
                             writes=(oT_h,))
                    c.dma("sp", oT_d.ap()[hs, :], oT_h.t[:, :], reads=(oT_h,))
                c.barrier()
            with ExitStack() as es:
                oT = c.sb("d_oT", [128, KC, TT], BF16, es)
                xres = c.sb("d_xres", [P, NS, D], F32, es)
                wo = [c.sb("d_wo%d" % i, [128, KC, 256], BF16, es) for i in range(2)]
                lnb = self.ln_bufs(c, es, P, "d")
                gbc = self.load_bc(c, es, "d_g", W["ln1_g"].ap()[li:li + 1, :], P, D)
                bbc = self.load_bc(c, es, "d_b", W["ln1_b"].ap()[li:li + 1, :], P, D)
                cnts = {"wo": 0, "po": 0, "tc": [0]}
                for tile in range(S.ntiles):
                    t0 = tile * TT
                    c.dma("sp", oT.t[:, :, :], oT_d.ap().rearrange("(kc p) t -> p kc t", p=128)[:, :, t0:t0 + TT],
                          writes=(oT,))
                    c.dma("sp", xres.t[:, :, :], x_in.ap()[t0:t0 + TT, :].rearrange("(s p) d -> p s d", p=P),
                          writes=(xres,))
                    self.out_proj_ln(c, S, es, oT, KC, wout_b, wout_r, xres, gbc, bbc, lnb, x_out, xT_out, tile, ident,
                                     psb[4:6], psb[6:8], wo, cnts)
                c.barrier()


    def proj_fm(self, c, xT, TT, win_b, win_r, col0, nchunks, ws_, cw, pbs, pcnt, evac):
        for jb in range((nchunks + 1) // 2):
            ncol = min(256, (nchunks - jb * 2) * 128)
            w = ws_[cw[0] % 2]
            cw[0] += 1
            c.dma("sp", w.t[:, :, :ncol], win_b.ap().rearrange("(kc p) n -> p kc n", p=128)[
                :, :, col0 + jb * 256:col0 + jb * 256 + ncol], reads=(win_r,), writes=(w,))
            for half in range(ncol // 128):
                hc = jb * 2 + half
                pb = pbs[pcnt[0] % len(pbs)]
                pcnt[0] += 1
                for k in range(KC):
                    c.op("pe", lambda e, k=k, pb=pb, w=w, half=half: e.matmul(
                        pb.t[:, :TT], lhsT=w.t[:, k, half * 128:(half + 1) * 128], rhs=xT.t[:, k, :],
                        start=(k == 0), stop=(k == KC - 1)), reads=(w, xT), writes=(pb,))
                evac(hc, pb)

    def proj_tm(self, c, xT, S, win_b, win_r, col0, ncols, ws_, cw, pbs, pcnt, evac):
        P, NS = S.P, S.NS
        for og in range(ncols // 256):
            w = ws_[cw[0] % 2]
            cw[0] += 1
            c.dma("sp", w.t[:, :, :], win_b.ap().rearrange("(kc p) n -> p kc n", p=128)[
                :, :, col0 + og * 256:col0 + (og + 1) * 256], reads=(win_r,), writes=(w,))
            for s_ in range(NS):
                pb = pbs[pcnt[0] % len(pbs)]
                pcnt[0] += 1
                for k in range(KC):
                    c.op("pe", lambda e, k=k, pb=pb, w=w, s_=s_: e.matmul(
                        pb.t[:P, :256], lhsT=xT.t[:, k, s_ * P:(s_ + 1) * P], rhs=w.t[:, k, :],
                        start=(k == 0), stop=(k == KC - 1)), reads=(w, xT), writes=(pb,))
                evac(s_, og, pb)

    def b_stage(self, c, S, key, li, j, x_in, xT_in, x_out, xT_out, W, ident, psb):
        nc = self.nc
        P, NS, TT, T = S.P, S.NS, S.TT, S.T
        smp = key == "s"
        L = 32 if smp else 64
        NCH = TT // L
        H = 8
        qT_d = self.dscr("b_qT_" + key, [1024, T], BF16)
        kT_d = self.dscr("b_kT_" + key, [1024, T], BF16)
        k_d = self.dscr("b_k_" + key, [T, 1024], F32)
        v_d = self.dscr("b_v_" + key, [T, D], BF16)
        og_d = self.dscr("b_og_" + key, [T, D], F32)
        i_d = self.dscr("b_i_" + key, [8, T], F32)
        lf_d = self.dscr("b_lf_" + key, [8, T], F32)
        hT_d = self.dscr("b_hT_" + key, [D, T], BF16)
        C_out, n_out, m_out = self.b_outs[key]
        win_b, win_r = W["b_w_in_b"][j]
        wout_b, wout_r = W["b_w_out_b"][j]
        C = self.consts
        with ExitStack() as es:
            xT = c.sb("b_xT", [128, KC, TT], BF16, es)
            ws_ = [c.sb("b_w%d" % i, [128, KC, 256], BF16, es) for i in range(2)]
            stg = [c.sb("b_stg%d" % i, [128, TT], BF16, es) for i in range(2)]
            tmf = c.sb("b_tmf", [P, NS, D], F32, es)
            vb16 = c.sb("b_vb16", [P, NS, D], BF16, es)
            wg = c.sb("b_wg", [128, KC, 16], BF16, es)
            bg = c.sb("b_bg", [8, 2], F32, es)
            nbg = c.sb("b_nbg", [8, 2], F32, es)
            rows = [c.sb("b_rows%d" % i, [8, TT], F32, es) for i in range(2)]
            c.dma("sp", wg.t[:, :, :], win_b.ap().rearrange("(kc p) n -> p kc n", p=128)[:, :, 6144:6160],
                  reads=(win_r,), writes=(wg,))
            with nc.allow_non_contiguous_dma(reason="tiny gate bias load"):
                c.dma("sp", bg.t[:, :], W["b_b_gates"].ap()[j, :].rearrange("(a h) -> h a", a=2), writes=(bg,))
            c.op("dve", lambda e: e.tensor_scalar(out=nbg.t[:, :], in0=bg.t[:, :], scalar1=-1.0, scalar2=None, op0=ALU.mult),
                 reads=(bg,), writes=(nbg,))
            cw = [0]
            pcnt = [0]
            for tile in range(S.ntiles):
                t0 = tile * TT
                c.dma("sp", xT.t[:, :, :], xT_in.ap().rearrange("(kc p) t -> p kc t", p=128)[:, :, t0:t0 + TT],
                      writes=(xT,))
                for part, dst, sc in ((0, qT_d, 128.0 ** -0.5), (1, kT_d, 1.0)):
                    def ev(hc, pb, dst=dst, sc=sc):
                        sg_ = stg[pcnt[0] % 2]
                        c.op("act", lambda e: e.activation(out=sg_.t[:, :], in_=pb.t[:, :TT], func=AF.Copy, scale=sc),
                             reads=(pb,), writes=(sg_,))
                        c.dma("sp", dst.ap()[hc * 128:(hc + 1) * 128, t0:t0 + TT], sg_.t[:, :], reads=(sg_,))
                    self.proj_fm(c, xT, TT, win_b, win_r, part * 1024, 8, ws_, cw, psb[0:2], pcnt, ev)
                def evk(s_, og, pb):
                    c.op("dve", lambda e: e.tensor_copy(out=tmf.t[:P, s_, og * 256:(og + 1) * 256], in_=pb.t[:P, :256]),
                         reads=(pb,), writes=(tmf,))
                self.proj_tm(c, xT, S, win_b, win_r, 1024, 1024, ws_, cw, psb[2:4], pcnt, evk)
                c.dma("sp", k_d.ap()[t0:t0 + TT, :].rearrange("(s p) d -> p s d", p=P), tmf.t[:, :, :1024], reads=(tmf,))
                def evv(s_, og, pb):
                    c.op("act", lambda e: e.copy(out=vb16.t[:P, s_, og * 256:(og + 1) * 256], in_=pb.t[:P, :256]),
                         reads=(pb,), writes=(vb16,))
                self.proj_tm(c, xT, S, win_b, win_r, 2048, D, ws_, cw, psb[2:4], pcnt, evv)
                c.dma("sp", v_d.ap()[t0:t0 + TT, :].rearrange("(s p) d -> p s d", p=P), vb16.t[:, :, :], reads=(vb16,))
                def evo(s_, og, pb):
                    c.op("act", lambda e: e.activation(out=tmf.t[:P, s_, og * 256:(og + 1) * 256], in_=pb.t[:P, :256],
                                                       func=AF.Sigmoid), reads=(pb,), writes=(tmf,))
                self.proj_tm(c, xT, S, win_b, win_r, 4096, D, ws_, cw, psb[2:4], pcnt, evo)
                c.dma("sp", og_d.ap()[t0:t0 + TT, :].rearrange("(s p) d -> p s d", p=P), tmf.t[:, :, :], reads=(tmf,))
                for gi in range(2):
                    pb = psb[4 + gi]
                    r_ = rows[gi]
                    for k in range(KC):
                        c.op("pe", lambda e, k=k, pb=pb, gi=gi: e.matmul(pb.t[:8, :TT], lhsT=wg.t[:, k, gi * 8:(gi + 1) * 8],
                                                                       rhs=xT.t[:, k, :], start=(k == 0), stop=(k == KC - 1)),
                             reads=(wg, xT), writes=(pb,))
                    if gi == 0:
                        c.op("act", lambda e, pb=pb, r_=r_: e.activation(out=r_.t[:, :], in_=pb.t[:8, :TT], func=AF.Identity,
                                                                         bias=bg.t[:, 0:1], scale=1.0), reads=(pb, bg),
                             writes=(r_,))
                        c.dma("sp", i_d.ap()[:, t0:t0 + TT], r_.t[:, :], reads=(r_,))
                    else:
                        c.op("act", lambda e, pb=pb, r_=r_: e.activation(out=r_.t[:, :], in_=pb.t[:8, :TT], func=AF.Exp,
                                                                         bias=nbg.t[:, 1:2], scale=-1.0), reads=(pb, nbg),
                             writes=(r_,))
                        c.op("act", lambda e, r_=r_: e.activation(out=r_.t[:, :], in_=r_.t[:, :], func=AF.Ln,
                                                                  bias=C["one"].t[:8, :], scale=1.0), reads=(r_, C["one"]),
                             writes=(r_,))
                        c.op("dve", lambda e, r_=r_: e.tensor_scalar(out=r_.t[:, :], in0=r_.t[:, :], scalar1=-1.0,
                                                                     scalar2=None, op0=ALU.mult), reads=(r_,), writes=(r_,))
                        c.dma("sp", lf_d.ap()[:, t0:t0 + TT], r_.t[:, :], reads=(r_,))
            c.barrier()
        with ExitStack() as es:
            qT_t = c.sb("b_qTt", [128, H, TT], BF16, es)
            kT_t = c.sb("b_kTt", [128, H, TT], BF16, es)
            ktm = [c.sb("b_ktm%d" % i, [L, 1024], F32, es) for i in range(2)]
            v_c = [c.sb("b_vc%d" % i, [L, D], BF16, es) for i in range(2)]
            og_c = [c.sb("b_ogc%d" % i, [L, D], F32, es) for i in range(2)]
            CT = c.sb("b_CT", [128, H, 256], F32, es)
            CTb = c.sb("b_CTb", [128, H, 256], BF16, es)
            nT = c.sb("b_nT", [128, H], F32, es)
            nTb = c.sb("b_nTb", [128, H], BF16, es)
            Sp = c.sb("b_Sp", [L, H, L], BF16, es)
            mbeta = c.sb("b_mbeta", [L, H, L], F32, es)
            kp = c.sb("b_kp", [L, H, 128], BF16, es)
            hb = c.sb("b_hb", [L, D], F32, es)
            hsq = c.sb("b_hsq", [L, D], F32, es)
            hT_t = c.sb("b_hTt", [128, KC, TT], BF16, es)
            ngbc = self.load_bc(c, es, "b_ng", W["b_norm_g"].ap()[j:j + 1, :], L, D)
            Fb = c.sb("b_F", [8, 1 + TT], F32, es)
            mb = c.sb("b_m", [8, 1 + TT], F32, es)
            ir = c.sb("b_ir", [8, TT], F32, es)
            lfr = c.sb("b_lfr", [8, TT], F32, es)
            zr = c.sb("b_zr", [8, TT], F32, es)
            bcum = c.sb("b_bcum", [8, L], F32, es)
            Mr = c.sb("b_Mr", [8, L], F32, es)
            rp = c.sb("b_rp", [8, 3, L], F32, es)
            sm = c.sb("b_sm", [8, 4], F32, es)
            dg = c.sb("b_dg", [8, 8], F32, es)
            colp = c.sb("b_colp", [L, 24], F32, es)
            gbcst = c.sb("b_gb", [128, 8], F32, es)
            dn = c.sb("b_dn", [L, 8], F32, es)
            sc_ = c.sb("b_sc", [L, 8], F32, es)
            ss = c.sb("b_ss", [L, 8], F32, es)
            c.op("pool", lambda e: e.memset(zr.t[:, :], 0.0), writes=(zr,))
            c.op("pool", lambda e: e.memset(Fb.t[:, 0:1], 0.0), writes=(Fb,))
            if smp:
                cin = c.sb("b_cin", [128, H, 2, 128], F32, es)
                c.dma("sp", cin.t[:, :, :, :], W["state_b_C"].ap().rearrange("h (eh el) d -> el h eh d", el=128), writes=(cin,))
                for h in range(H):
                    for eh in range(2):
                        pb = psb[(h * 2 + eh) % 2]
                        c.op("pe", lambda e, h=h, eh=eh, pb=pb: e.transpose(out=pb.t[:, :128], in_=cin.t[:, h, eh, :],
                                                                             identity=ident.t[:, :]), reads=(cin, ident),
                             writes=(pb,))
                        c.op("dve", lambda e, h=h, eh=eh, pb=pb: e.tensor_copy(out=CT.t[:, h, eh * 128:(eh + 1) * 128],
                                                                                in_=pb.t[:, :128]), reads=(pb,), writes=(CT,))
                nin = c.sb("b_nin", [8, 128], F32, es)
                c.dma("sp", nin.t[:, :], W["state_b_n"].ap(), writes=(nin,))
                c.op("pe", lambda e: e.transpose(out=psb[6].t[:, :8], in_=nin.t[:, :], identity=ident.t[:8, :8]),
                     reads=(nin, ident), writes=(psb[6],))
                c.op("dve", lambda e: e.tensor_copy(out=nT.t[:, :], in_=psb[6].t[:, :8]), reads=(psb[6],), writes=(nT,))
                c.dma("sp", mb.t[:, 0:1], W["state_b_m"].ap().rearrange("(h o) -> h o", o=1), writes=(mb,))
            else:
                c.op("pool", lambda e: e.memset(CT.t[:, :, :], 0.0), writes=(CT,))
                c.op("pool", lambda e: e.memset(nT.t[:, :], 0.0), writes=(nT,))
                c.op("pool", lambda e: e.memset(mb.t[:, 0:1], 0.0), writes=(mb,))
            cc = 0
            for tile in range(S.ntiles):
                t0 = tile * TT
                c.dma("sp", qT_t.t[:, :, :], qT_d.ap().rearrange("(h p) t -> p h t", p=128)[:, :, t0:t0 + TT], writes=(qT_t,))
                c.dma("sp", kT_t.t[:, :, :], kT_d.ap().rearrange("(h p) t -> p h t", p=128)[:, :, t0:t0 + TT], writes=(kT_t,))
                c.dma("sp", ir.t[:, :], i_d.ap()[:, t0:t0 + TT], writes=(ir,))
                c.dma("sp", lfr.t[:, :], lf_d.ap()[:, t0:t0 + TT], writes=(lfr,))
                c.op("dve", lambda e: e.tensor_tensor_scan(out=Fb.t[:, 1:1 + TT], data0=lfr.t[:, :], data1=zr.t[:, :],
                                                           initial=Fb.t[:, 0:1], op0=ALU.add, op1=ALU.add),
                     reads=(lfr, zr, Fb), writes=(Fb,))
                c.op("dve", lambda e: e.tensor_tensor_scan(out=mb.t[:, 1:1 + TT], data0=lfr.t[:, :], data1=ir.t[:, :],
                                                           initial=mb.t[:, 0:1], op0=ALU.add, op1=ALU.max),
                     reads=(lfr, ir, mb), writes=(mb,))
                for ch in range(NCH):
                    c0 = ch * L
                    r0 = t0 + c0
                    kt = ktm[cc % 2]
                    vc = v_c[cc % 2]
                    oc = og_c[cc % 2]
                    cc += 1
                    c.dma("sp", kt.t[:, :], k_d.ap()[r0:r0 + L, :], writes=(kt,))
                    c.dma("sp", vc.t[:, :], v_d.ap()[r0:r0 + L, :], writes=(vc,))
                    c.dma("sp", oc.t[:, :], og_d.ap()[r0:r0 + L, :], writes=(oc,))
                    Fc = Fb.t[:, 1 + c0:1 + c0 + L]
                    mc = mb.t[:, 1 + c0:1 + c0 + L]
                    c.op("dve", lambda e, Fc=Fc, c0=c0: e.tensor_scalar(out=bcum.t[:, :], in0=Fc, scalar1=Fb.t[:, c0:c0 + 1],
                                                                       scalar2=None, op0=ALU.subtract), reads=(Fb,),
                         writes=(bcum,))
                    c.op("dve", lambda e, mc=mc: e.tensor_tensor(out=Mr.t[:, :], in0=mc, in1=bcum.t[:, :], op=ALU.subtract),
                         reads=(mb, bcum), writes=(Mr,))
                    c.op("dve", lambda e: e.tensor_scalar(out=sm.t[:, 0:1], in0=Mr.t[:, L - 1:L], scalar1=-1.0, scalar2=None,
                                                          op0=ALU.mult), reads=(Mr,), writes=(sm,))
                    c.op("dve", lambda e, c0=c0: e.tensor_tensor(out=sm.t[:, 1:2], in0=mb.t[:, c0:c0 + 1], in1=Mr.t[:, L - 1:L],
                                                                 op=ALU.subtract), reads=(mb, Mr, sm), writes=(sm,))
                    c.op("act", lambda e: e.activation(out=sm.t[:, 2:3], in_=sm.t[:, 1:2], func=AF.Exp), reads=(sm,),
                         writes=(sm,))
                    c.op("dve", lambda e, c0=c0: e.tensor_tensor(out=rp.t[:, 0, :], in0=ir.t[:, c0:c0 + L], in1=bcum.t[:, :],
                                                                 op=ALU.subtract), reads=(ir, bcum), writes=(rp,))
                    c.op("act", lambda e: e.activation(out=rp.t[:, 0, :], in_=rp.t[:, 0, :], func=AF.Exp, bias=sm.t[:, 0:1],
                                                       scale=1.0), reads=(rp, sm), writes=(rp,))
                    c.op("dve", lambda e: e.tensor_scalar(out=rp.t[:, 1, :], in0=Mr.t[:, :], scalar1=sm.t[:, 0:1], scalar2=None,
                                                          op0=ALU.add), reads=(Mr, sm), writes=(rp,))
                    c.op("act", lambda e: e.activation(out=rp.t[:, 1, :], in_=rp.t[:, 1, :], func=AF.Exp, scale=-1.0),
                         reads=(rp,), writes=(rp,))
                    c.op("act", lambda e, mc=mc: e.activation(out=rp.t[:, 2, :], in_=mc, func=AF.Exp, scale=-1.0),
                         reads=(mb,), writes=(rp,))
                    pm = psb[6]
                    for q3 in range(3):
                        c.op("pe", lambda e, q3=q3: e.transpose(out=pm.t[:L, 32 + q3 * 8:32 + (q3 + 1) * 8], in_=rp.t[:, q3, :],
                                                                identity=ident.t[:8, :8]), reads=(rp, ident), writes=(pm,))
                    c.op("dve", lambda e: e.tensor_copy(out=colp.t[:, :], in_=pm.t[:L, 32:56]), reads=(pm,), writes=(colp,))
                    c.op("dve", lambda e: e.tensor_scalar(out=dg.t[:, :], in0=ident.t[:8, :8], scalar1=sm.t[:, 2:3], scalar2=None,
                                                          op0=ALU.mult), reads=(ident, sm), writes=(dg,))
                    c.op("pe", lambda e: e.matmul(pm.t[:, 64:72], lhsT=C["ones_f"].t[:8, :128], rhs=dg.t[:, :], start=True,
                                                  stop=True), reads=(C["ones_f"], dg), writes=(pm,))
                    c.op("dve", lambda e: e.tensor_copy(out=gbcst.t[:, :], in_=pm.t[:, 64:72]), reads=(pm,), writes=(gbcst,))
                    beta = colp.t[:, 0:8]
                    c.op("pool", lambda e: e.tensor_tensor(
                        out=mbeta.t[:, :, :], in0=C["tri"].t[:L, :L].unsqueeze(1).broadcast_to([L, H, L]),
                        in1=beta.unsqueeze(2).broadcast_to([L, H, L]), op=ALU.mult), reads=(C["tri"], colp), writes=(mbeta,))
                    c.op("pool", lambda e, kt=kt: e.tensor_tensor(
                        out=kp.t[:, :, :], in0=kt.t[:, :].rearrange("p (h d) -> p h d", h=H),
                        in1=beta.unsqueeze(2).broadcast_to([L, H, 128]), op=ALU.mult), reads=(kt, colp), writes=(kp,))
                    c.op("dve", lambda e: e.tensor_tensor(out=CT.t[:, :, :], in0=CT.t[:, :, :],
                                                          in1=gbcst.t[:, :].unsqueeze(2).broadcast_to([128, H, 256]),
                                                          op=ALU.mult), reads=(CT, gbcst), writes=(CT,))
                    c.op("act", lambda e: e.copy(out=CTb.t[:, :, :], in_=CT.t[:, :, :]), reads=(CT,), writes=(CTb,))
                    c.op("dve", lambda e: e.tensor_tensor(out=nT.t[:, :], in0=nT.t[:, :], in1=gbcst.t[:, :], op=ALU.mult),
                         reads=(nT, gbcst), writes=(nT,))
                    c.op("dve", lambda e: e.tensor_copy(out=nTb.t[:, :], in_=nT.t[:, :]), reads=(nT,), writes=(nTb,))
                    pD = psb[7]
                    for hh in range(2):
                        pS = psb[hh]
                        for hl in range(4):
                            h = hh * 4 + hl
                            c.op("pe", lambda e, h=h, hl=hl, pS=pS, c0=c0: e.matmul(
                                pS.t[:L, hl * L:(hl + 1) * L], lhsT=kT_t.t[:, h, c0:c0 + L], rhs=qT_t.t[:, h, c0:c0 + L],
                                start=True, stop=True), reads=(kT_t, qT_t), writes=(pS,))
                        c.op("dve", lambda e, hh=hh, pS=pS: e.tensor_tensor(
                            out=Sp.t[:, hh * 4:(hh + 1) * 4, :], in0=pS.t[:L, :4 * L].rearrange("p (h t) -> p h t", h=4),
                            in1=mbeta.t[:, hh * 4:(hh + 1) * 4, :], op=ALU.mult), reads=(pS, mbeta), writes=(Sp,))
                        for hl in range(4):
                            h = hh * 4 + hl
                            pI = psb[2 + hl // 2]
                            co = (hl % 2) * 256
                            c.op("pe", lambda e, h=h, pI=pI, co=co, vc=vc: e.matmul(
                                pI.t[:L, co:co + 256], lhsT=Sp.t[:, h, :], rhs=vc.t[:, h * 256:(h + 1) * 256], start=True,
                                stop=False), reads=(Sp, vc), writes=(pI,))
                            c.op("pe", lambda e, h=h, pI=pI, co=co, c0=c0: e.matmul(
                                pI.t[:L, co:co + 256], lhsT=qT_t.t[:, h, c0:c0 + L], rhs=CTb.t[:, h, :], start=False,
                                stop=True), reads=(qT_t, CTb), writes=(pI,))
                            c.op("pe", lambda e, h=h: e.matmul(pD.t[:L, h:h + 1], lhsT=Sp.t[:, h, :], rhs=C["ones_b"].t[:L, 0:1],
                                                              start=True, stop=False), reads=(Sp, C["ones_b"]), writes=(pD,))
                            c.op("pe", lambda e, h=h, c0=c0: e.matmul(pD.t[:L, h:h + 1], lhsT=qT_t.t[:, h, c0:c0 + L],
                                                                     rhs=nTb.t[:, h:h + 1], start=False, stop=True),
                                 reads=(qT_t, nTb), writes=(pD,))
                        hs4 = slice(hh * 4, (hh + 1) * 4)
                        A_ = colp.t[:, 8 + hh * 4:8 + (hh + 1) * 4]
                        fl = colp.t[:, 16 + hh * 4:16 + (hh + 1) * 4]
                        c.op("dve", lambda e, hs4=hs4, A_=A_: e.tensor_tensor(out=dn.t[:, hs4], in0=pD.t[:L, hs4], in1=A_,
                                                                             op=ALU.mult), reads=(pD, colp), writes=(dn,))
                        c.op("act", lambda e, hs4=hs4: e.activation(out=dn.t[:, hs4], in_=dn.t[:, hs4], func=AF.Abs),
                             reads=(dn,), writes=(dn,))
                        c.op("dve", lambda e, hs4=hs4, fl=fl: e.tensor_tensor(out=dn.t[:, hs4], in0=dn.t[:, hs4], in1=fl,
                                                                             op=ALU.max), reads=(dn, colp), writes=(dn,))
                        c.op("dve", lambda e, hs4=hs4: e.reciprocal(out=dn.t[:, hs4], in_=dn.t[:, hs4]), reads=(dn,),
                             writes=(dn,))
                        c.op("dve", lambda e, hs4=hs4, A_=A_: e.tensor_tensor(out=sc_.t[:, hs4], in0=A_, in1=dn.t[:, hs4],
                                                                             op=ALU.mult), reads=(dn, colp), writes=(sc_,))
                        for q2 in range(2):
                            pI = psb[2 + q2]
                            h0 = hh * 4 + q2 * 2
                            c.op("dve", lambda e, pI=pI, h0=h0: e.tensor_tensor(
                                out=hb.t[:, h0 * 256:(h0 + 2) * 256].rearrange("p (h e) -> p h e", h=2),
                                in0=pI.t[:L, :512].rearrange("p (h e) -> p h e", h=2),
                                in1=sc_.t[:, h0:h0 + 2].unsqueeze(2).broadcast_to([L, 2, 256]), op=ALU.mult),
                                reads=(pI, sc_), writes=(hb,))
                        for hl in range(4):
                            h = hh * 4 + hl
                            pC = psb[4 + hl // 2]
                            co = (hl % 2) * 256
                            c.op("pe", lambda e, h=h, pC=pC, co=co, vc=vc: e.matmul(
                                pC.t[:, co:co + 256], lhsT=kp.t[:, h, :], rhs=vc.t[:, h * 256:(h + 1) * 256], start=True,
                                stop=True), reads=(kp, vc), writes=(pC,))
                            c.op("pe", lambda e, h=h: e.matmul(pD.t[:, 16 + h:17 + h], lhsT=kp.t[:, h, :],
                                                              rhs=C["ones_b"].t[:L, 0:1], start=True, stop=True),
                                 reads=(kp, C["ones_b"]), writes=(pD,))
                        for q2 in range(2):
                            pC = psb[4 + q2]
                            h0 = hh * 4 + q2 * 2
                            c.op("dve", lambda e, pC=pC, h0=h0: e.tensor_tensor(
                                out=CT.t[:, h0:h0 + 2, :], in0=CT.t[:, h0:h0 + 2, :],
                                in1=pC.t[:, :512].rearrange("p (h e) -> p h e", h=2), op=ALU.add), reads=(pC, CT), writes=(CT,))
                    c.op("dve", lambda e: e.tensor_tensor(out=nT.t[:, :], in0=nT.t[:, :], in1=pD.t[:, 16:24], op=ALU.add),
                         reads=(pD, nT), writes=(nT,))
                    c.op("pool", lambda e: e.tensor_tensor(out=hsq.t[:, :], in0=hb.t[:, :], in1=hb.t[:, :], op=ALU.mult),
                         reads=(hb,), writes=(hsq,))
                    c.op("dve", lambda e: e.tensor_reduce(out=ss.t[:, :], in_=hsq.t[:, :].rearrange("p (h e) -> p h e", h=H),
                                                          axis=mybir.AxisListType.X, op=ALU.add), reads=(hsq,), writes=(ss,))
                    c.op("act", lambda e: e.activation(out=ss.t[:, :], in_=ss.t[:, :], func=AF.Sqrt, bias=self.eps_t.t[:L, :],
                                                       scale=1.0 / 256.0), reads=(ss, self.eps_t), writes=(ss,))
                    c.op("dve", lambda e: e.reciprocal(out=ss.t[:, :], in_=ss.t[:, :]), reads=(ss,), writes=(ss,))
                    c.op("dve", lambda e: e.tensor_tensor(out=hb.t[:, :].rearrange("p (h e) -> p h e", h=H),
                                                          in0=hb.t[:, :].rearrange("p (h e) -> p h e", h=H),
                                                          in1=ss.t[:, :].unsqueeze(2).broadcast_to([L, H, 256]), op=ALU.mult),
                         reads=(hb, ss), writes=(hb,))
                    c.op("pool", lambda e: e.tensor_tensor(out=hb.t[:, :], in0=hb.t[:, :], in1=ngbc.t[:, :], op=ALU.mult),
                         reads=(hb, ngbc), writes=(hb,))
                    c.op("pool", lambda e, oc=oc: e.tensor_tensor(out=hb.t[:, :], in0=hb.t[:, :], in1=oc.t[:, :], op=ALU.mult),
                         reads=(hb, oc), writes=(hb,))
                    for g4 in range(4):
                        pT = psb[6] if g4 % 2 == 0 else psb[7]
                        for j4 in range(4):
                            kc = g4 * 4 + j4
                            c.op("pe", lambda e, kc=kc, j4=j4, pT=pT: e.transpose(
                                out=pT.t[:, 128 + j4 * L:128 + (j4 + 1) * L], in_=hb.t[:L, kc * 128:(kc + 1) * 128],
                                identity=ident.t[:L, :L]), reads=(hb, ident), writes=(pT,))
                        c.op("act", lambda e, g4=g4, pT=pT, c0=c0: e.copy(
                            out=hT_t.t[:, g4 * 4:(g4 + 1) * 4, c0:c0 + L],
                            in_=pT.t[:, 128:128 + 4 * L].rearrange("p (a b) -> p a b", b=L)), reads=(pT,), writes=(hT_t,))
                c.dma("sp", hT_d.ap().rearrange("(kc p) t -> p kc t", p=128)[:, :, t0:t0 + TT], hT_t.t[:, :, :], reads=(hT_t,))
                c.op("dve", lambda e: e.tensor_copy(out=Fb.t[:, 0:1], in_=Fb.t[:, TT:TT + 1]), reads=(Fb,), writes=(Fb,))
                c.op("dve", lambda e: e.tensor_copy(out=mb.t[:, 0:1], in_=mb.t[:, TT:TT + 1]), reads=(mb,), writes=(mb,))
            cout = c.sb("b_cout", [128, H, 2, 128], F32, es)
            for h in range(H):
                for eh in range(2):
                    pb = psb[(h * 2 + eh) % 2]
                    c.op("pe", lambda e, h=h, eh=eh, pb=pb: e.transpose(out=pb.t[:, :128], in_=CT.t[:, h, eh * 128:(eh + 1) * 128],
                                                                         identity=ident.t[:, :]), reads=(CT, ident), writes=(pb,))
                    c.op("dve", lambda e, h=h, eh=eh, pb=pb: e.tensor_copy(out=cout.t[:, h, eh, :], in_=pb.t[:, :128]),
                         reads=(pb,), writes=(cout,))
            c.dma("sp", C_out.ap().rearrange("h (eh el) d -> el h eh d", el=128), cout.t[:, :, :, :], reads=(cout,))
            nout = c.sb("b_nout", [8, 128], F32, es)
            c.op("pe", lambda e: e.transpose(out=psb[6].t[:8, :128], in_=nT.t[:, :], identity=ident.t[:, :]),
                 reads=(nT, ident), writes=(psb[6],))
            c.op("dve", lambda e: e.tensor_copy(out=nout.t[:, :], in_=psb[6].t[:8, :128]), reads=(psb[6],), writes=(nout,))
            c.dma("sp", n_out.ap(), nout.t[:, :], reads=(nout,))
            c.dma("sp", m_out.ap().rearrange("(h o) -> h o", o=1), mb.t[:, 0:1], reads=(mb,))
            c.barrier()
        self.outproj_stage(c, S, li, hT_d, KC, wout_b, wout_r, x_in, x_out, xT_out, W, ident, psb, "b")

    def outproj_stage(self, c, S, li, hT_d, nkc, wout_b, wout_r, x_in, x_out, xT_out, W, ident, psb, pre):
        P, NS, TT = S.P, S.NS, S.TT
        with ExitStack() as es:
            oT = c.sb(pre + "_oT", [128, nkc, TT], BF16, es)
            xres = c.sb(pre + "_xres", [P, NS, D], F32, es)
            wo = [c.sb(pre + "_wo%d" % i, [128, nkc, 256], BF16, es) for i in range(2)]
            lnb = self.ln_bufs(c, es, P, pre)
            gbc = self.load_bc(c, es, pre + "_g", W["ln1_g"].ap()[li:li + 1, :], P, D)
            bbc = self.load_bc(c, es, pre + "_b", W["ln1_b"].ap()[li:li + 1, :], P, D)
            cnts = {"wo": 0, "po": 0, "tc": [0]}
            for tile in range(S.ntiles):
                t0 = tile * TT
                c.dma("sp", oT.t[:, :, :], hT_d.ap().rearrange("(kc p) t -> p kc t", p=128)[:, :, t0:t0 + TT], writes=(oT,))
                c.dma("sp", xres.t[:, :, :], x_in.ap()[t0:t0 + TT, :].rearrange("(s p) d -> p s d", p=P), writes=(xres,))
                self.out_proj_ln(c, S, es, oT, nkc, wout_b, wout_r, xres, gbc, bbc, lnb, x_out, xT_out, tile, ident,
                                 psb[4:6], psb[6:8], wo, cnts)
            c.barrier()


    def c_stage(self, c, S, key, li, j, x_in, xT_in, x_out, xT_out, W, ident, psb):
        nc = self.nc
        P, NS, TT, T = S.P, S.NS, S.TT, S.T
        smp = key == "s"
        L = 32 if smp else 64
        NCH = TT // L
        H = 8
        qT_d = self.dscr("c_qT_" + key, [D, T], BF16)
        kT_d = self.dscr("c_kT_" + key, [D, T], BF16)
        kpp_d = self.dscr("c_kpp_" + key, [T, D], BF16)
        v_d = self.dscr("c_v_" + key, [T, 2 * D], BF16)
        g_d = self.dscr("c_g_" + key, [T, 2 * D], F32)
        hT_d = self.dscr("c_hT_" + key, [2 * D, T], BF16)
        S_out = self.c_outs[key]
        win_b, win_r = W["c_w_in_b"][j]
        wout_b, wout_r = W["c_w_out_b"][j]
        C = self.consts
        CI = self.cin
        with ExitStack() as es:
            xT = c.sb("c_xT", [128, KC, TT], BF16, es)
            ws_ = [c.sb("c_w%d" % i, [128, KC, 256], BF16, es) for i in range(2)]
            stg = [c.sb("c_stg%d" % i, [128, 2, TT], BF16, es) for i in range(2)]
            x1s = c.sb("c_x1s", [128, TT], F32, es)
            rt = [c.sb("c_rt%d" % i, [128, TT], F32, es) for i in range(4)]
            csf = c.sb("c_csf", [128, 4, TT], F32, es)
            cst = c.sb("c_cst", [P, NS, 2, 128], F32, es)
            tmf = c.sb("c_tmf", [P, NS, D], F32, es)
            tm1 = c.sb("c_tm1", [P, H, 128], F32, es)
            tm2 = c.sb("c_tm2", [P, H, 128], F32, es)
            o16 = c.sb("c_o16", [P, NS, D], BF16, es)
            dkt = c.sb("c_dkt", [P, H], F32, es)
            c.dma("sp", dkt.t[:, :], CI["c_dk_" + key].ap(), writes=(dkt,))
            cw = [0]
            pcnt = [0]
            for tile in range(S.ntiles):
                t0 = tile * TT
                c.dma("sp", xT.t[:, :, :], xT_in.ap().rearrange("(kc p) t -> p kc t", p=128)[:, :, t0:t0 + TT],
                      writes=(xT,))
                c.dma("sp", csf.t[:, 0, :], CI["c_cosT_" + key].ap()[:, t0:t0 + TT], writes=(csf,))
                c.dma("sp", csf.t[:, 1, :], CI["c_sinT_" + key].ap()[:, t0:t0 + TT], writes=(csf,))
                c.op("pool", lambda e: e.tensor_scalar(out=csf.t[:, 2:4, :], in0=csf.t[:, 0:2, :], scalar1=1.0 / 16.0,
                                                       scalar2=None, op0=ALU.mult), reads=(csf,), writes=(csf,))
                c.dma("sp", cst.t[:, :, 0, :], CI["c_cos_" + key].ap()[t0:t0 + TT, :].rearrange("(s p) f -> p s f", p=P),
                      writes=(cst,))
                c.dma("sp", cst.t[:, :, 1, :], CI["c_sin_" + key].ap()[t0:t0 + TT, :].rearrange("(s p) f -> p s f", p=P),
                      writes=(cst,))
                c.op("pool", lambda e: e.tensor_scalar(out=cst.t[:, :, :, :], in0=cst.t[:, :, :, :], scalar1=1.0 / 16.0,
                                                       scalar2=None, op0=ALU.mult), reads=(cst,), writes=(cst,))
                for part, dst, ci in ((0, qT_d, 0), (1, kT_d, 2)):
                    def ev(hc, pb, dst=dst, ci=ci):
                        if hc % 2 == 0:
                            c.op("act", lambda e: e.copy(out=x1s.t[:, :], in_=pb.t[:, :TT]), reads=(pb,), writes=(x1s,))
                            return
                        sg_ = stg[(hc // 2) % 2]
                        cos_, sin_ = csf.t[:, ci, :], csf.t[:, ci + 1, :]
                        c.op("pool", lambda e: e.tensor_tensor(out=rt[0].t[:, :], in0=x1s.t[:, :], in1=cos_, op=ALU.mult),
                             reads=(x1s, csf), writes=(rt[0],))
                        c.op("dve", lambda e: e.tensor_tensor(out=rt[1].t[:, :], in0=pb.t[:, :TT], in1=sin_, op=ALU.mult),
                             reads=(pb, csf), writes=(rt[1],))
                        c.op("pool", lambda e: e.tensor_tensor(out=sg_.t[:, 0, :], in0=rt[0].t[:, :], in1=rt[1].t[:, :],
                                                               op=ALU.subtract), reads=(rt[0], rt[1]), writes=(sg_,))
                        c.op("pool", lambda e: e.tensor_tensor(out=rt[2].t[:, :], in0=x1s.t[:, :], in1=sin_, op=ALU.mult),
                             reads=(x1s, csf), writes=(rt[2],))
                        c.op("dve", lambda e: e.tensor_tensor(out=rt[3].t[:, :], in0=pb.t[:, :TT], in1=cos_, op=ALU.mult),
                             reads=(pb, csf), writes=(rt[3],))
                        c.op("pool", lambda e: e.tensor_tensor(out=sg_.t[:, 1, :], in0=rt[2].t[:, :], in1=rt[3].t[:, :],
                                                               op=ALU.add), reads=(rt[2], rt[3]), writes=(sg_,))
                        h = hc // 2
                        c.dma("sp", dst.ap()[h * 256:(h + 1) * 256, t0:t0 + TT].rearrange("(a p) t -> p a t", p=128),
                              sg_.t[:, :, :], reads=(sg_,))
                    self.proj_fm(c, xT, TT, win_b, win_r, part * D, 16, ws_, cw, psb[0:2], pcnt, ev)
                def evk(s_, og, pb):
                    c.op("act", lambda e: e.copy(out=tmf.t[:P, s_, og * 256:(og + 1) * 256], in_=pb.t[:P, :256]),
                         reads=(pb,), writes=(tmf,))
                self.proj_tm(c, xT, S, win_b, win_r, D, D, ws_, cw, psb[2:4], pcnt, evk)
                for s_ in range(NS):
                    kv = tmf.t[:P, s_, :].rearrange("p (h a f) -> p h a f", h=H, a=2)
                    ov = o16.t[:P, s_, :].rearrange("p (h a f) -> p h a f", h=H, a=2)
                    cos_ = cst.t[:P, s_, 0, :].unsqueeze(1).broadcast_to([P, H, 128])
                    sin_ = cst.t[:P, s_, 1, :].unsqueeze(1).broadcast_to([P, H, 128])
                    c.op("dve", lambda e, kv=kv, cos_=cos_: e.tensor_tensor(out=tm1.t[:, :, :], in0=kv[:, :, 0, :], in1=cos_,
                                                                           op=ALU.mult), reads=(tmf, cst), writes=(tm1,))
                    c.op("pool", lambda e, kv=kv, sin_=sin_: e.tensor_tensor(out=tm2.t[:, :, :], in0=kv[:, :, 1, :], in1=sin_,
                                                                            op=ALU.mult), reads=(tmf, cst), writes=(tm2,))
                    c.op("dve", lambda e: e.tensor_tensor(out=tm1.t[:, :, :], in0=tm1.t[:, :, :], in1=tm2.t[:, :, :],
                                                          op=ALU.subtract), reads=(tm1, tm2), writes=(tm1,))
                    c.op("pool", lambda e, kv=kv, sin_=sin_: e.tensor_tensor(out=tm2.t[:, :, :], in0=kv[:, :, 0, :], in1=sin_,
                                                                            op=ALU.mult), reads=(tmf, cst, tm1), writes=(tm2,))
                    c.op("dve", lambda e, ov=ov: e.tensor_tensor(
                        out=ov[:, :, 0, :], in0=tm1.t[:, :, :], in1=dkt.t[:P, :].unsqueeze(2).broadcast_to([P, H, 128]),
                        op=ALU.mult), reads=(tm1, dkt), writes=(o16,))
                    c.op("dve", lambda e, kv=kv, cos_=cos_: e.tensor_tensor(out=tm1.t[:, :, :], in0=kv[:, :, 1, :], in1=cos_,
                                                                           op=ALU.mult), reads=(tmf, cst, o16), writes=(tm1,))
                    c.op("pool", lambda e: e.tensor_tensor(out=tm2.t[:, :, :], in0=tm2.t[:, :, :], in1=tm1.t[:, :, :],
                                                           op=ALU.add), reads=(tm1, tm2), writes=(tm2,))
                    c.op("pool", lambda e, ov=ov: e.tensor_tensor(
                        out=ov[:, :, 1, :], in0=tm2.t[:, :, :], in1=dkt.t[:P, :].unsqueeze(2).broadcast_to([P, H, 128]),
                        op=ALU.mult), reads=(tm2, dkt), writes=(o16,))
                c.dma("sp", kpp_d.ap()[t0:t0 + TT, :].rearrange("(s p) d -> p s d", p=P), o16.t[:, :, :], reads=(o16,))
                for hf in range(2):
                    def evv(s_, og, pb):
                        c.op("act", lambda e: e.copy(out=o16.t[:P, s_, og * 256:(og + 1) * 256], in_=pb.t[:P, :256]),
                             reads=(pb,), writes=(o16,))
                    self.proj_tm(c, xT, S, win_b, win_r, 2 * D + hf * D, D, ws_, cw, psb[2:4], pcnt, evv)
                    c.dma("sp", v_d.ap()[t0:t0 + TT, hf * D:(hf + 1) * D].rearrange("(s p) d -> p s d", p=P), o16.t[:, :, :],
                          reads=(o16,))
                for hf in range(2):
                    def evg(s_, og, pb):
                        c.op("act", lambda e: e.activation(out=tmf.t[:P, s_, og * 256:(og + 1) * 256], in_=pb.t[:P, :256],
                                                           func=AF.Silu), reads=(pb,), writes=(tmf,))
                    self.proj_tm(c, xT, S, win_b, win_r, 4 * D + hf * D, D, ws_, cw, psb[2:4], pcnt, evg)
                    c.dma("sp", g_d.ap()[t0:t0 + TT, hf * D:(hf + 1) * D].rearrange("(s p) d -> p s d", p=P), tmf.t[:, :, :],
                          reads=(tmf,))
            c.barrier()
        with ExitStack() as es:
            qT_t = c.sb("c_qTt", [128, 16, TT], BF16, es)
            kT_t = c.sb("c_kTt", [128, 16, TT], BF16, es)
            qd = c.sb("c_qd", [128, 16, L], BF16, es)
            kpc = [c.sb("c_kpc%d" % i, [L, D], BF16, es) for i in range(2)]
            v_c = [c.sb("c_vc%d" % i, [L, 2 * D], BF16, es) for i in range(2)]
            g_c = c.sb("c_gc", [L, 2 * D], F32, es)
            Sst = c.sb("c_S", [128, H, 2, 512], F32, es)
            Sb = c.sb("c_Sb", [128, H, 2, 512], BF16, es)
            Sp = c.sb("c_Sp", [L, H, L], BF16, es)
            ob = c.sb("c_ob", [L, 2 * D], F32, es)
            hT_cs = [c.sb("c_hTc%d" % i, [128, 32, L], BF16, es) for i in range(2)]
            gng = self.load_bc(c, es, "c_gng", W["c_gn_g"].ap()[j:j + 1, :], L, 2 * D)
            gnb = self.load_bc(c, es, "c_gnb", W["c_gn_b"].ap()[j:j + 1, :], L, 2 * D)
            dint = c.sb("c_dint", [L, H, L], F32, es)
            dq = c.sb("c_dq", [128, H, L], F32, es)
            st = c.sb("c_st", [L, H, 6], F32, es)
            mv = c.sb("c_mv", [L, H, 2], F32, es)
            rs = c.sb("c_rs", [L, H], F32, es)
            nmr = c.sb("c_nmr", [L, H], F32, es)
            c.dma("sp", dint.t[:, :, :], CI["c_dint_" + key].ap(), writes=(dint,))
            c.dma("sp", dq.t[:, :, :], CI["c_dq_" + key].ap(), writes=(dq,))
            if smp:
                c.dma("sp", Sst.t[:, :, :, :], W["state_c_S"].ap().rearrange("h (a dl) e -> dl h a e", dl=128), writes=(Sst,))
            else:
                c.op("pool", lambda e: e.memset(Sst.t[:, :, :, :], 0.0), writes=(Sst,))
            c.op("act", lambda e: e.copy(out=Sb.t[:, :, :, :], in_=Sst.t[:, :, :, :]), reads=(Sst,), writes=(Sb,))
            ds = self.c_decay_s[key]
            cc = 0
            ucnt = 0
            for tile in range(S.ntiles):
                t0 = tile * TT
                c.dma("sp", qT_t.t[:, :, :], qT_d.ap().rearrange("(kc p) t -> p kc t", p=128)[:, :, t0:t0 + TT], writes=(qT_t,))
                c.dma("sp", kT_t.t[:, :, :], kT_d.ap().rearrange("(kc p) t -> p kc t", p=128)[:, :, t0:t0 + TT], writes=(kT_t,))
                for ch in range(NCH):
                    c0 = ch * L
                    r0 = t0 + c0
                    kc_ = kpc[cc % 2]
                    vc = v_c[cc % 2]
                    cc += 1
                    c.dma("sp", kc_.t[:, :], kpp_d.ap()[r0:r0 + L, :], writes=(kc_,))
                    c.dma("sp", vc.t[:, :], v_d.ap()[r0:r0 + L, :], writes=(vc,))
                    c.dma("sp", g_c.t[:, :], g_d.ap()[r0:r0 + L, :], writes=(g_c,))
                    c.op("pool", lambda e, c0=c0: e.tensor_tensor(
                        out=qd.t[:, :, :].rearrange("p (h a) t -> p h a t", a=2),
                        in0=qT_t.t[:, :, c0:c0 + L].rearrange("p (h a) t -> p h a t", a=2),
                        in1=dq.t[:, :, :].unsqueeze(2).broadcast_to([128, H, 2, L]), op=ALU.mult), reads=(qT_t, dq), writes=(qd,))
                    for hh in range(2):
                        pS = psb[hh]
                        for hl in range(4):
                            h = hh * 4 + hl
                            for a in range(2):
                                c.op("pe", lambda e, h=h, hl=hl, a=a, pS=pS, c0=c0: e.matmul(
                                    pS.t[:L, hl * L:(hl + 1) * L], lhsT=kT_t.t[:, 2 * h + a, c0:c0 + L],
                                    rhs=qT_t.t[:, 2 * h + a, c0:c0 + L], start=(a == 0), stop=(a == 1)),
                                    reads=(kT_t, qT_t), writes=(pS,))
                        c.op("dve", lambda e, hh=hh, pS=pS: e.tensor_tensor(
                            out=Sp.t[:, hh * 4:(hh + 1) * 4, :], in0=pS.t[:L, :4 * L].rearrange("p (h t) -> p h t", h=4),
                            in1=dint.t[:, hh * 4:(hh + 1) * 4, :], op=ALU.mult), reads=(pS, dint), writes=(Sp,))
                    for h in range(H):
                        pO = psb[2 + h % 2]
                        c.op("pe", lambda e, h=h, pO=pO, vc=vc: e.matmul(pO.t[:L, :512], lhsT=Sp.t[:, h, :],
                                                                        rhs=vc.t[:, h * 512:(h + 1) * 512], start=True, stop=False),
                             reads=(Sp, vc), writes=(pO,))
                        for a in range(2):
                            c.op("pe", lambda e, h=h, a=a, pO=pO: e.matmul(pO.t[:L, :512], lhsT=qd.t[:, 2 * h + a, :],
                                                                          rhs=Sb.t[:, h, a, :], start=False, stop=(a == 1)),
                                 reads=(qd, Sb), writes=(pO,))
                        c.op("dve", lambda e, h=h, pO=pO: e.bn_stats(out=st.t[:, h, :], in_=pO.t[:L, :512]), reads=(pO,),
                             writes=(st,))
                        c.op("dve", lambda e, h=h: e.bn_aggr(out=mv.t[:, h, :], in_=st.t[:, h, :]), reads=(st,), writes=(mv,))
                        c.op("act", lambda e, h=h: e.activation(out=rs.t[:, h:h + 1], in_=mv.t[:, h, 1:2], func=AF.Sqrt,
                                                                bias=self.eps_t.t[:L, :], scale=1.0), reads=(mv, self.eps_t),
                             writes=(rs,))
                        c.op("dve", lambda e, h=h: e.reciprocal(out=rs.t[:, h:h + 1], in_=rs.t[:, h:h + 1]), reads=(rs,),
                             writes=(rs,))
                        c.op("dve", lambda e, h=h: e.scalar_tensor_tensor(out=nmr.t[:, h:h + 1], in0=mv.t[:, h, 0:1], scalar=-1.0,
                                                                          in1=rs.t[:, h:h + 1], op0=ALU.mult, op1=ALU.mult),
                             reads=(mv, rs), writes=(nmr,))
                        c.op("act", lambda e, h=h, pO=pO: e.activation(out=ob.t[:, h * 512:(h + 1) * 512], in_=pO.t[:L, :512],
                                                                       func=AF.Identity, bias=nmr.t[:, h:h + 1],
                                                                       scale=rs.t[:, h:h + 1]), reads=(pO, rs, nmr), writes=(ob,))
                    for h in range(H):
                        for a in range(2):
                            pU = psb[4 + ucnt % 2]
                            ucnt += 1
                            c.op("pe", lambda e, h=h, a=a, pU=pU, kc_=kc_, vc=vc: e.matmul(
                                pU.t[:, :512], lhsT=kc_.t[:, h * 256 + a * 128:h * 256 + (a + 1) * 128],
                                rhs=vc.t[:, h * 512:(h + 1) * 512], start=True, stop=True), reads=(kc_, vc), writes=(pU,))
                            c.op("dve", lambda e, h=h, a=a, pU=pU: e.scalar_tensor_tensor(
                                out=Sst.t[:, h, a, :], in0=Sst.t[:, h, a, :], scalar=float(ds[h]), in1=pU.t[:, :512],
                                op0=ALU.mult, op1=ALU.add), reads=(Sst, pU), writes=(Sst,))
                    c.op("act", lambda e: e.copy(out=Sb.t[:, :, :, :], in_=Sst.t[:, :, :, :]), reads=(Sst,), writes=(Sb,))
                    c.op("pool", lambda e: e.tensor_tensor(out=ob.t[:, :], in0=ob.t[:, :], in1=gng.t[:, :], op=ALU.mult),
                         reads=(ob, gng), writes=(ob,))
                    c.op("pool", lambda e: e.tensor_tensor(out=ob.t[:, :], in0=ob.t[:, :], in1=gnb.t[:, :], op=ALU.add),
                         reads=(ob, gnb), writes=(ob,))
                    c.op("pool", lambda e: e.tensor_tensor(out=ob.t[:, :], in0=ob.t[:, :], in1=g_c.t[:, :], op=ALU.mult),
                         reads=(ob, g_c), writes=(ob,))
                    hT_c = hT_cs[cc % 2]
                    for g4 in range(8):
                        pT = psb[6 + g4 % 2]
                        for j4 in range(4):
                            kc = g4 * 4 + j4
                            c.op("pe", lambda e, kc=kc, j4=j4, pT=pT: e.transpose(
                                out=pT.t[:, j4 * L:(j4 + 1) * L], in_=ob.t[:L, kc * 128:(kc + 1) * 128],
                                identity=ident.t[:L, :L]), reads=(ob, ident), writes=(pT,))
                        c.op("act", lambda e, g4=g4, pT=pT, hT_c=hT_c: e.copy(
                            out=hT_c.t[:, g4 * 4:(g4 + 1) * 4, :],
                            in_=pT.t[:, :4 * L].rearrange("p (a b) -> p a b", b=L)), reads=(pT,), writes=(hT_c,))
                    c.dma("sp", hT_d.ap().rearrange("(kc p) t -> p kc t", p=128)[:, :, r0:r0 + L], hT_c.t[:, :, :], reads=(hT_c,))
            c.dma("sp", S_out.ap().rearrange("h (a dl) e -> dl h a e", dl=128), Sst.t[:, :, :, :], reads=(Sst,))
            c.barrier()
        self.outproj_stage(c, S, li, hT_d, 32, wout_b, wout_r, x_in, x_out, xT_out, W, ident, psb, "c")

    def build(self):
        nc = self.nc
        TP, TS = self.TP, self.TS
        SP = Seg("p", TP, min(512, TP))
        SS = Seg("s", TS, TS)
        self.SP, self.SS = SP, SS
        W = {}
        self.W = W
        L = self.layers
        xp = self.din("x_prompt", [TP, D])
        xs = self.din("x_sample", [TS, D])
        ident_d = self.din("c_ident", [128, 128])
        for nm in ("ln1_g", "ln1_b", "ln2_g", "ln2_b"):
            W[nm] = self.din(nm, [DEPTH, D])
        yp = self.dout("y_prompt", [TP, D])
        ys = self.dout("y_sample", [TS, D])
        wlist = []
        if self.do_ffn:
            ffn_w_in = self.din("ffn_w_in", [DEPTH, D, 2 * FFN_H])
            ffn_w_out = self.din("ffn_w_out", [DEPTH, FFN_H, D])
            for li in L:
                wlist.append(("ffn_w_in_b", li, _sub(ffn_w_in, li), D, 2 * FFN_H))
                wlist.append(("ffn_w_out_b", li, _sub(ffn_w_out, li), FFN_H, D))
        mix = self.cfg.get("mixers", True)
        if mix and 0 in L:
            a_w_in = self.din("a_w_in", [1, D, 2 * D])
            a_w_out = self.din("a_w_out", [1, D, D])
            W["a_b_in"] = self.din("a_b_in", [1, 2 * D])
            W["a_vn_g"] = self.din("a_vn_g", [1, D])
            W["a_vn_b"] = self.din("a_vn_b", [1, D])
            W["a_w_s"] = self.din("a_w_s", [1, 8, 128, 128])
            W["a_b_s"] = self.din("a_b_s", [1, 8, 128])
            wlist.append(("a_w_in_b", 0, _sub(a_w_in, 0), D, 2 * D))
            wlist.append(("a_w_out_b", 0, _sub(a_w_out, 0), D, D))
            self.av_out = self.dout("new_a_v_sample", [TS, D])
        self.declare_more(W, wlist, mix, L)
        xtm = {"p": [self.dscr("xp_tm%d" % i, [TP, D], F32) for i in range(2)],
               "s": [self.dscr("xs_tm%d" % i, [TS, D], F32) for i in range(2)]}
        xT = {"p": [self.dscr("xp_T%d" % i, [D, TP], BF16) for i in range(2)],
              "s": [self.dscr("xs_T%d" % i, [D, TS], BF16) for i in range(2)]}
        with ExitStack() as es:
            c = Ctx(nc, es)
            self.c = c
            psb = [Buf(es.enter_context(nc.psum_tensor("ps%d" % i, [128, 512], F32))) for i in range(8)]
            ident = c.sb("ident", [128, 128], F32)
            c.dma("sp", ident.t[:, :], ident_d.ap(), writes=(ident,))
            self.eps_t = c.sb("eps", [128, 1], F32)
            c.op("pool", lambda e: e.memset(self.eps_t.t[:, :], LN_EPS), writes=(self.eps_t,))
            self.load_consts(c, es)
            for key, li, src, K_, N_ in wlist:
                wb = self.dscr("%s%d" % (key, li), [K_, N_], BF16)
                r = self.convert_weight(c, src, wb, K_, N_)
                W.setdefault(key, {})[li] = (wb, r)
            self.make_xT(c, SP, xp, xT["p"][0], ident, psb)
            self.make_xT(c, SS, xs, xT["s"][0], ident, psb)
            cur = {"p": (xp, xT["p"][0]), "s": (xs, xT["s"][0])}
            flip = {"p": 1, "s": 1}
            nstage = len(L) * ((1 if mix else 0) + (1 if self.do_ffn else 0))
            done = 0
            for idx, li in enumerate(L):
                for S, key, yfin in ((SP, "p", yp), (SS, "s", ys)):
                    kinds = (["m"] if mix else []) + (["f"] if self.do_ffn else [])
                    for kind in kinds:
                        x_in, xT_in = cur[key]
                        f = flip[key]
                        last = (idx == len(L) - 1) and kind == kinds[-1]
                        x_out = yfin if last else xtm[key][f]
                        xT_out = None if last else xT[key][f]
                        if kind == "f":
                            self.ffn_stage(c, S, li, x_in, xT_in, x_out, xT_out, W, ident, psb)
                        else:
                            self.mixer_stage(c, S, key, li, x_in, xT_in, x_out, xT_out, W, ident, psb)
                        cur[key] = (x_out, xT_out)
                        flip[key] = 1 - f
            c.finish()
        return nc

    def declare_more(self, W, wlist, mix, L):
        TP, TS = self.TP, self.TS
        self.cin = {}
        for nm, shp in (("c_tri", [128, 128]), ("c_negm", [128, 4, 512])):
            self.cin[nm] = self.din(nm, shp)
        if mix and 1 in L:
            b_w_in = self.din("b_w_in", [1, D, 6160])
            b_w_out = self.din("b_w_out", [1, D, D])
            W["b_b_gates"] = self.din("b_b_gates", [1, 16])
            W["b_norm_g"] = self.din("b_norm_g", [1, D])
            W["state_b_C"] = self.din("state_b_C", [8, 256, 128])
            W["state_b_n"] = self.din("state_b_n", [8, 128])
            W["state_b_m"] = self.din("state_b_m", [8])
            wlist.append(("b_w_in_b", 0, _sub(b_w_in, 0), D, 6160))
            wlist.append(("b_w_out_b", 0, _sub(b_w_out, 0), D, D))
            self.b_outs = {"p": (self.dout("new_b_C_prompt", [8, 256, 128]), self.dout("new_b_n_prompt", [8, 128]),
                                 self.dout("new_b_m_prompt", [8])),
                           "s": (self.dout("new_b_C_sample", [8, 256, 128]), self.dout("new_b_n_sample", [8, 128]),
                                 self.dout("new_b_m_sample", [8]))}
        if mix and 2 in L:
            c_w_in = self.din("c_w_in", [1, D, 6 * D])
            c_w_out = self.din("c_w_out", [1, 2 * D, D])
            W["c_gn_g"] = self.din("c_gn_g", [1, 2 * D])
            W["c_gn_b"] = self.din("c_gn_b", [1, 2 * D])
            W["state_c_S"] = self.din("state_c_S", [8, 256, 512])
            wlist.append(("c_w_in_b", 0, _sub(c_w_in, 0), D, 6 * D))
            wlist.append(("c_w_out_b", 0, _sub(c_w_out, 0), 2 * D, D))
            self.c_outs = {"p": self.dout("new_c_S_prompt", [8, 256, 512]), "s": self.dout("new_c_S_sample", [8, 256, 512])}
            self.c_decay_s = {}
            for key, T_, L_ in (("p", TP, 64), ("s", TS, 32)):
                P_ = min(128, T_)
                self.cin["c_cosT_" + key] = self.din("c_cosT_" + key, [128, T_])
                self.cin["c_sinT_" + key] = self.din("c_sinT_" + key, [128, T_])
                self.cin["c_cos_" + key] = self.din("c_cos_" + key, [T_, 128])
                self.cin["c_sin_" + key] = self.din("c_sin_" + key, [T_, 128])
                self.cin["c_dk_" + key] = self.din("c_dk_" + key, [P_, 8])
                self.cin["c_dint_" + key] = self.din("c_dint_" + key, [L_, 8, L_])
                self.cin["c_dq_" + key] = self.din("c_dq_" + key, [128, 8, L_])
                self.c_decay_s[key] = ret_tables(T_, L_, 0)["decay_s"]
        if mix and 3 in L:
            d_w_in = self.din("d_w_in", [1, D, 3 * D + 16])
            d_w_out = self.din("d_w_out", [1, D, D])
            W["d_b_f"] = self.din("d_b_f", [1, 16])
            W["cache_d_k"] = self.din("cache_d_k", [PAST, D])
            W["cache_d_v"] = self.din("cache_d_v", [PAST, D])
            W["cache_d_logf"] = self.din("cache_d_logf", [PAST, 16])
            wlist.append(("d_w_in_b", 0, _sub(d_w_in, 0), D, 3 * D + 16))
            wlist.append(("d_w_out_b", 0, _sub(d_w_out, 0), D, D))
            self.d_outs = {"p": (self.dout("new_d_k_prompt", [TP, D]), self.dout("new_d_v_prompt", [TP, D]),
                                 self.dout("new_d_logf_prompt", [TP, 16])),
                           "s": (self.dout("new_d_k_sample", [TS, D]), self.dout("new_d_v_sample", [TS, D]),
                                 self.dout("new_d_logf_sample", [TS, 16]))}

    def load_consts(self, c, es):
        C = {}
        self.consts = C
        C["tri"] = c.sb("c_tri", [128, 128], F32)
        c.dma("sp", C["tri"].t[:, :], self.cin["c_tri"].ap(), writes=(C["tri"],))
        C["ones_f"] = c.sb("c_ones_f", [128, 128], F32)
        c.op("pool", lambda e: e.memset(C["ones_f"].t[:, :], 1.0), writes=(C["ones_f"],))
        C["ones_b"] = c.sb("c_ones_b", [128, 128], BF16)
        c.op("pool", lambda e: e.memset(C["ones_b"].t[:, :], 1.0), writes=(C["ones_b"],))
        C["one"] = c.sb("c_one", [128, 1], F32)
        c.op("pool", lambda e: e.memset(C["one"].t[:, :], 1.0), writes=(C["one"],))

    def mixer_stage(self, c, S, key, li, x_in, xT_in, x_out, xT_out, W, ident, psb):
        kind, j = li % 4, li // 4
        if kind == 0:
            self.a_stage(c, S, li, j, x_in, xT_in, x_out, xT_out, W, ident, psb,
                         v_out=self.av_out if key == "s" else None)
        elif kind == 1:
            self.b_stage(c, S, key, li, j, x_in, xT_in, x_out, xT_out, W, ident, psb)
        elif kind == 2:
            self.c_stage(c, S, key, li, j, x_in, xT_in, x_out, xT_out, W, ident, psb)
        elif kind == 3:
            self.d_stage(c, S, key, li, j, x_in, xT_in, x_out, xT_out, W, ident, psb)
        else:
            raise NotImplementedError

    def make_xT(self, c, S, x_tm, xT_out, ident, psb):
        P, NS, TT = S.P, S.NS, S.TT
        with ExitStack() as es:
            xin = [c.sb("m_x%d" % i, [P, D], F32, es) for i in range(2)]
            xTo = [c.sb("m_xT%d" % i, [128, KC, P], BF16, es) for i in range(2)]
            n = 0
            pcnt = 0
            for r0 in range(0, S.T, P):
                xi = xin[n % 2]
                xo = xTo[n % 2]
                n += 1
                c.dma("sp", xi.t[:, :], x_tm.ap()[r0:r0 + P, :], writes=(xi,))
                for g4 in range(4):
                    pb = psb[pcnt % 8]
                    pcnt += 1
                    for j in range(4):
                        kc = g4 * 4 + j
                        c.op("pe", lambda e, kc=kc, j=j, pb=pb, xi=xi: e.transpose(
                            out=pb.t[:, j * P:(j + 1) * P], in_=xi.t[:P, kc * 128:(kc + 1) * 128],
                            identity=ident.t[:P, :P]), reads=(xi, ident), writes=(pb,))
                    if g4 % 2 == 0:
                        c.op("act", lambda e, g4=g4, pb=pb, xo=xo: e.copy(
                            out=xo.t[:, g4 * 4:(g4 + 1) * 4, :P], in_=pb.t[:, :4 * P].rearrange("p (a b) -> p a b", b=P)),
                            reads=(pb,), writes=(xo,))
                    else:
                        c.op("dve", lambda e, g4=g4, pb=pb, xo=xo: e.tensor_copy(
                            out=xo.t[:, g4 * 4:(g4 + 1) * 4, :P], in_=pb.t[:, :4 * P].rearrange("p (a b) -> p a b", b=P)),
                            reads=(pb,), writes=(xo,))
                c.dma("sp", xT_out.ap().rearrange("(kc p) t -> p kc t", p=128)[:, :, r0:r0 + P], xo.t[:, :, :P],
                      reads=(xo,))
            c.barrier()


class _Sub:
    def __init__(self, t, li):
        self.t = t
        self.li = li

    def ap(self):
        return self.t.ap()[self.li]


def _sub(t, li):
    return _Sub(t, li)


def ret_tables(T, L, pos0):
    f = np.float32
    half = 128
    inv = (f(10000.0) ** (-np.arange(half, dtype=f) / f(half))).astype(f)
    pos = (pos0 + np.arange(T)).astype(f)
    ang = (pos[:, None] * inv[None, :]).astype(f)
    cos = np.cos(ang).astype(f)
    sin = np.sin(ang).astype(f)
    lg = np.log1p(-(f(2.0) ** (f(-5.0) - np.arange(8, dtype=f)))).astype(f)
    idx = np.arange(L, dtype=f)
    causal = idx[:, None] >= idx[None, :]
    dintra = np.where(causal[:, :, None], np.exp((idx[:, None] - idx[None, :])[:, :, None] * lg), 0.0).astype(f)
    dq = np.exp((idx + f(1.0))[:, None] * lg).astype(f)
    dk = np.exp((f(L) - f(1.0) - idx)[:, None] * lg).astype(f)
    dsv = np.exp(f(L) * lg).astype(f)
    P_ = min(128, T)
    return {"cosT": np.ascontiguousarray(cos.T), "sinT": np.ascontiguousarray(sin.T), "cos": cos, "sin": sin,
            "dk": np.ascontiguousarray(np.tile(dk, (P_ // L, 1))),
            "dint": np.ascontiguousarray(np.transpose(dintra, (1, 2, 0))),
            "dq": np.ascontiguousarray(np.broadcast_to(np.transpose(dq, (1, 0))[None], (128, 8, L))),
            "decay_s": dsv}


def const_inputs_c(TP, TS):
    out = {}
    for key, T_, L_, p0 in (("p", TP, 64, 0), ("s", TS, 32, PAST)):
        t = ret_tables(T_, L_, p0)
        for nm in ("cosT", "sinT", "cos", "sin", "dk", "dint", "dq"):
            out["c_%s_%s" % (nm, key)] = t[nm]
    return out


def const_inputs():
    k = np.arange(128)[:, None, None]
    jj = np.arange(4)[None, :, None]
    q = np.arange(512)[None, None, :]
    negm = np.where(jj * 128 + k <= q, 0.0, -30000.0).astype(np.float32)
    tri = (np.arange(128)[:, None] <= np.arange(128)[None, :]).astype(np.float32)
    return {"c_ident": np.eye(128, dtype=np.float32), "c_tri": tri, "c_negm": negm}


_CACHE = {}


def kernel(**inputs):
    TP, TS, NCORE = 8192, 32, 8
    inp = {k: np.asarray(v) for k, v in inputs.items()}
    if "nc" not in _CACHE:
        prog = Prog({"TP": TP, "TS": TS, "layers": [0, 1, 2, 3], "ffn": True, "mixers": True})
        _CACHE["nc"] = prog.build()
    nc = _CACHE["nc"]
    consts = {**const_inputs(), **const_inputs_c(TP, TS)}
    shared = {}
    for nm in ("a_w_in", "a_b_in", "a_vn_g", "a_vn_b", "a_w_s", "a_b_s", "a_w_out", "b_w_in", "b_b_gates", "b_norm_g",
               "b_w_out", "c_w_in", "c_gn_g", "c_gn_b", "c_w_out", "d_w_in", "d_b_f", "d_w_out", "ffn_w_in", "ffn_w_out",
               "ln1_g", "ln1_b", "ln2_g", "ln2_b"):
        shared[nm] = np.ascontiguousarray(inp[nm], dtype=np.float32)
    in_maps = []
    for core in range(NCORE):
        b = core // 2
        m = dict(consts)
        m.update(shared)
        m["x_prompt"] = np.ascontiguousarray(inp["x_prompt"][b])
        m["x_sample"] = np.ascontiguousarray(inp["x_sample"][core])
        m["state_b_C"] = np.ascontiguousarray(inp["state_b_C"][0, core])
        m["state_b_n"] = np.ascontiguousarray(inp["state_b_n"][0, core])
        m["state_b_m"] = np.ascontiguousarray(inp["state_b_m"][0, core])
        m["state_c_S"] = np.ascontiguousarray(inp["state_c_S"][0, core])
        m["cache_d_k"] = np.ascontiguousarray(inp["cache_d_k"][0, core]).reshape(PAST, D)
        m["cache_d_v"] = np.ascontiguousarray(inp["cache_d_v"][0, core]).reshape(PAST, D)
        m["cache_d_logf"] = np.ascontiguousarray(inp["cache_d_logf"][0, core])
        in_maps.append(m)
    res = run_bass_kernel_spmd(nc, in_maps, core_ids=list(range(NCORE)))
    R = res.results
    ev = [0, 2, 4, 6]

    def gp(name, shape):
        return np.stack([R[cidx][name].reshape(shape) for cidx in ev])

    def gs(name, shape):
        return np.stack([R[cidx][name].reshape(shape) for cidx in range(NCORE)])

    outs = (
        gp("y_prompt", (TP, D)),
        gs("y_sample", (TS, D)),
        gs("new_a_v_sample", (TS, D))[None],
        gp("new_b_C_prompt", (8, 256, 128))[None],
        gp("new_b_n_prompt", (8, 128))[None],
        gp("new_b_m_prompt", (8,))[None],
        gs("new_b_C_sample", (8, 256, 128))[None],
        gs("new_b_n_sample", (8, 128))[None],
        gs("new_b_m_sample", (8,))[None],
        gp("new_c_S_prompt", (8, 256, 512))[None],
        gs("new_c_S_sample", (8, 256, 512))[None],
        gp("new_d_k_prompt", (TP, 16, 128))[None],
        gp("new_d_v_prompt", (TP, 16, 128))[None],
        gp("new_d_logf_prompt", (TP, 16))[None],
        gs("new_d_k_sample", (TS, 16, 128))[None],
        gs("new_d_v_sample", (TS, 16, 128))[None],
        gs("new_d_logf_sample", (TS, 16))[None],
    )
    return tuple(np.ascontiguousarray(o, dtype=np.float32) for o in outs)
```

```python
import math
from contextlib import ExitStack

import numpy as np
import concourse.bass as bass
import concourse.mybir as mybir
from concourse.bass_utils import run_bass_kernel_spmd

F32 = mybir.dt.float32
BF16 = mybir.dt.bfloat16
AF = mybir.ActivationFunctionType
ALU = mybir.AluOpType

D = 2048
KC = D // 128
FFN_H = 5632
DEPTH = 4
ALPHA = (2.0 * DEPTH) ** 0.25
LN_EPS = 1e-5
PAST = 1024


class Buf:
    __slots__ = ("w", "r", "t")

    def __init__(self, t=None):
        self.w = None
        self.r = {}
        self.t = t


class PEProxy:
    def __init__(self, eng):
        self._e = eng
        self.last_stop = True

    def matmul(self, *a, **k):
        self.last_stop = bool(k.get("stop", True))
        return self._e.matmul(*a, **k)

    def transpose(self, *a, **k):
        self.last_stop = True
        return self._e.transpose(*a, **k)

    def wait_ge(self, *a, **k):
        return self._e.wait_ge(*a, **k)


class Ctx:
    def __init__(self, nc, es):
        self.nc = nc
        self.es = es
        self.E = {"pe": PEProxy(nc.tensor), "act": nc.scalar, "dve": nc.vector, "pool": nc.gpsimd, "sp": nc.sync}
        self.sems = {}
        self.cnt = {}
        for e in self.E:
            self.sems[e] = es.enter_context(nc.semaphore("s_" + e))
            self.cnt[e] = 0
        self.dq = {}
        for q, n in (("sp", 16), ("pool", 8), ("act", 4)):
            names = []
            for i in range(n):
                nm = "d_%s%d" % (q, i)
                self.sems[nm] = es.enter_context(nc.semaphore(nm))
                self.cnt[nm] = 0
                names.append(nm)
            self.dq[q] = [names, 0]
        self.known = {e: {} for e in self.E}
        self.n_ins = 0

    def sb(self, name, shape, dt, es=None):
        self.n_sb = getattr(self, "n_sb", 0) + 1
        t = (es or self.es).enter_context(self.nc.sbuf_tensor("%s_%d" % (name, self.n_sb), list(shape), dt))
        return Buf(t)

    def _wait(self, e, deps):
        kn = self.known[e]
        for s, v in deps.items():
            if s == "pe" and e == "pe":
                continue
            if kn.get(s, 0) >= v:
                continue
            self.E[e].wait_ge(self.sems[s], v)
            kn[s] = v
            self.n_ins += 1

    @staticmethod
    def _deps(reads, writes):
        deps = {}
        for b in reads:
            if b.w is not None and deps.get(b.w[0], 0) < b.w[1]:
                deps[b.w[0]] = b.w[1]
        for b in writes:
            if b.w is not None and deps.get(b.w[0], 0) < b.w[1]:
                deps[b.w[0]] = b.w[1]
            for s, v in b.r.items():
                if deps.get(s, 0) < v:
                    deps[s] = v
        return deps

    @staticmethod
    def _record(tok, reads, writes):
        s, v = tok
        for b in reads:
            if b.r.get(s, 0) < v:
                b.r[s] = v
        for b in writes:
            b.w = tok
            b.r = {}

    def op(self, e, fn, reads=(), writes=()):
        deps = self._deps(reads, writes)
        self._wait(e, deps)
        ins = fn(self.E[e])
        self.n_ins += 1
        if e == "pe" and not self.E["pe"].last_stop:
            self._record((e, self.cnt[e] + 1), reads, writes)
            return
        self.cnt[e] += 1
        ins.then_inc(self.sems[e], 1)
        self._record((e, self.cnt[e]), reads, writes)

    def dma(self, q, out, in_, reads=(), writes=()):
        names, idx = self.dq[q]
        nm = names[idx % len(names)]
        self.dq[q][1] = idx + 1
        deps = self._deps(reads, writes)
        if self.cnt[nm] > 0 and deps.get(nm, 0) < self.cnt[nm]:
            deps[nm] = self.cnt[nm]
        self._wait(q, deps)
        ins = self.E[q].dma_start(out=out, in_=in_)
        self.cnt[nm] += 16
        ins.then_inc(self.sems[nm], 16)
        self.n_ins += 1
        self._record((nm, self.cnt[nm]), reads, writes)

    def barrier(self):
        allc = {s: v for s, v in self.cnt.items() if v > 0}
        for e in self.E:
            self._wait(e, allc)

    def finish(self):
        allc = {s: v for s, v in self.cnt.items() if v > 0}
        self._wait("sp", allc)


def bcast_rows(ap_row, nparts):
    return ap_row.partition_broadcast(nparts)


class Seg:
    def __init__(self, name, T, TT):
        self.name = name
        self.T = T
        self.TT = TT
        self.P = min(128, TT)
        self.NS = TT // self.P
        self.ntiles = T // TT


class Prog:
    def __init__(self, cfg):
        self.cfg = cfg
        self.TP = cfg["TP"]
        self.TS = cfg.get("TS", 32)
        self.layers = cfg.get("layers", [0, 1, 2, 3])
        self.do_ffn = cfg.get("ffn", True)
        self.nc = bass.Bass("TRN2", target_bir_lowering=False)
        self.dram = {}

    def din(self, name, shape, dt=F32):
        t = self.nc.dram_tensor(name, list(shape), dt, kind="ExternalInput")
        self.dram[name] = t
        return t

    def dout(self, name, shape, dt=F32):
        t = self.nc.dram_tensor(name, list(shape), dt, kind="ExternalOutput")
        self.dram[name] = t
        return t

    def dscr(self, name, shape, dt):
        t = self.nc.dram_tensor(name, list(shape), dt)
        self.dram[name] = t
        return t

    def convert_weight(self, c, src, dst, K, N):
        rows = max(128, (1 << 20) // N // 128 * 128)
        b = Buf()
        r0 = 0
        while r0 < K:
            r1 = min(K, r0 + rows)
            c.dma("pool", dst.ap()[r0:r1, :], src.ap()[r0:r1, :], writes=(b,))
            r0 = r1
        return b

    def ln_epilogue(self, c, S, es, zb, z_ap, gbc, bbc, xn_bufs, out_tm_ap, out_T, t0, P, ident, psT, tcnt):
        st, mv, rs, nmr, xn, xTo = xn_bufs
        for q in range(4):
            c.op("dve", lambda e, q=q: e.bn_stats(out=st.t[:P, q, :], in_=z_ap[:, q * 512:(q + 1) * 512]),
                 reads=(zb,), writes=(st,))
        c.op("dve", lambda e: e.bn_aggr(out=mv.t[:P, :], in_=st.t[:P].rearrange("p a b -> p (a b)")),
             reads=(st,), writes=(mv,))
        c.op("act", lambda e: e.activation(out=rs.t[:P, :], in_=mv.t[:P, 1:2], func=AF.Sqrt, bias=self.eps_t.t[:P, :],
                                           scale=1.0), reads=(mv, self.eps_t), writes=(rs,))
        c.op("dve", lambda e: e.reciprocal(out=rs.t[:P, :], in_=rs.t[:P, :]), reads=(rs,), writes=(rs,))
        c.op("dve", lambda e: e.scalar_tensor_tensor(out=nmr.t[:P, :], in0=mv.t[:P, 0:1], scalar=-1.0,
                                                     in1=rs.t[:P, :], op0=ALU.mult, op1=ALU.mult),
             reads=(mv, rs), writes=(nmr,))
        c.op("act", lambda e: e.activation(out=xn.t[:P, :], in_=z_ap, func=AF.Identity, bias=nmr.t[:P, 0:1],
                                           scale=rs.t[:P, 0:1]), reads=(zb, rs, nmr), writes=(xn,))
        c.op("dve", lambda e: e.tensor_tensor(out=xn.t[:P, :], in0=xn.t[:P, :], in1=gbc.t[:P, :], op=ALU.mult),
             reads=(xn, gbc), writes=(xn,))
        c.op("pool", lambda e: e.tensor_tensor(out=xn.t[:P, :], in0=xn.t[:P, :], in1=bbc.t[:P, :], op=ALU.add),
             reads=(xn, bbc), writes=(xn,))
        c.dma("sp", out_tm_ap, xn.t[:P, :], reads=(xn,))
        if out_T is None:
            return
        for g4 in range(4):
            pb = psT[tcnt[0] % len(psT)]
            tcnt[0] += 1
            for j in range(4):
                kc = g4 * 4 + j
                c.op("pe", lambda e, kc=kc, j=j, pb=pb: e.transpose(out=pb.t[:, j * P:(j + 1) * P],
                                                                      in_=xn.t[:P, kc * 128:(kc + 1) * 128],
                                                                      identity=ident.t[:P, :P]),
                     reads=(xn, ident), writes=(pb,))
            eng = "act" if g4 % 2 == 0 else "dve"
            if eng == "act":
                c.op("act", lambda e, g4=g4, pb=pb: e.copy(out=xTo.t[:, g4 * 4:(g4 + 1) * 4, :P],
                                                            in_=pb.t[:, :4 * P].rearrange("p (a b) -> p a b", b=P)),
                     reads=(pb,), writes=(xTo,))
            else:
                c.op("dve", lambda e, g4=g4, pb=pb: e.tensor_copy(out=xTo.t[:, g4 * 4:(g4 + 1) * 4, :P],
                                                                   in_=pb.t[:, :4 * P].rearrange("p (a b) -> p a b", b=P)),
                     reads=(pb,), writes=(xTo,))
        c.dma("sp", out_T.ap().rearrange("(kc p) t -> p kc t", p=128)[:, :, t0:t0 + P], xTo.t[:, :, :P], reads=(xTo,))

    def out_proj_ln(self, c, S, es, hT, nkc, wout_b, wout_ready, xres, gbc, bbc, lnb, x_out, xT_out, tile, ident,
                    ps_o, psT, wo_slots, cnts):
        P, NS, TT = S.P, S.NS, S.TT
        t0 = tile * TT
        OG = 256
        for og in range(D // OG):
            wo = wo_slots[cnts["wo"] % 2]
            cnts["wo"] += 1
            c.dma("sp", wo.t[:, :nkc, :],
                  wout_b.ap().rearrange("(kc p) n -> p kc n", p=128)[:, :, og * OG:(og + 1) * OG],
                  reads=(wout_ready,), writes=(wo,))
            for s in range(NS):
                pb = ps_o[cnts["po"] % len(ps_o)]
                cnts["po"] += 1
                for k in range(nkc):
                    c.op("pe", lambda e, k=k, s=s, pb=pb, wo=wo: e.matmul(pb.t[:P, :OG], lhsT=hT.t[:, k, s * P:(s + 1) * P],
                                                                          rhs=wo.t[:, k, :], start=(k == 0),
                                                                          stop=(k == nkc - 1)),
                         reads=(hT, wo), writes=(pb,))
                c.op("dve", lambda e, s=s, og=og, pb=pb: e.scalar_tensor_tensor(
                    out=xres.t[:P, s, og * OG:(og + 1) * OG], in0=xres.t[:P, s, og * OG:(og + 1) * OG], scalar=ALPHA,
                    in1=pb.t[:P, :OG], op0=ALU.mult, op1=ALU.add), reads=(xres, pb), writes=(xres,))
        for s in range(NS):
            self.ln_epilogue(c, S, es, xres, xres.t[:P, s, :], gbc, bbc, lnb,
                             x_out.ap()[t0 + s * P:t0 + (s + 1) * P, :], xT_out, t0 + s * P, P, ident, psT,
                             cnts["tc"])

    def ffn_stage(self, c, S, li, x_in, xT_in, x_out, xT_out, W, ident, psb):
        nc = self.nc
        P, NS, TT = S.P, S.NS, S.TT
        with ExitStack() as es:
            xT = c.sb("f_xT", [128, KC, TT], BF16, es)
            hT = c.sb("f_hT", [128, FFN_H // 128, TT], BF16, es)
            xres = c.sb("f_xres", [P, NS, D], F32, es)
            wi = [c.sb("f_wi%d" % i, [128, 2, KC, 256], BF16, es) for i in range(2)]
            wo = [c.sb("f_wo%d" % i, [128, FFN_H // 128, 256], BF16, es) for i in range(2)]
            sg = [c.sb("f_sg%d" % i, [128, TT], F32, es) for i in range(2)]
            gbc = c.sb("f_g", [P, D], F32, es)
            bbc = c.sb("f_b", [P, D], F32, es)
            lnb = (c.sb("f_st", [P, 4, 6], F32, es), c.sb("f_mv", [P, 2], F32, es), c.sb("f_rs", [P, 1], F32, es),
                   c.sb("f_nmr", [P, 1], F32, es), c.sb("f_xn", [P, D], F32, es), c.sb("f_xTo", [128, KC, P], BF16, es))
            c.dma("sp", gbc.t[:, :], W["ln2_g"].ap()[li:li + 1, :].partition_broadcast(P), writes=(gbc,))
            c.dma("sp", bbc.t[:, :], W["ln2_b"].ap()[li:li + 1, :].partition_broadcast(P), writes=(bbc,))
            win_b, win_r = W["ffn_w_in_b"][li]
            wout_b, wout_r = W["ffn_w_out_b"][li]
            cnts = {"wo": 0, "po": 0, "tc": [0], "wi": 0, "pg": 0}
            ps_g = psb[0:2]
            ps_u = psb[2:4]
            ps_o = psb[4:6]
            psT = psb[6:8]
            NHB = FFN_H // 256
            for tile in range(S.ntiles):
                t0 = tile * TT
                c.dma("sp", xT.t[:, :, :], xT_in.ap().rearrange("(kc p) t -> p kc t", p=128)[:, :, t0:t0 + TT],
                      writes=(xT,))
                c.dma("sp", xres.t[:, :, :], x_in.ap()[t0:t0 + TT, :].rearrange("(s p) d -> p s d", p=P),
                      writes=(xres,))
                for j in range(NHB):
                    w = wi[cnts["wi"] % 2]
                    cnts["wi"] += 1
                    for gu in range(2):
                        c0 = gu * FFN_H + j * 256
                        c.dma("sp", w.t[:, gu, :, :],
                              win_b.ap().rearrange("(kc p) n -> p kc n", p=128)[:, :, c0:c0 + 256],
                              reads=(win_r,), writes=(w,))
                    for half in range(2):
                        hc = j * 2 + half
                        pg = ps_g[cnts["pg"] % 2]
                        pu = ps_u[cnts["pg"] % 2]
                        sgb = sg[cnts["pg"] % 2]
                        cnts["pg"] += 1
                        for gu, pb in ((0, pg), (1, pu)):
                            for k in range(KC):
                                c.op("pe", lambda e, k=k, gu=gu, pb=pb, w=w, half=half: e.matmul(
                                    pb.t[:, :TT], lhsT=w.t[:, gu, k, half * 128:(half + 1) * 128], rhs=xT.t[:, k, :],
                                    start=(k == 0), stop=(k == KC - 1)), reads=(w, xT), writes=(pb,))
                        c.op("act", lambda e, pg=pg, sgb=sgb: e.activation(out=sgb.t[:, :], in_=pg.t[:, :TT], func=AF.Silu),
                             reads=(pg,), writes=(sgb,))
                        c.op("dve", lambda e, pu=pu, sgb=sgb, hc=hc: e.tensor_tensor(out=hT.t[:, hc, :], in0=sgb.t[:, :],
                                                                                       in1=pu.t[:, :TT], op=ALU.mult),
                             reads=(pu, sgb), writes=(hT,))
                self.out_proj_ln(c, S, es, hT, FFN_H // 128, wout_b, wout_r, xres, gbc, bbc, lnb, x_out, xT_out, tile,
                                 ident, ps_o, psT, wo, cnts)
            c.barrier()


    def ln_bufs(self, c, es, P, pre):
        return (c.sb(pre + "_st", [P, 4, 6], F32, es), c.sb(pre + "_mv", [P, 2], F32, es), c.sb(pre + "_rs", [P, 1], F32, es),
                c.sb(pre + "_nmr", [P, 1], F32, es), c.sb(pre + "_xn", [P, D], F32, es), c.sb(pre + "_xTo", [128, KC, P], BF16, es))

    def load_bc(self, c, es, name, row_ap, P, n):
        b = c.sb(name, [P, n], F32, es)
        c.dma("sp", b.t[:, :], row_ap.partition_broadcast(P), writes=(b,))
        return b

    def row_stats(self, c, src, src_ap, P, n, st, mv, rs, nmr):
        nq = max(1, n // 512)
        w = n // nq
        for q in range(nq):
            c.op("dve", lambda e, q=q: e.bn_stats(out=st.t[:P, q, :], in_=src_ap[:, q * w:(q + 1) * w]),
                 reads=(src,), writes=(st,))
        c.op("dve", lambda e: e.bn_aggr(out=mv.t[:P, :], in_=st.t[:P, :nq, :].rearrange("p a b -> p (a b)")),
             reads=(st,), writes=(mv,))
        c.op("act", lambda e: e.activation(out=rs.t[:P, :], in_=mv.t[:P, 1:2], func=AF.Sqrt, bias=self.eps_t.t[:P, :],
                                           scale=1.0), reads=(mv, self.eps_t), writes=(rs,))
        c.op("dve", lambda e: e.reciprocal(out=rs.t[:P, :], in_=rs.t[:P, :]), reads=(rs,), writes=(rs,))
        c.op("dve", lambda e: e.scalar_tensor_tensor(out=nmr.t[:P, :], in0=mv.t[:P, 0:1], scalar=-1.0,
                                                     in1=rs.t[:P, :], op0=ALU.mult, op1=ALU.mult),
             reads=(mv, rs), writes=(nmr,))

    def a_stage(self, c, S, li, j, x_in, xT_in, x_out, xT_out, W, ident, psb, v_out=None):
        nc = self.nc
        P, NS, TT = S.P, S.NS, S.TT
        BL = P
        with ExitStack() as es:
            xT = c.sb("a_xT", [128, KC, TT], BF16, es)
            xres = c.sb("a_xres", [P, NS, D], F32, es)
            uT = c.sb("a_uT", [128, KC, TT], BF16, es)
            yT = c.sb("a_yT", [128, KC, TT], BF16, es)
            vln = c.sb("a_vln", [P, NS, D], BF16, es)
            ws_ = [c.sb("a_w%d" % i, [128, KC, 256], BF16, es) for i in range(2)]
            wo = [c.sb("a_wo%d" % i, [128, KC, 256], BF16, es) for i in range(2)]
            tt = [c.sb("a_t%d" % i, [128, TT], F32, es) for i in range(2)]
            lnb = self.ln_bufs(c, es, P, "a")
            st, mv, rs, nmr = lnb[0], lnb[1], lnb[2], lnb[3]
            gbc = self.load_bc(c, es, "a_g", W["ln1_g"].ap()[li:li + 1, :], P, D)
            bbc = self.load_bc(c, es, "a_b", W["ln1_b"].ap()[li:li + 1, :], P, D)
            vg = self.load_bc(c, es, "a_vg", W["a_vn_g"].ap()[j:j + 1, :], P, D)
            vb = self.load_bc(c, es, "a_vb", W["a_vn_b"].ap()[j:j + 1, :], P, D)
            binv = self.load_bc(c, es, "a_binv", W["a_b_in"].ap()[j:j + 1, D:2 * D], P, D)
            bs = self.load_bc(c, es, "a_bs", W["a_b_s"].ap()[j:j + 1].rearrange("a g p -> a (g p)"), 128, 8 * 128)
            binu = c.sb("a_binu", [128, KC], F32, es)
            with nc.allow_non_contiguous_dma(reason="tiny bias column load"):
                c.dma("sp", binu.t[:, :], W["a_b_in"].ap()[j, 0:D].rearrange("(kc p) -> p kc", p=128), writes=(binu,))
            wsf = c.sb("a_wsf", [128, 8, 128], F32, es)
            wsT = c.sb("a_wsT", [128, 8, 128], BF16, es)
            c.dma("sp", wsf.t[:, :, :], W["a_w_s"].ap()[j].rearrange("g p q -> p g q"), writes=(wsf,))
            for g in range(8):
                pb = psb[g % 8]
                c.op("pe", lambda e, g=g, pb=pb: e.transpose(out=pb.t[:, :128], in_=wsf.t[:, g, :], identity=ident.t[:, :]),
                     reads=(wsf, ident), writes=(pb,))
                c.op("dve", lambda e, g=g, pb=pb: e.tensor_copy(out=wsT.t[:, g, :], in_=pb.t[:, :128]), reads=(pb,),
                     writes=(wsT,))
            if BL > 64:
                c.op("pool", lambda e: e.memset(wsT.t[64:128, :, 0:64], 0.0), writes=(wsT,))
            win_b, win_r = W["a_w_in_b"][j]
            wout_b, wout_r = W["a_w_out_b"][j]
            cnts = {"wo": 0, "po": 0, "tc": [0], "w": 0, "pu": 0}
            ps_u = psb[0:2]
            ps_v = psb[2:4]
            ps_o = psb[4:6]
            psT = psb[6:8]
            for tile in range(S.ntiles):
                t0 = tile * TT
                c.dma("sp", xT.t[:, :, :], xT_in.ap().rearrange("(kc p) t -> p kc t", p=128)[:, :, t0:t0 + TT],
                      writes=(xT,))
                for jb in range(D // 256):
                    w = ws_[cnts["w"] % 2]
                    cnts["w"] += 1
                    c.dma("sp", w.t[:, :, :], win_b.ap().rearrange("(kc p) n -> p kc n", p=128)[:, :, jb * 256:(jb + 1) * 256],
                          reads=(win_r,), writes=(w,))
                    for half in range(2):
                        ec = jb * 2 + half
                        pb = ps_u[cnts["pu"] % 2]
                        cnts["pu"] += 1
                        for k in range(KC):
                            c.op("pe", lambda e, k=k, pb=pb, w=w, half=half: e.matmul(
                                pb.t[:, :TT], lhsT=w.t[:, k, half * 128:(half + 1) * 128], rhs=xT.t[:, k, :],
                                start=(k == 0), stop=(k == KC - 1)), reads=(w, xT), writes=(pb,))
                        c.op("act", lambda e, pb=pb, ec=ec: e.activation(out=uT.t[:, ec, :], in_=pb.t[:, :TT],
                                                                         func=AF.Gelu_apprx_tanh, bias=binu.t[:, ec:ec + 1],
                                                                         scale=1.0), reads=(pb, binu), writes=(uT,))
                for og in range(D // 256):
                    w = ws_[cnts["w"] % 2]
                    cnts["w"] += 1
                    c.dma("sp", w.t[:, :, :],
                          win_b.ap().rearrange("(kc p) n -> p kc n", p=128)[:, :, D + og * 256:D + (og + 1) * 256],
                          reads=(win_r,), writes=(w,))
                    for s_ in range(NS):
                        pb = ps_v[cnts["pu"] % 2]
                        cnts["pu"] += 1
                        for k in range(KC):
                            c.op("pe", lambda e, k=k, pb=pb, w=w, s_=s_: e.matmul(
                                pb.t[:P, :256], lhsT=xT.t[:, k, s_ * P:(s_ + 1) * P], rhs=w.t[:, k, :],
                                start=(k == 0), stop=(k == KC - 1)), reads=(w, xT), writes=(pb,))
                        c.op("dve", lambda e, pb=pb, s_=s_, og=og: e.tensor_tensor(
                            out=xres.t[:P, s_, og * 256:(og + 1) * 256], in0=pb.t[:P, :256],
                            in1=binv.t[:P, og * 256:(og + 1) * 256], op=ALU.add), reads=(pb, binv), writes=(xres,))
                for s_ in range(NS):
                    va = xres.t[:P, s_, :]
                    c.op("act", lambda e, va=va: e.activation(out=va, in_=va, func=AF.Gelu_apprx_tanh), reads=(xres,),
                         writes=(xres,))
                    self.row_stats(c, xres, va, P, D, st, mv, rs, nmr)
                    c.op("act", lambda e, va=va: e.activation(out=va, in_=va, func=AF.Identity, bias=nmr.t[:P, 0:1],
                                                               scale=rs.t[:P, 0:1]), reads=(xres, rs, nmr), writes=(xres,))
                    c.op("dve", lambda e, va=va: e.tensor_tensor(out=va, in0=va, in1=vg.t[:P, :], op=ALU.mult),
                         reads=(xres, vg), writes=(xres,))
                    if v_out is None:
                        c.op("pool", lambda e, va=va, s_=s_: e.tensor_tensor(out=vln.t[:P, s_, :], in0=va, in1=vb.t[:P, :],
                                                                              op=ALU.add), reads=(xres, vb), writes=(vln,))
                    else:
                        c.op("pool", lambda e, va=va: e.tensor_tensor(out=va, in0=va, in1=vb.t[:P, :], op=ALU.add),
                             reads=(xres, vb), writes=(xres,))
                        c.dma("sp", v_out.ap()[t0 + s_ * P:t0 + (s_ + 1) * P, :], va, reads=(xres,))
                        c.op("dve", lambda e, va=va, s_=s_: e.tensor_copy(out=vln.t[:P, s_, :], in_=va), reads=(xres,),
                             writes=(vln,))
                for ec in range(KC):
                    g = ec // 2
                    pb = ps_u[cnts["pu"] % 2]
                    tb = tt[cnts["pu"] % 2]
                    cnts["pu"] += 1
                    for s_ in range(NS):
                        c.op("pe", lambda e, pb=pb, s_=s_, ec=ec, g=g: e.matmul(
                            pb.t[:, s_ * BL:(s_ + 1) * BL], lhsT=vln.t[:BL, s_, ec * 128:(ec + 1) * 128],
                            rhs=wsT.t[:BL, g, :BL], start=True, stop=True), reads=(vln, wsT), writes=(pb,))
                    for s_ in range(NS):
                        c.op("dve", lambda e, pb=pb, tb=tb, s_=s_, g=g: e.tensor_tensor(
                            out=tb.t[:, s_ * BL:(s_ + 1) * BL], in0=pb.t[:, s_ * BL:(s_ + 1) * BL],
                            in1=bs.t[:, g * 128:g * 128 + BL], op=ALU.add), reads=(pb, bs), writes=(tb,))
                    c.op("pool", lambda e, tb=tb, ec=ec: e.tensor_tensor(out=yT.t[:, ec, :], in0=tb.t[:, :TT],
                                                                          in1=uT.t[:, ec, :], op=ALU.mult),
                         reads=(tb, uT), writes=(yT,))
                c.dma("sp", xres.t[:, :, :], x_in.ap()[t0:t0 + TT, :].rearrange("(s p) d -> p s d", p=P),
                      writes=(xres,))
                self.out_proj_ln(c, S, es, yT, KC, wout_b, wout_r, xres, gbc, bbc, lnb, x_out, xT_out, tile, ident,
                                 ps_o, psT, wo, cnts)
            c.barrier()


    def d_stage(self, c, S, key, li, j, x_in, xT_in, x_out, xT_out, W, ident, psb):
        nc = self.nc
        P, NS, TT, T = S.P, S.NS, S.TT, S.T
        NB = T // P
        smp = key == "s"
        qT_d = self.dscr("d_qT_" + key, [D, T], BF16)
        kT_d = self.dscr("d_kT_" + key, [D, T], BF16)
        v_d = self.dscr("d_v_" + key, [T, D], BF16)
        oT_d = self.dscr("d_oT_" + key, [D, T], BF16)
        crow_d = self.dscr("d_crow_" + key, [16, T], F32)
        k_out, v_out, f_out = self.d_outs[key]
        win_b, win_r = W["d_w_in_b"][j]
        wout_b, wout_r = W["d_w_out_b"][j]
        C = self.consts
        with ExitStack() as es0:
            negc = c.sb("d_negc", [P, NB, 16], F32, es0)
            C = dict(C)
            C["negm"] = c.sb("c_negm", [128, 4, 512], F32, es0)
            c.dma("sp", C["negm"].t[:, :, :], self.cin["c_negm"].ap(), writes=(C["negm"],))
            if smp:
                negcc = c.sb("d_negcc", [128, 8, 16], F32, es0)
            with ExitStack() as es:
                xT = c.sb("d_xT", [128, KC, TT], BF16, es)
                ws_ = [c.sb("d_w%d" % i, [128, KC, 256], BF16, es) for i in range(2)]
                stg = [c.sb("d_stg%d" % i, [128, TT], BF16, es) for i in range(2)]
                tmf = c.sb("d_tmf", [P, NS, D], F32, es)
                vb16 = c.sb("d_vb16", [P, NS, D], BF16, es)
                wf = c.sb("d_wf", [128, KC, 16], BF16, es)
                bfb = self.load_bc(c, es, "d_bf", W["d_b_f"].ap()[j:j + 1, :], 128, 16)
                carry = c.sb("d_carry", [128, 16], F32, es)
                ft = c.sb("d_ft", [128, 16], F32, es)
                lf = c.sb("d_lf", [128, 16], F32, es)
                cs = c.sb("d_cs", [128, 16], F32, es)
                crow = c.sb("d_crow", [16, TT], F32, es)
                c.dma("sp", wf.t[:, :, :], win_b.ap().rearrange("(kc p) n -> p kc n", p=128)[:, :, 3 * D:3 * D + 16],
                      reads=(win_r,), writes=(wf,))
                c.op("pool", lambda e: e.memset(carry.t[:, :], 0.0), writes=(carry,))
                pcnt = [0]

                def cumsum_block(lf_ap, lfb, n, negdst_ap, negdst, crow_ap=None):
                    pa = psb[6]
                    pt = psb[7]
                    c.op("pe", lambda e: e.matmul(pa.t[:n, :16], lhsT=C["tri"].t[:n, :n], rhs=lf_ap, start=True, stop=True),
                         reads=(C["tri"], lfb), writes=(pa,))
                    c.op("pe", lambda e: e.matmul(pt.t[:128, :16], lhsT=C["ones_f"].t[:n, :128], rhs=lf_ap, start=True,
                                                  stop=True), reads=(C["ones_f"], lfb), writes=(pt,))
                    c.op("dve", lambda e: e.tensor_tensor(out=cs.t[:n, :], in0=pa.t[:n, :16], in1=carry.t[:n, :], op=ALU.add),
                         reads=(pa, carry), writes=(cs,))
                    c.op("dve", lambda e: e.tensor_scalar(out=negdst_ap, in0=cs.t[:n, :], scalar1=-1.0, scalar2=None,
                                                          op0=ALU.mult), reads=(cs,), writes=(negdst,))
                    c.op("dve", lambda e: e.tensor_tensor(out=carry.t[:, :], in0=pt.t[:128, :16], in1=carry.t[:, :],
                                                          op=ALU.add), reads=(pt, carry), writes=(carry,))
                    if crow_ap is not None:
                        c.op("pe", lambda e: e.transpose(out=pa.t[:16, 32:32 + n], in_=cs.t[:n, :16], identity=ident.t[:n, :n]),
                             reads=(cs, ident), writes=(pa,))
                        c.op("act", lambda e: e.copy(out=crow_ap, in_=pa.t[:16, 32:32 + n]), reads=(pa,), writes=(crow,))

                if smp:
                    lfc = c.sb("d_lfc", [128, 8, 16], F32, es)
                    c.dma("sp", lfc.t[:, :, :], W["cache_d_logf"].ap().rearrange("(b p) h -> p b h", p=128), writes=(lfc,))
                    for b in range(8):
                        cumsum_block(lfc.t[:, b, :], lfc, 128, negcc.t[:, b, :], negcc)
                    for b in range(8):
                        c.op("dve", lambda e, b=b: e.tensor_tensor(out=negcc.t[:, b, :], in0=negcc.t[:, b, :],
                                                                   in1=carry.t[:, :], op=ALU.add), reads=(negcc, carry),
                             writes=(negcc,))
                    c.op("pool", lambda e: e.memset(carry.t[:, :], 0.0), writes=(carry,))
                cw = 0
                for tile in range(S.ntiles):
                    t0 = tile * TT
                    c.dma("sp", xT.t[:, :, :], xT_in.ap().rearrange("(kc p) t -> p kc t", p=128)[:, :, t0:t0 + TT],
                          writes=(xT,))
                    for part, dst in ((0, qT_d), (1, kT_d)):
                        for jb in range(D // 256):
                            w = ws_[cw % 2]
                            cw += 1
                            c.dma("sp", w.t[:, :, :], win_b.ap().rearrange("(kc p) n -> p kc n", p=128)[
                                :, :, part * D + jb * 256:part * D + (jb + 1) * 256], reads=(win_r,), writes=(w,))
                            for half in range(2):
                                hc = jb * 2 + half
                                pb = psb[pcnt[0] % 2]
                                sg_ = stg[pcnt[0] % 2]
                                pcnt[0] += 1
                                for k in range(KC):
                                    c.op("pe", lambda e, k=k, pb=pb, w=w, half=half: e.matmul(
                                        pb.t[:, :TT], lhsT=w.t[:, k, half * 128:(half + 1) * 128], rhs=xT.t[:, k, :],
                                        start=(k == 0), stop=(k == KC - 1)), reads=(w, xT), writes=(pb,))
                                sc = (128.0 ** -0.5) if part == 0 else 1.0
                                c.op("act", lambda e, pb=pb, sg_=sg_, sc=sc: e.activation(out=sg_.t[:, :], in_=pb.t[:, :TT],
                                                                                         func=AF.Copy, scale=sc),
                                     reads=(pb,), writes=(sg_,))
                                c.dma("sp", dst.ap()[hc * 128:(hc + 1) * 128, t0:t0 + TT], sg_.t[:, :], reads=(sg_,))
                    for part, out_d in ((1, k_out), (2, v_out)):
                        for og in range(D // 256):
                            w = ws_[cw % 2]
                            cw += 1
                            c.dma("sp", w.t[:, :, :], win_b.ap().rearrange("(kc p) n -> p kc n", p=128)[
                                :, :, part * D + og * 256:part * D + (og + 1) * 256], reads=(win_r,), writes=(w,))
                            for s_ in range(NS):
                                pb = psb[2 + pcnt[0] % 2]
                                pcnt[0] += 1
                                for k in range(KC):
                                    c.op("pe", lambda e, k=k, pb=pb, w=w, s_=s_: e.matmul(
                                        pb.t[:P, :256], lhsT=xT.t[:, k, s_ * P:(s_ + 1) * P], rhs=w.t[:, k, :],
                                        start=(k == 0), stop=(k == KC - 1)), reads=(w, xT), writes=(pb,))
                                if pcnt[0] % 2:
                                    c.op("act", lambda e, pb=pb, s_=s_, og=og: e.copy(
                                        out=tmf.t[:P, s_, og * 256:(og + 1) * 256], in_=pb.t[:P, :256]), reads=(pb,),
                                        writes=(tmf,))
                                else:
                                    c.op("dve", lambda e, pb=pb, s_=s_, og=og: e.tensor_copy(
                                        out=tmf.t[:P, s_, og * 256:(og + 1) * 256], in_=pb.t[:P, :256]), reads=(pb,),
                                        writes=(tmf,))
                        c.dma("sp", out_d.ap()[t0:t0 + TT, :].rearrange("(s p) d -> p s d", p=P), tmf.t[:, :, :],
                              reads=(tmf,))
                        if part == 2:
                            c.op("pool", lambda e: e.tensor_copy(out=vb16.t[:, :, :], in_=tmf.t[:, :, :]), reads=(tmf,),
                                 writes=(vb16,))
                            c.dma("sp", v_d.ap()[t0:t0 + TT, :].rearrange("(s p) d -> p s d", p=P), vb16.t[:, :, :],
                                  reads=(vb16,))
                    for s_ in range(NS):
                        pb = psb[4 + s_ % 2]
                        for k in range(KC):
                            c.op("pe", lambda e, k=k, pb=pb, s_=s_: e.matmul(
                                pb.t[:P, :16], lhsT=xT.t[:, k, s_ * P:(s_ + 1) * P], rhs=wf.t[:, k, :],
                                start=(k == 0), stop=(k == KC - 1)), reads=(wf, xT), writes=(pb,))
                        c.op("dve", lambda e, pb=pb: e.tensor_tensor(out=ft.t[:P, :], in0=pb.t[:P, :16], in1=bfb.t[:P, :],
                                                                     op=ALU.add), reads=(pb, bfb), writes=(ft,))
                        c.op("act", lambda e: e.activation(out=ft.t[:P, :], in_=ft.t[:P, :], func=AF.Exp, scale=-1.0),
                             reads=(ft,), writes=(ft,))
                        c.op("act", lambda e: e.activation(out=ft.t[:P, :], in_=ft.t[:P, :], func=AF.Ln,
                                                           bias=C["one"].t[:P, :], scale=1.0), reads=(ft, C["one"]),
                             writes=(ft,))
                        c.op("dve", lambda e: e.tensor_scalar(out=lf.t[:P, :], in0=ft.t[:P, :], scalar1=-1.0, scalar2=None,
                                                              op0=ALU.mult), reads=(ft,), writes=(lf,))
                        r0 = t0 + s_ * P
                        c.dma("sp", f_out.ap()[r0:r0 + P, :], lf.t[:P, :], reads=(lf,))
                        cumsum_block(lf.t[:P, :], lf, P, negc.t[:P, tile * NS + s_, :], negc,
                                     crow.t[:16, s_ * P:(s_ + 1) * P])
                    c.dma("sp", crow_d.ap()[:, t0:t0 + TT], crow.t[:16, :TT], reads=(crow,))
                c.barrier()
            with ExitStack() as es:
                TK = T + (PAST if smp else 0)
                LA = 2
                NBUF = 4
                hs_ = [dict(kT=c.sb("d_kTh%d" % i, [128, TK], BF16, es), qT=c.sb("d_qTh%d" % i, [128, T], BF16, es),
                            V=c.sb("d_Vh%d" % i, [P, NB, 128], BF16, es), oT=c.sb("d_oTh%d" % i, [128, T], BF16, es))
                       for i in range(2)]
                tb = [c.sb("d_t%d" % i, [128, TT], F32, es) for i in range(NBUF)]
                Eb = [c.sb("d_E%d" % i, [128, TT], BF16, es) for i in range(NBUF)]
                cq = [c.sb("d_cq%d" % i, [128, TT], F32, es) for i in range(3)]
                rden = c.sb("d_rden", [128, TT], F32, es)
                if smp:
                    ckf = c.sb("d_ckf", [128, 8, 128], F32, es)
                    V_c = [c.sb("d_Vc%d" % i, [128, 8, 128], BF16, es) for i in range(2)]

                def load_head(h):
                    B_ = hs_[h % 2]
                    hs = slice(h * 128, (h + 1) * 128)
                    c.dma("sp", B_["qT"].t[:, :], qT_d.ap()[hs, :], writes=(B_["qT"],))
                    c.dma("sp", B_["V"].t[:, :, :], v_d.ap()[:, hs].rearrange("(b p) d -> p b d", p=P), writes=(B_["V"],))
                    if smp:
                        c.dma("sp", B_["kT"].t[:, PAST:PAST + T], kT_d.ap()[hs, :], writes=(B_["kT"],))
                        c.dma("sp", ckf.t[:, :, :], W["cache_d_k"].ap()[:, hs].rearrange("(b p) d -> p b d", p=128),
                              writes=(ckf,))
                        c.dma("pool", V_c[h % 2].t[:, :, :],
                              W["cache_d_v"].ap()[:, hs].rearrange("(b p) d -> p b d", p=128), writes=(V_c[h % 2],))
                        for b in range(8):
                            pb = psb[b % 4]
                            c.op("pe", lambda e, b=b, pb=pb: e.transpose(out=pb.t[:, :128], in_=ckf.t[:, b, :],
                                                                         identity=ident.t[:, :]), reads=(ckf, ident),
                                 writes=(pb,))
                            c.op("act", lambda e, b=b, pb=pb: e.copy(out=B_["kT"].t[:, b * 128:(b + 1) * 128],
                                                                     in_=pb.t[:, :128]), reads=(pb,), writes=(B_["kT"],))
                    else:
                        c.dma("sp", B_["kT"].t[:, :], kT_d.ap()[hs, :], writes=(B_["kT"],))

                its = []
                qn = 0
                for h in range(16):
                    B_ = hs_[h % 2]
                    for qt in range(S.ntiles):
                        q0 = qt * TT
                        blocks = []
                        if smp:
                            for b in range(8):
                                blocks.append((B_["kT"].t[:, b * 128:(b + 1) * 128], V_c[h % 2].t[:, b, :],
                                               negcc.t[:, b, h:h + 1], 128, None, V_c[h % 2], negcc))
                            blocks.append((B_["kT"].t[:, PAST:PAST + T], B_["V"].t[:P, 0, :], negc.t[:P, 0, h:h + 1], P, 0,
                                           B_["V"], negc))
                        else:
                            for kb in range((q0 + TT) // 128):
                                jm = kb - q0 // 128
                                blocks.append((B_["kT"].t[:, kb * 128:(kb + 1) * 128], B_["V"].t[:, kb, :],
                                               negc.t[:, kb, h:h + 1], 128, jm if jm >= 0 else None, B_["V"], negc))
                        for bi, blk in enumerate(blocks):
                            its.append(dict(h=h, qt=qt, q0=q0, bi=bi, nb=len(blocks), blk=blk, qn=qn, B=B_))
                        qn += 1

                def front(n, it):
                    h, q0, bi, B_ = it["h"], it["q0"], it["bi"], it["B"]
                    k_ap, v_ap, nc_ap, bl, jm, vbuf, ncbuf = it["blk"]
                    cqb = cq[it["qn"] % 3]
                    if bi == 0:
                        c.dma("sp", cqb.t[:, :], crow_d.ap()[h:h + 1, q0:q0 + TT].partition_broadcast(128), writes=(cqb,))
                    pS = psb[n % NBUF]
                    t_ = tb[n % NBUF]
                    E_ = Eb[n % NBUF]
                    c.op("pe", lambda e: e.matmul(pS.t[:bl, :TT], lhsT=k_ap, rhs=B_["qT"].t[:, q0:q0 + TT], start=True,
                                                  stop=True), reads=(B_["kT"], B_["qT"]), writes=(pS,))
                    c.op("dve", lambda e: e.tensor_tensor(out=t_.t[:bl, :], in0=pS.t[:bl, :TT], in1=cqb.t[:bl, :], op=ALU.add),
                         reads=(pS, cqb), writes=(t_,))
                    if jm is not None:
                        c.op("pool", lambda e: e.tensor_tensor(out=t_.t[:bl, :], in0=t_.t[:bl, :], in1=C["negm"].t[:bl, jm, :TT],
                                                               op=ALU.add), reads=(t_, C["negm"]), writes=(t_,))
                    c.op("act", lambda e: e.activation(out=E_.t[:bl, :], in_=t_.t[:bl, :], func=AF.Exp, bias=nc_ap, scale=1.0),
                         reads=(t_, ncbuf), writes=(E_,))

                def back(n, it):
                    h, q0, bi, nb, B_ = it["h"], it["q0"], it["bi"], it["nb"], it["B"]
                    k_ap, v_ap, nc_ap, bl, jm, vbuf, ncbuf = it["blk"]
                    E_ = Eb[n % NBUF]
                    if bi == 0 and it["qt"] == 0 and h + 1 < 16:
                        load_head(h + 1)
                    po = psb[4 + it["qn"] % 2]
                    pd = psb[6 + it["qn"] % 2]
                    c.op("pe", lambda e: e.matmul(po.t[:, :TT], lhsT=v_ap, rhs=E_.t[:bl, :], start=(bi == 0),
                                                  stop=(bi == nb - 1)), reads=(vbuf, E_), writes=(po,))
                    c.op("pe", lambda e: e.matmul(pd.t[:, :TT], lhsT=C["ones_b"].t[:bl, :], rhs=E_.t[:bl, :], start=(bi == 0),
                                                  stop=(bi == nb - 1)), reads=(C["ones_b"], E_), writes=(pd,))
                    if bi == nb - 1:
                        c.op("dve", lambda e: e.reciprocal(out=rden.t[:, :], in_=pd.t[:, :TT]), reads=(pd,), writes=(rden,))
                        c.op("dve", lambda e: e.tensor_tensor(out=B_["oT"].t[:, q0:q0 + TT], in0=po.t[:, :TT], in1=rden.t[:, :],
                                                              op=ALU.mult), reads=(po, rden), writes=(B_["oT"],))
                        if it["qt"] == S.ntiles - 1:
                            c.dma("sp", oT_d.ap()[h * 128:(h + 1) * 128, :], B_["oT"].t[:, :], reads=(B_["oT"],))

                NI = len(its)
                load_head(0)
                for n in range(NI + LA):
                    if n < NI:
                        front(n, its[n])
                    if n >= LA:
                        back(n - LA, its[n - LA])
                c.barrier()
            with ExitStack() as es:
                oT = c.sb("d_oT", [128, KC, TT], BF16, es)
                xres = c.sb("d_xres", [P, NS, D], F32, es)
                wo = [c.sb("d_wo%d" % i, [128, KC, 256], BF16, es) for i in range(2)]
                lnb = self.ln_bufs(c, es, P, "d")
                gbc = self.load_bc(c, es, "d_g", W["ln1_g"].ap()[li:li + 1, :], P, D)
                bbc = self.load_bc(c, es, "d_b", W["ln1_b"].ap()[li:li + 1, :], P, D)
                cnts = {"wo": 0, "po": 0, "tc": [0]}
                for tile in range(S.ntiles):
                    t0 = tile * TT
                    c.dma("sp", oT.t[:, :, :], oT_d.ap().rearrange("(kc p) t -> p kc t", p=128)[:, :, t0:t0 + TT],
                          writes=(oT,))
                    c.dma("sp", xres.t[:, :, :], x_in.ap()[t0:t0 + TT, :].rearrange("(s p) d -> p s d", p=P),
                          writes=(xres,))
                    self.out_proj_ln(c, S, es, oT, KC, wout_b, wout_r, xres, gbc, bbc, lnb, x_out, xT_out, tile, ident,
                                     psb[4:6], psb[6:8], wo, cnts)
                c.barrier()


    def proj_fm(self, c, xT, TT, win_b, win_r, col0, nchunks, ws_, cw, pbs, pcnt, evac):
        for jb in range((nchunks + 1) // 2):
            ncol = min(256, (nchunks - jb * 2) * 128)
            w = ws_[cw[0] % len(ws_)]
            cw[0] += 1
            c.dma("sp", w.t[:, :, :ncol], win_b.ap().rearrange("(kc p) n -> p kc n", p=128)[
                :, :, col0 + jb * 256:col0 + jb * 256 + ncol], reads=(win_r,), writes=(w,))
            for half in range(ncol // 128):
                hc = jb * 2 + half
                pb = pbs[pcnt[0] % len(pbs)]
                pcnt[0] += 1
                for k in range(KC):
                    c.op("pe", lambda e, k=k, pb=pb, w=w, half=half: e.matmul(
                        pb.t[:, :TT], lhsT=w.t[:, k, half * 128:(half + 1) * 128], rhs=xT.t[:, k, :],
                        start=(k == 0), stop=(k == KC - 1)), reads=(w, xT), writes=(pb,))
                evac(hc, pb)

    def proj_tm(self, c, xT, S, win_b, win_r, col0, ncols, ws_, cw, pbs, pcnt, evac):
        P, NS = S.P, S.NS
        for og in range(ncols // 256):
            w = ws_[cw[0] % len(ws_)]
            cw[0] += 1
            c.dma("sp", w.t[:, :, :], win_b.ap().rearrange("(kc p) n -> p kc n", p=128)[
                :, :, col0 + og * 256:col0 + (og + 1) * 256], reads=(win_r,), writes=(w,))
            for s_ in range(NS):
                pb = pbs[pcnt[0] % len(pbs)]
                pcnt[0] += 1
                for k in range(KC):
                    c.op("pe", lambda e, k=k, pb=pb, w=w, s_=s_: e.matmul(
                        pb.t[:P, :256], lhsT=xT.t[:, k, s_ * P:(s_ + 1) * P], rhs=w.t[:, k, :],
                        start=(k == 0), stop=(k == KC - 1)), reads=(w, xT), writes=(pb,))
                evac(s_, og, pb)

    def b_stage(self, c, S, key, li, j, x_in, xT_in, x_out, xT_out, W, ident, psb):
        nc = self.nc
        P, NS, TT, T = S.P, S.NS, S.TT, S.T
        smp = key == "s"
        L = 32 if smp else 64
        NCH = TT // L
        H = 8
        qT_d = self.dscr("b_qT_" + key, [1024, T], BF16)
        kT_d = self.dscr("b_kT_" + key, [1024, T], BF16)
        k_d = self.dscr("b_k_" + key, [T, 1024], F32)
        v_d = self.dscr("b_v_" + key, [T, D], BF16)
        og_d = self.dscr("b_og_" + key, [T, D], F32)
        i_d = self.dscr("b_i_" + key, [8, T], F32)
        lf_d = self.dscr("b_lf_" + key, [8, T], F32)
        hT_d = self.dscr("b_hT_" + key, [D, T], BF16)
        C_out, n_out, m_out = self.b_outs[key]
        win_b, win_r = W["b_w_in_b"][j]
        wout_b, wout_r = W["b_w_out_b"][j]
        C = self.consts
        with ExitStack() as es:
            xT = c.sb("b_xT", [128, KC, TT], BF16, es)
            ws_ = [c.sb("b_w%d" % i, [128, KC, 256], BF16, es) for i in range(3)]
            stg = [c.sb("b_stg%d" % i, [128, TT], BF16, es) for i in range(2)]
            tmf = c.sb("b_tmf", [P, NS, D], F32, es)
            vb16 = c.sb("b_vb16", [P, NS, D], BF16, es)
            wg = c.sb("b_wg", [128, KC, 16], BF16, es)
            bg = c.sb("b_bg", [8, 2], F32, es)
            nbg = c.sb("b_nbg", [8, 2], F32, es)
            rows = [c.sb("b_rows%d" % i, [8, TT], F32, es) for i in range(2)]
            c.dma("sp", wg.t[:, :, :], win_b.ap().rearrange("(kc p) n -> p kc n", p=128)[:, :, 6144:6160],
                  reads=(win_r,), writes=(wg,))
            with nc.allow_non_contiguous_dma(reason="tiny gate bias load"):
                c.dma("sp", bg.t[:, :], W["b_b_gates"].ap()[j, :].rearrange("(a h) -> h a", a=2), writes=(bg,))
            c.op("dve", lambda e: e.tensor_scalar(out=nbg.t[:, :], in0=bg.t[:, :], scalar1=-1.0, scalar2=None, op0=ALU.mult),
                 reads=(bg,), writes=(nbg,))
            cw = [0]
            pcnt = [0]
            for tile in range(S.ntiles):
                t0 = tile * TT
                c.dma("sp", xT.t[:, :, :], xT_in.ap().rearrange("(kc p) t -> p kc t", p=128)[:, :, t0:t0 + TT],
                      writes=(xT,))
                for part, dst, sc in ((0, qT_d, 128.0 ** -0.5), (1, kT_d, 1.0)):
                    def ev(hc, pb, dst=dst, sc=sc):
                        sg_ = stg[pcnt[0] % 2]
                        c.op("act", lambda e: e.activation(out=sg_.t[:, :], in_=pb.t[:, :TT], func=AF.Copy, scale=sc),
                             reads=(pb,), writes=(sg_,))
                        c.dma("sp", dst.ap()[hc * 128:(hc + 1) * 128, t0:t0 + TT], sg_.t[:, :], reads=(sg_,))
                    self.proj_fm(c, xT, TT, win_b, win_r, part * 1024, 8, ws_, cw, psb[0:2] + psb[6:8], pcnt, ev)
                def evk(s_, og, pb):
                    c.op("dve", lambda e: e.tensor_copy(out=tmf.t[:P, s_, og * 256:(og + 1) * 256], in_=pb.t[:P, :256]),
                         reads=(pb,), writes=(tmf,))
                self.proj_tm(c, xT, S, win_b, win_r, 1024, 1024, ws_, cw, psb[2:4], pcnt, evk)
                c.dma("sp", k_d.ap()[t0:t0 + TT, :].rearrange("(s p) d -> p s d", p=P), tmf.t[:, :, :1024], reads=(tmf,))
                def evv(s_, og, pb):
                    c.op("act", lambda e: e.copy(out=vb16.t[:P, s_, og * 256:(og + 1) * 256], in_=pb.t[:P, :256]),
                         reads=(pb,), writes=(vb16,))
                self.proj_tm(c, xT, S, win_b, win_r, 2048, D, ws_, cw, psb[2:4], pcnt, evv)
                c.dma("sp", v_d.ap()[t0:t0 + TT, :].rearrange("(s p) d -> p s d", p=P), vb16.t[:, :, :], reads=(vb16,))
                def evo(s_, og, pb):
                    c.op("act", lambda e: e.activation(out=tmf.t[:P, s_, og * 256:(og + 1) * 256], in_=pb.t[:P, :256],
                                                       func=AF.Sigmoid), reads=(pb,), writes=(tmf,))
                self.proj_tm(c, xT, S, win_b, win_r, 4096, D, ws_, cw, psb[2:4], pcnt, evo)
                c.dma("sp", og_d.ap()[t0:t0 + TT, :].rearrange("(s p) d -> p s d", p=P), tmf.t[:, :, :], reads=(tmf,))
                for gi in range(2):
                    pb = psb[4 + gi]
                    r_ = rows[gi]
                    for k in range(KC):
                        c.op("pe", lambda e, k=k, pb=pb, gi=gi: e.matmul(pb.t[:8, :TT], lhsT=wg.t[:, k, gi * 8:(gi + 1) * 8],
                                                                       rhs=xT.t[:, k, :], start=(k == 0), stop=(k == KC - 1)),
                             reads=(wg, xT), writes=(pb,))
                    if gi == 0:
                        c.op("act", lambda e, pb=pb, r_=r_: e.activation(out=r_.t[:, :], in_=pb.t[:8, :TT], func=AF.Identity,
                                                                         bias=bg.t[:, 0:1], scale=1.0), reads=(pb, bg),
                             writes=(r_,))
                        c.dma("sp", i_d.ap()[:, t0:t0 + TT], r_.t[:, :], reads=(r_,))
                    else:
                        c.op("act", lambda e, pb=pb, r_=r_: e.activation(out=r_.t[:, :], in_=pb.t[:8, :TT], func=AF.Exp,
                                                                         bias=nbg.t[:, 1:2], scale=-1.0), reads=(pb, nbg),
                             writes=(r_,))
                        c.op("act", lambda e, r_=r_: e.activation(out=r_.t[:, :], in_=r_.t[:, :], func=AF.Ln,
                                                                  bias=C["one"].t[:8, :], scale=1.0), reads=(r_, C["one"]),
                             writes=(r_,))
                        c.op("dve", lambda e, r_=r_: e.tensor_scalar(out=r_.t[:, :], in0=r_.t[:, :], scalar1=-1.0,
                                                                     scalar2=None, op0=ALU.mult), reads=(r_,), writes=(r_,))
                        c.dma("sp", lf_d.ap()[:, t0:t0 + TT], r_.t[:, :], reads=(r_,))
            c.barrier()
        with ExitStack() as es:
            qT_t = c.sb("b_qTt", [128, H, TT], BF16, es)
            kT_t = c.sb("b_kTt", [128, H, TT], BF16, es)
            ktm = [c.sb("b_ktm%d" % i, [L, 1024], F32, es) for i in range(2)]
            v_c = [c.sb("b_vc%d" % i, [L, D], BF16, es) for i in range(2)]
            og_c = [c.sb("b_ogc%d" % i, [L, D], F32, es) for i in range(2)]
            CT = c.sb("b_CT", [128, H, 256], F32, es)
            CTb = c.sb("b_CTb", [128, H, 256], BF16, es)
            nT = c.sb("b_nT", [128, H], F32, es)
            nTb = c.sb("b_nTb", [128, H], BF16, es)
            Sp = c.sb("b_Sp", [L, H, L], BF16, es)
            mbeta = c.sb("b_mbeta", [L, H, L], F32, es)
            kp = c.sb("b_kp", [L, H, 128], BF16, es)
            hb = c.sb("b_hb", [L, D], F32, es)
            hsq = c.sb("b_hsq", [L, D], F32, es)
            hT_t = c.sb("b_hTt", [128, KC, TT], BF16, es)
            ngbc = self.load_bc(c, es, "b_ng", W["b_norm_g"].ap()[j:j + 1, :], L, D)
            Fb = c.sb("b_F", [8, 1 + TT], F32, es)
            mb = c.sb("b_m", [8, 1 + TT], F32, es)
            ir = c.sb("b_ir", [8, TT], F32, es)
            lfr = c.sb("b_lfr", [8, TT], F32, es)
            zr = c.sb("b_zr", [8, TT], F32, es)
            bcum = c.sb("b_bcum", [8, L], F32, es)
            Mr = c.sb("b_Mr", [8, L], F32, es)
            rp = c.sb("b_rp", [8, 3, L], F32, es)
            sm = c.sb("b_sm", [8, 4], F32, es)
            dg = c.sb("b_dg", [8, 8], F32, es)
            colp = c.sb("b_colp", [L, 24], F32, es)
            gbcst = c.sb("b_gb", [128, 8], F32, es)
            dn = c.sb("b_dn", [L, 8], F32, es)
            sc_ = c.sb("b_sc", [L, 8], F32, es)
            ss = c.sb("b_ss", [L, 8], F32, es)
            c.op("pool", lambda e: e.memset(zr.t[:, :], 0.0), writes=(zr,))
            c.op("pool", lambda e: e.memset(Fb.t[:, 0:1], 0.0), writes=(Fb,))
            if smp:
                cin = c.sb("b_cin", [128, H, 2, 128], F32, es)
                c.dma("sp", cin.t[:, :, :, :], W["state_b_C"].ap().rearrange("h (eh el) d -> el h eh d", el=128), writes=(cin,))
                for h in range(H):
                    for eh in range(2):
                        pb = psb[(h * 2 + eh) % 2]
                        c.op("pe", lambda e, h=h, eh=eh, pb=pb: e.transpose(out=pb.t[:, :128], in_=cin.t[:, h, eh, :],
                                                                             identity=ident.t[:, :]), reads=(cin, ident),
                             writes=(pb,))
                        c.op("dve", lambda e, h=h, eh=eh, pb=pb: e.tensor_copy(out=CT.t[:, h, eh * 128:(eh + 1) * 128],
                                                                                in_=pb.t[:, :128]), reads=(pb,), writes=(CT,))
                nin = c.sb("b_nin", [8, 128], F32, es)
                c.dma("sp", nin.t[:, :], W["state_b_n"].ap(), writes=(nin,))
                c.op("pe", lambda e: e.transpose(out=psb[6].t[:, :8], in_=nin.t[:, :], identity=ident.t[:8, :8]),
                     reads=(nin, ident), writes=(psb[6],))
                c.op("dve", lambda e: e.tensor_copy(out=nT.t[:, :], in_=psb[6].t[:, :8]), reads=(psb[6],), writes=(nT,))
                c.dma("sp", mb.t[:, 0:1], W["state_b_m"].ap().rearrange("(h o) -> h o", o=1), writes=(mb,))
            else:
                c.op("pool", lambda e: e.memset(CT.t[:, :, :], 0.0), writes=(CT,))
                c.op("pool", lambda e: e.memset(nT.t[:, :], 0.0), writes=(nT,))
                c.op("pool", lambda e: e.memset(mb.t[:, 0:1], 0.0), writes=(mb,))
            cc = 0
            for tile in range(S.ntiles):
                t0 = tile * TT
                c.dma("sp", qT_t.t[:, :, :], qT_d.ap().rearrange("(h p) t -> p h t", p=128)[:, :, t0:t0 + TT], writes=(qT_t,))
                c.dma("sp", kT_t.t[:, :, :], kT_d.ap().rearrange("(h p) t -> p h t", p=128)[:, :, t0:t0 + TT], writes=(kT_t,))
                c.dma("sp", ir.t[:, :], i_d.ap()[:, t0:t0 + TT], writes=(ir,))
                c.dma("sp", lfr.t[:, :], lf_d.ap()[:, t0:t0 + TT], writes=(lfr,))
                c.op("dve", lambda e: e.tensor_tensor_scan(out=Fb.t[:, 1:1 + TT], data0=lfr.t[:, :], data1=zr.t[:, :],
                                                           initial=Fb.t[:, 0:1], op0=ALU.add, op1=ALU.add),
                     reads=(lfr, zr, Fb), writes=(Fb,))
                c.op("dve", lambda e: e.tensor_tensor_scan(out=mb.t[:, 1:1 + TT], data0=lfr.t[:, :], data1=ir.t[:, :],
                                                           initial=mb.t[:, 0:1], op0=ALU.add, op1=ALU.max),
                     reads=(lfr, ir, mb), writes=(mb,))
                for ch in range(NCH):
                    c0 = ch * L
                    r0 = t0 + c0
                    kt = ktm[cc % 2]
                    vc = v_c[cc % 2]
                    oc = og_c[cc % 2]
                    cc += 1
                    c.dma("sp", kt.t[:, :], k_d.ap()[r0:r0 + L, :], writes=(kt,))
                    c.dma("sp", vc.t[:, :], v_d.ap()[r0:r0 + L, :], writes=(vc,))
                    c.dma("sp", oc.t[:, :], og_d.ap()[r0:r0 + L, :], writes=(oc,))
                    Fc = Fb.t[:, 1 + c0:1 + c0 + L]
                    mc = mb.t[:, 1 + c0:1 + c0 + L]
                    c.op("dve", lambda e, Fc=Fc, c0=c0: e.tensor_scalar(out=bcum.t[:, :], in0=Fc, scalar1=Fb.t[:, c0:c0 + 1],
                                                                       scalar2=None, op0=ALU.subtract), reads=(Fb,),
                         writes=(bcum,))
                    c.op("dve", lambda e, mc=mc: e.tensor_tensor(out=Mr.t[:, :], in0=mc, in1=bcum.t[:, :], op=ALU.subtract),
                         reads=(mb, bcum), writes=(Mr,))
                    c.op("dve", lambda e: e.tensor_scalar(out=sm.t[:, 0:1], in0=Mr.t[:, L - 1:L], scalar1=-1.0, scalar2=None,
                                                          op0=ALU.mult), reads=(Mr,), writes=(sm,))
                    c.op("dve", lambda e, c0=c0: e.tensor_tensor(out=sm.t[:, 1:2], in0=mb.t[:, c0:c0 + 1], in1=Mr.t[:, L - 1:L],
                                                                 op=ALU.subtract), reads=(mb, Mr, sm), writes=(sm,))
                    c.op("act", lambda e: e.activation(out=sm.t[:, 2:3], in_=sm.t[:, 1:2], func=AF.Exp), reads=(sm,),
                         writes=(sm,))
                    c.op("dve", lambda e, c0=c0: e.tensor_tensor(out=rp.t[:, 0, :], in0=ir.t[:, c0:c0 + L], in1=bcum.t[:, :],
                                                                 op=ALU.subtract), reads=(ir, bcum), writes=(rp,))
                    c.op("act", lambda e: e.activation(out=rp.t[:, 0, :], in_=rp.t[:, 0, :], func=AF.Exp, bias=sm.t[:, 0:1],
                                                       scale=1.0), reads=(rp, sm), writes=(rp,))
                    c.op("dve", lambda e: e.tensor_scalar(out=rp.t[:, 1, :], in0=Mr.t[:, :], scalar1=sm.t[:, 0:1], scalar2=None,
                                                          op0=ALU.add), reads=(Mr, sm), writes=(rp,))
                    c.op("act", lambda e: e.activation(out=rp.t[:, 1, :], in_=rp.t[:, 1, :], func=AF.Exp, scale=-1.0),
                         reads=(rp,), writes=(rp,))
                    c.op("act", lambda e, mc=mc: e.activation(out=rp.t[:, 2, :], in_=mc, func=AF.Exp, scale=-1.0),
                         reads=(mb,), writes=(rp,))
                    pm = psb[6]
                    for q3 in range(3):
                        c.op("pe", lambda e, q3=q3: e.transpose(out=pm.t[:L, 32 + q3 * 8:32 + (q3 + 1) * 8], in_=rp.t[:, q3, :],
                                                                identity=ident.t[:8, :8]), reads=(rp, ident), writes=(pm,))
                    c.op("dve", lambda e: e.tensor_copy(out=colp.t[:, :], in_=pm.t[:L, 32:56]), reads=(pm,), writes=(colp,))
                    c.op("dve", lambda e: e.tensor_scalar(out=dg.t[:, :], in0=ident.t[:8, :8], scalar1=sm.t[:, 2:3], scalar2=None,
                                                          op0=ALU.mult), reads=(ident, sm), writes=(dg,))
                    c.op("pe", lambda e: e.matmul(pm.t[:, 64:72], lhsT=C["ones_f"].t[:8, :128], rhs=dg.t[:, :], start=True,
                                                  stop=True), reads=(C["ones_f"], dg), writes=(pm,))
                    c.op("dve", lambda e: e.tensor_copy(out=gbcst.t[:, :], in_=pm.t[:, 64:72]), reads=(pm,), writes=(gbcst,))
                    beta = colp.t[:, 0:8]
                    c.op("pool", lambda e: e.tensor_tensor(
                        out=mbeta.t[:, :, :], in0=C["tri"].t[:L, :L].unsqueeze(1).broadcast_to([L, H, L]),
                        in1=beta.unsqueeze(2).broadcast_to([L, H, L]), op=ALU.mult), reads=(C["tri"], colp), writes=(mbeta,))
                    c.op("pool", lambda e, kt=kt: e.tensor_tensor(
                        out=kp.t[:, :, :], in0=kt.t[:, :].rearrange("p (h d) -> p h d", h=H),
                        in1=beta.unsqueeze(2).broadcast_to([L, H, 128]), op=ALU.mult), reads=(kt, colp), writes=(kp,))
                    c.op("dve", lambda e: e.tensor_tensor(out=CT.t[:, :, :], in0=CT.t[:, :, :],
                                                          in1=gbcst.t[:, :].unsqueeze(2).broadcast_to([128, H, 256]),
                                                          op=ALU.mult), reads=(CT, gbcst), writes=(CT,))
                    c.op("act", lambda e: e.copy(out=CTb.t[:, :, :], in_=CT.t[:, :, :]), reads=(CT,), writes=(CTb,))
                    c.op("dve", lambda e: e.tensor_tensor(out=nT.t[:, :], in0=nT.t[:, :], in1=gbcst.t[:, :], op=ALU.mult),
                         reads=(nT, gbcst), writes=(nT,))
                    c.op("dve", lambda e: e.tensor_copy(out=nTb.t[:, :], in_=nT.t[:, :]), reads=(nT,), writes=(nTb,))
                    pD = psb[7]
                    for hh in range(2):
                        pS = psb[hh]
                        for hl in range(4):
                            h = hh * 4 + hl
                            c.op("pe", lambda e, h=h, hl=hl, pS=pS, c0=c0: e.matmul(
                                pS.t[:L, hl * L:(hl + 1) * L], lhsT=kT_t.t[:, h, c0:c0 + L], rhs=qT_t.t[:, h, c0:c0 + L],
                                start=True, stop=True), reads=(kT_t, qT_t), writes=(pS,))
                        c.op("dve", lambda e, hh=hh, pS=pS: e.tensor_tensor(
                            out=Sp.t[:, hh * 4:(hh + 1) * 4, :], in0=pS.t[:L, :4 * L].rearrange("p (h t) -> p h t", h=4),
                            in1=mbeta.t[:, hh * 4:(hh + 1) * 4, :], op=ALU.mult), reads=(pS, mbeta), writes=(Sp,))
                        for hl in range(4):
                            h = hh * 4 + hl
                            pI = psb[2 + hl // 2]
                            co = (hl % 2) * 256
                            c.op("pe", lambda e, h=h, pI=pI, co=co, vc=vc: e.matmul(
                                pI.t[:L, co:co + 256], lhsT=Sp.t[:, h, :], rhs=vc.t[:, h * 256:(h + 1) * 256], start=True,
                                stop=False), reads=(Sp, vc), writes=(pI,))
                            c.op("pe", lambda e, h=h, pI=pI, co=co, c0=c0: e.matmul(
                                pI.t[:L, co:co + 256], lhsT=qT_t.t[:, h, c0:c0 + L], rhs=CTb.t[:, h, :], start=False,
                                stop=True), reads=(qT_t, CTb), writes=(pI,))
                            c.op("pe", lambda e, h=h: e.matmul(pD.t[:L, h:h + 1], lhsT=Sp.t[:, h, :], rhs=C["ones_b"].t[:L, 0:1],
                                                              start=True, stop=False), reads=(Sp, C["ones_b"]), writes=(pD,))
                            c.op("pe", lambda e, h=h, c0=c0: e.matmul(pD.t[:L, h:h + 1], lhsT=qT_t.t[:, h, c0:c0 + L],
                                                                     rhs=nTb.t[:, h:h + 1], start=False, stop=True),
                                 reads=(qT_t, nTb), writes=(pD,))
                        hs4 = slice(hh * 4, (hh + 1) * 4)
                        A_ = colp.t[:, 8 + hh * 4:8 + (hh + 1) * 4]
                        fl = colp.t[:, 16 + hh * 4:16 + (hh + 1) * 4]
                        c.op("dve", lambda e, hs4=hs4, A_=A_: e.tensor_tensor(out=dn.t[:, hs4], in0=pD.t[:L, hs4], in1=A_,
                                                                             op=ALU.mult), reads=(pD, colp), writes=(dn,))
                        c.op("act", lambda e, hs4=hs4: e.activation(out=dn.t[:, hs4], in_=dn.t[:, hs4], func=AF.Abs),
                             reads=(dn,), writes=(dn,))
                        c.op("dve", lambda e, hs4=hs4, fl=fl: e.tensor_tensor(out=dn.t[:, hs4], in0=dn.t[:, hs4], in1=fl,
                                                                             op=ALU.max), reads=(dn, colp), writes=(dn,))
                        c.op("dve", lambda e, hs4=hs4: e.reciprocal(out=dn.t[:, hs4], in_=dn.t[:, hs4]), reads=(dn,),
                             writes=(dn,))
                        c.op("dve", lambda e, hs4=hs4, A_=A_: e.tensor_tensor(out=sc_.t[:, hs4], in0=A_, in1=dn.t[:, hs4],
                                                                             op=ALU.mult), reads=(dn, colp), writes=(sc_,))
                        for q2 in range(2):
                            pI = psb[2 + q2]
                            h0 = hh * 4 + q2 * 2
                            c.op("dve", lambda e, pI=pI, h0=h0: e.tensor_tensor(
                                out=hb.t[:, h0 * 256:(h0 + 2) * 256].rearrange("p (h e) -> p h e", h=2),
                                in0=pI.t[:L, :512].rearrange("p (h e) -> p h e", h=2),
                                in1=sc_.t[:, h0:h0 + 2].unsqueeze(2).broadcast_to([L, 2, 256]), op=ALU.mult),
                                reads=(pI, sc_), writes=(hb,))
                        for hl in range(4):
                            h = hh * 4 + hl
                            pC = psb[4 + hl // 2]
                            co = (hl % 2) * 256
                            c.op("pe", lambda e, h=h, pC=pC, co=co, vc=vc: e.matmul(
                                pC.t[:, co:co + 256], lhsT=kp.t[:, h, :], rhs=vc.t[:, h * 256:(h + 1) * 256], start=True,
                                stop=True), reads=(kp, vc), writes=(pC,))
                            c.op("pe", lambda e, h=h: e.matmul(pD.t[:, 16 + h:17 + h], lhsT=kp.t[:, h, :],
                                                              rhs=C["ones_b"].t[:L, 0:1], start=True, stop=True),
                                 reads=(kp, C["ones_b"]), writes=(pD,))
                        for q2 in range(2):
                            pC = psb[4 + q2]
                            h0 = hh * 4 + q2 * 2
                            c.op("dve", lambda e, pC=pC, h0=h0: e.tensor_tensor(
                                out=CT.t[:, h0:h0 + 2, :], in0=CT.t[:, h0:h0 + 2, :],
                                in1=pC.t[:, :512].rearrange("p (h e) -> p h e", h=2), op=ALU.add), reads=(pC, CT), writes=(CT,))
                    c.op("dve", lambda e: e.tensor_tensor(out=nT.t[:, :], in0=nT.t[:, :], in1=pD.t[:, 16:24], op=ALU.add),
                         reads=(pD, nT), writes=(nT,))
                    c.op("pool", lambda e: e.tensor_tensor(out=hsq.t[:, :], in0=hb.t[:, :], in1=hb.t[:, :], op=ALU.mult),
                         reads=(hb,), writes=(hsq,))
                    c.op("dve", lambda e: e.tensor_reduce(out=ss.t[:, :], in_=hsq.t[:, :].rearrange("p (h e) -> p h e", h=H),
                                                          axis=mybir.AxisListType.X, op=ALU.add), reads=(hsq,), writes=(ss,))
                    c.op("act", lambda e: e.activation(out=ss.t[:, :], in_=ss.t[:, :], func=AF.Sqrt, bias=self.eps_t.t[:L, :],
                                                       scale=1.0 / 256.0), reads=(ss, self.eps_t), writes=(ss,))
                    c.op("dve", lambda e: e.reciprocal(out=ss.t[:, :], in_=ss.t[:, :]), reads=(ss,), writes=(ss,))
                    c.op("dve", lambda e: e.tensor_tensor(out=hb.t[:, :].rearrange("p (h e) -> p h e", h=H),
                                                          in0=hb.t[:, :].rearrange("p (h e) -> p h e", h=H),
                                                          in1=ss.t[:, :].unsqueeze(2).broadcast_to([L, H, 256]), op=ALU.mult),
                         reads=(hb, ss), writes=(hb,))
                    c.op("pool", lambda e: e.tensor_tensor(out=hb.t[:, :], in0=hb.t[:, :], in1=ngbc.t[:, :], op=ALU.mult),
                         reads=(hb, ngbc), writes=(hb,))
                    c.op("pool", lambda e, oc=oc: e.tensor_tensor(out=hb.t[:, :], in0=hb.t[:, :], in1=oc.t[:, :], op=ALU.mult),
                         reads=(hb, oc), writes=(hb,))
                    for g4 in range(4):
                        pT = psb[6] if g4 % 2 == 0 else psb[7]
                        for j4 in range(4):
                            kc = g4 * 4 + j4
                            c.op("pe", lambda e, kc=kc, j4=j4, pT=pT: e.transpose(
                                out=pT.t[:, 128 + j4 * L:128 + (j4 + 1) * L], in_=hb.t[:L, kc * 128:(kc + 1) * 128],
                                identity=ident.t[:L, :L]), reads=(hb, ident), writes=(pT,))
                        c.op("act", lambda e, g4=g4, pT=pT, c0=c0: e.copy(
                            out=hT_t.t[:, g4 * 4:(g4 + 1) * 4, c0:c0 + L],
                            in_=pT.t[:, 128:128 + 4 * L].rearrange("p (a b) -> p a b", b=L)), reads=(pT,), writes=(hT_t,))
                c.dma("sp", hT_d.ap().rearrange("(kc p) t -> p kc t", p=128)[:, :, t0:t0 + TT], hT_t.t[:, :, :], reads=(hT_t,))
                c.op("dve", lambda e: e.tensor_copy(out=Fb.t[:, 0:1], in_=Fb.t[:, TT:TT + 1]), reads=(Fb,), writes=(Fb,))
                c.op("dve", lambda e: e.tensor_copy(out=mb.t[:, 0:1], in_=mb.t[:, TT:TT + 1]), reads=(mb,), writes=(mb,))
            cout = c.sb("b_cout", [128, H, 2, 128], F32, es)
            for h in range(H):
                for eh in range(2):
                    pb = psb[(h * 2 + eh) % 2]
                    c.op("pe", lambda e, h=h, eh=eh, pb=pb: e.transpose(out=pb.t[:, :128], in_=CT.t[:, h, eh * 128:(eh + 1) * 128],
                                                                         identity=ident.t[:, :]), reads=(CT, ident), writes=(pb,))
                    c.op("dve", lambda e, h=h, eh=eh, pb=pb: e.tensor_copy(out=cout.t[:, h, eh, :], in_=pb.t[:, :128]),
                         reads=(pb,), writes=(cout,))
            c.dma("sp", C_out.ap().rearrange("h (eh el) d -> el h eh d", el=128), cout.t[:, :, :, :], reads=(cout,))
            nout = c.sb("b_nout", [8, 128], F32, es)
            c.op("pe", lambda e: e.transpose(out=psb[6].t[:8, :128], in_=nT.t[:, :], identity=ident.t[:, :]),
                 reads=(nT, ident), writes=(psb[6],))
            c.op("dve", lambda e: e.tensor_copy(out=nout.t[:, :], in_=psb[6].t[:8, :128]), reads=(psb[6],), writes=(nout,))
            c.dma("sp", n_out.ap(), nout.t[:, :], reads=(nout,))
            c.dma("sp", m_out.ap().rearrange("(h o) -> h o", o=1), mb.t[:, 0:1], reads=(mb,))
            c.barrier()
        self.outproj_stage(c, S, li, hT_d, KC, wout_b, wout_r, x_in, x_out, xT_out, W, ident, psb, "b")

    def outproj_stage(self, c, S, li, hT_d, nkc, wout_b, wout_r, x_in, x_out, xT_out, W, ident, psb, pre):
        P, NS, TT = S.P, S.NS, S.TT
        with ExitStack() as es:
            oT = c.sb(pre + "_oT", [128, nkc, TT], BF16, es)
            xres = c.sb(pre + "_xres", [P, NS, D], F32, es)
            wo = [c.sb(pre + "_wo%d" % i, [128, nkc, 256], BF16, es) for i in range(2)]
            lnb = self.ln_bufs(c, es, P, pre)
            gbc = self.load_bc(c, es, pre + "_g", W["ln1_g"].ap()[li:li + 1, :], P, D)
            bbc = self.load_bc(c, es, pre + "_b", W["ln1_b"].ap()[li:li + 1, :], P, D)
            cnts = {"wo": 0, "po": 0, "tc": [0]}
            for tile in range(S.ntiles):
                t0 = tile * TT
                c.dma("sp", oT.t[:, :, :], hT_d.ap().rearrange("(kc p) t -> p kc t", p=128)[:, :, t0:t0 + TT], writes=(oT,))
                c.dma("sp", xres.t[:, :, :], x_in.ap()[t0:t0 + TT, :].rearrange("(s p) d -> p s d", p=P), writes=(xres,))
                self.out_proj_ln(c, S, es, oT, nkc, wout_b, wout_r, xres, gbc, bbc, lnb, x_out, xT_out, tile, ident,
                                 psb[4:6], psb[6:8], wo, cnts)
            c.barrier()


    def c_stage(self, c, S, key, li, j, x_in, xT_in, x_out, xT_out, W, ident, psb):
        nc = self.nc
        P, NS, TT, T = S.P, S.NS, S.TT, S.T
        smp = key == "s"
        L = 32 if smp else 64
        NCH = TT // L
        H = 8
        qT_d = self.dscr("c_qT_" + key, [D, T], BF16)
        kT_d = self.dscr("c_kT_" + key, [D, T], BF16)
        kpp_d = self.dscr("c_kpp_" + key, [T, D], BF16)
        v_d = self.dscr("c_v_" + key, [T, 2 * D], BF16)
        g_d = self.dscr("c_g_" + key, [T, 2 * D], F32)
        hT_d = self.dscr("c_hT_" + key, [2 * D, T], BF16)
        S_out = self.c_outs[key]
        win_b, win_r = W["c_w_in_b"][j]
        wout_b, wout_r = W["c_w_out_b"][j]
        C = self.consts
        CI = self.cin
        with ExitStack() as es:
            xT = c.sb("c_xT", [128, KC, TT], BF16, es)
            ws_ = [c.sb("c_w%d" % i, [128, KC, 256], BF16, es) for i in range(4)]
            stg = [c.sb("c_stg%d" % i, [128, 2, TT], BF16, es) for i in range(2)]
            x1s = c.sb("c_x1s", [128, TT], F32, es)
            rt = [c.sb("c_rt%d" % i, [128, TT], F32, es) for i in range(4)]
            csf = c.sb("c_csf", [128, 4, TT], F32, es)
            cst = c.sb("c_cst", [P, NS, 2, 128], F32, es)
            tmf = c.sb("c_tmf", [P, NS, D], F32, es)
            tm1 = c.sb("c_tm1", [P, H, 128], F32, es)
            tm2 = c.sb("c_tm2", [P, H, 128], F32, es)
            o16 = c.sb("c_o16", [P, NS, D], BF16, es)
            dkt = c.sb("c_dkt", [P, H], F32, es)
            c.dma("sp", dkt.t[:, :], CI["c_dk_" + key].ap(), writes=(dkt,))
            cw = [0]
            pcnt = [0]
            for tile in range(S.ntiles):
                t0 = tile * TT
                c.dma("sp", xT.t[:, :, :], xT_in.ap().rearrange("(kc p) t -> p kc t", p=128)[:, :, t0:t0 + TT],
                      writes=(xT,))
                c.dma("sp", csf.t[:, 0, :], CI["c_cosT_" + key].ap()[:, t0:t0 + TT], writes=(csf,))
                c.dma("sp", csf.t[:, 1, :], CI["c_sinT_" + key].ap()[:, t0:t0 + TT], writes=(csf,))
                c.op("pool", lambda e: e.tensor_scalar(out=csf.t[:, 2:4, :], in0=csf.t[:, 0:2, :], scalar1=1.0 / 16.0,
                                                       scalar2=None, op0=ALU.mult), reads=(csf,), writes=(csf,))
                c.dma("sp", cst.t[:, :, 0, :], CI["c_cos_" + key].ap()[t0:t0 + TT, :].rearrange("(s p) f -> p s f", p=P),
                      writes=(cst,))
                c.dma("sp", cst.t[:, :, 1, :], CI["c_sin_" + key].ap()[t0:t0 + TT, :].rearrange("(s p) f -> p s f", p=P),
                      writes=(cst,))
                c.op("pool", lambda e: e.tensor_scalar(out=cst.t[:, :, :, :], in0=cst.t[:, :, :, :], scalar1=1.0 / 16.0,
                                                       scalar2=None, op0=ALU.mult), reads=(cst,), writes=(cst,))
                for part, dst, ci in ((0, qT_d, 0), (1, kT_d, 2)):
                    def ev(hc, pb, dst=dst, ci=ci):
                        if hc % 2 == 0:
                            c.op("act", lambda e: e.copy(out=x1s.t[:, :], in_=pb.t[:, :TT]), reads=(pb,), writes=(x1s,))
                            return
                        sg_ = stg[(hc // 2) % 2]
                        cos_, sin_ = csf.t[:, ci, :], csf.t[:, ci + 1, :]
                        c.op("pool", lambda e: e.tensor_tensor(out=rt[0].t[:, :], in0=x1s.t[:, :], in1=cos_, op=ALU.mult),
                             reads=(x1s, csf), writes=(rt[0],))
                        c.op("dve", lambda e: e.tensor_tensor(out=rt[1].t[:, :], in0=pb.t[:, :TT], in1=sin_, op=ALU.mult),
                             reads=(pb, csf), writes=(rt[1],))
                        c.op("pool", lambda e: e.tensor_tensor(out=sg_.t[:, 0, :], in0=rt[0].t[:, :], in1=rt[1].t[:, :],
                                                               op=ALU.subtract), reads=(rt[0], rt[1]), writes=(sg_,))
                        c.op("pool", lambda e: e.tensor_tensor(out=rt[2].t[:, :], in0=x1s.t[:, :], in1=sin_, op=ALU.mult),
                             reads=(x1s, csf), writes=(rt[2],))
                        c.op("dve", lambda e: e.tensor_tensor(out=rt[3].t[:, :], in0=pb.t[:, :TT], in1=cos_, op=ALU.mult),
                             reads=(pb, csf), writes=(rt[3],))
                        c.op("pool", lambda e: e.tensor_tensor(out=sg_.t[:, 1, :], in0=rt[2].t[:, :], in1=rt[3].t[:, :],
                                                               op=ALU.add), reads=(rt[2], rt[3]), writes=(sg_,))
                        h = hc // 2
                        c.dma("sp", dst.ap()[h * 256:(h + 1) * 256, t0:t0 + TT].rearrange("(a p) t -> p a t", p=128),
                              sg_.t[:, :, :], reads=(sg_,))
                    self.proj_fm(c, xT, TT, win_b, win_r, part * D, 16, ws_, cw, psb[0:2] + psb[4:8], pcnt, ev)
                def evk(s_, og, pb):
                    c.op("act", lambda e: e.copy(out=tmf.t[:P, s_, og * 256:(og + 1) * 256], in_=pb.t[:P, :256]),
                         reads=(pb,), writes=(tmf,))
                self.proj_tm(c, xT, S, win_b, win_r, D, D, ws_, cw, psb[2:4], pcnt, evk)
                for s_ in range(NS):
                    kv = tmf.t[:P, s_, :].rearrange("p (h a f) -> p h a f", h=H, a=2)
                    ov = o16.t[:P, s_, :].rearrange("p (h a f) -> p h a f", h=H, a=2)
                    cos_ = cst.t[:P, s_, 0, :].unsqueeze(1).broadcast_to([P, H, 128])
                    sin_ = cst.t[:P, s_, 1, :].unsqueeze(1).broadcast_to([P, H, 128])
                    c.op("dve", lambda e, kv=kv, cos_=cos_: e.tensor_tensor(out=tm1.t[:, :, :], in0=kv[:, :, 0, :], in1=cos_,
                                                                           op=ALU.mult), reads=(tmf, cst), writes=(tm1,))
                    c.op("pool", lambda e, kv=kv, sin_=sin_: e.tensor_tensor(out=tm2.t[:, :, :], in0=kv[:, :, 1, :], in1=sin_,
                                                                            op=ALU.mult), reads=(tmf, cst), writes=(tm2,))
                    c.op("dve", lambda e: e.tensor_tensor(out=tm1.t[:, :, :], in0=tm1.t[:, :, :], in1=tm2.t[:, :, :],
                                                          op=ALU.subtract), reads=(tm1, tm2), writes=(tm1,))
                    c.op("pool", lambda e, kv=kv, sin_=sin_: e.tensor_tensor(out=tm2.t[:, :, :], in0=kv[:, :, 0, :], in1=sin_,
                                                                            op=ALU.mult), reads=(tmf, cst, tm1), writes=(tm2,))
                    c.op("dve", lambda e, ov=ov: e.tensor_tensor(
                        out=ov[:, :, 0, :], in0=tm1.t[:, :, :], in1=dkt.t[:P, :].unsqueeze(2).broadcast_to([P, H, 128]),
                        op=ALU.mult), reads=(tm1, dkt), writes=(o16,))
                    c.op("dve", lambda e, kv=kv, cos_=cos_: e.tensor_tensor(out=tm1.t[:, :, :], in0=kv[:, :, 1, :], in1=cos_,
                                                                           op=ALU.mult), reads=(tmf, cst, o16), writes=(tm1,))
                    c.op("pool", lambda e: e.tensor_tensor(out=tm2.t[:, :, :], in0=tm2.t[:, :, :], in1=tm1.t[:, :, :],
                                                           op=ALU.add), reads=(tm1, tm2), writes=(tm2,))
                    c.op("pool", lambda e, ov=ov: e.tensor_tensor(
                        out=ov[:, :, 1, :], in0=tm2.t[:, :, :], in1=dkt.t[:P, :].unsqueeze(2).broadcast_to([P, H, 128]),
                        op=ALU.mult), reads=(tm2, dkt), writes=(o16,))
                c.dma("sp", kpp_d.ap()[t0:t0 + TT, :].rearrange("(s p) d -> p s d", p=P), o16.t[:, :, :], reads=(o16,))
                for hf in range(2):
                    def evv(s_, og, pb):
                        c.op("act", lambda e: e.copy(out=o16.t[:P, s_, og * 256:(og + 1) * 256], in_=pb.t[:P, :256]),
                             reads=(pb,), writes=(o16,))
                    self.proj_tm(c, xT, S, win_b, win_r, 2 * D + hf * D, D, ws_, cw, psb[2:4], pcnt, evv)
                    c.dma("sp", v_d.ap()[t0:t0 + TT, hf * D:(hf + 1) * D].rearrange("(s p) d -> p s d", p=P), o16.t[:, :, :],
                          reads=(o16,))
                for hf in range(2):
                    def evg(s_, og, pb):
                        c.op("act", lambda e: e.activation(out=tmf.t[:P, s_, og * 256:(og + 1) * 256], in_=pb.t[:P, :256],
                                                           func=AF.Silu), reads=(pb,), writes=(tmf,))
                    self.proj_tm(c, xT, S, win_b, win_r, 4 * D + hf * D, D, ws_, cw, psb[2:4], pcnt, evg)
                    c.dma("sp", g_d.ap()[t0:t0 + TT, hf * D:(hf + 1) * D].rearrange("(s p) d -> p s d", p=P), tmf.t[:, :, :],
                          reads=(tmf,))
            c.barrier()
        with ExitStack() as es:
            qT_t = c.sb("c_qTt", [128, 16, TT], BF16, es)
            kT_t = c.sb("c_kTt", [128, 16, TT], BF16, es)
            qd = c.sb("c_qd", [128, 16, L], BF16, es)
            kpc = [c.sb("c_kpc%d" % i, [L, D], BF16, es) for i in range(2)]
            v_c = [c.sb("c_vc%d" % i, [L, 2 * D], BF16, es) for i in range(2)]
            g_c = c.sb("c_gc", [L, 2 * D], F32, es)
            Sst = c.sb("c_S", [128, H, 2, 512], F32, es)
            Sb = c.sb("c_Sb", [128, H, 2, 512], BF16, es)
            Sp = c.sb("c_Sp", [L, H, L], BF16, es)
            ob = c.sb("c_ob", [L, 2 * D], F32, es)
            hT_cs = [c.sb("c_hTc%d" % i, [128, 32, L], BF16, es) for i in range(2)]
            gng = self.load_bc(c, es, "c_gng", W["c_gn_g"].ap()[j:j + 1, :], L, 2 * D)
            gnb = self.load_bc(c, es, "c_gnb", W["c_gn_b"].ap()[j:j + 1, :], L, 2 * D)
            dint = c.sb("c_dint", [L, H, L], F32, es)
            dq = c.sb("c_dq", [128, H, L], F32, es)
            st = c.sb("c_st", [L, H, 6], F32, es)
            mv = c.sb("c_mv", [L, H, 2], F32, es)
            rs = c.sb("c_rs", [L, H], F32, es)
            nmr = c.sb("c_nmr", [L, H], F32, es)
            c.dma("sp", dint.t[:, :, :], CI["c_dint_" + key].ap(), writes=(dint,))
            c.dma("sp", dq.t[:, :, :], CI["c_dq_" + key].ap(), writes=(dq,))
            if smp:
                c.dma("sp", Sst.t[:, :, :, :], W["state_c_S"].ap().rearrange("h (a dl) e -> dl h a e", dl=128), writes=(Sst,))
            else:
                c.op("pool", lambda e: e.memset(Sst.t[:, :, :, :], 0.0), writes=(Sst,))
            c.op("act", lambda e: e.copy(out=Sb.t[:, :, :, :], in_=Sst.t[:, :, :, :]), reads=(Sst,), writes=(Sb,))
            ds = self.c_decay_s[key]
            cc = 0
            ucnt = 0
            for tile in range(S.ntiles):
                t0 = tile * TT
                c.dma("sp", qT_t.t[:, :, :], qT_d.ap().rearrange("(kc p) t -> p kc t", p=128)[:, :, t0:t0 + TT], writes=(qT_t,))
                c.dma("sp", kT_t.t[:, :, :], kT_d.ap().rearrange("(kc p) t -> p kc t", p=128)[:, :, t0:t0 + TT], writes=(kT_t,))
                for ch in range(NCH):
                    c0 = ch * L
                    r0 = t0 + c0
                    kc_ = kpc[cc % 2]
                    vc = v_c[cc % 2]
                    cc += 1
                    c.dma("sp", kc_.t[:, :], kpp_d.ap()[r0:r0 + L, :], writes=(kc_,))
                    c.dma("sp", vc.t[:, :], v_d.ap()[r0:r0 + L, :], writes=(vc,))
                    c.dma("sp", g_c.t[:, :], g_d.ap()[r0:r0 + L, :], writes=(g_c,))
                    c.op("pool", lambda e, c0=c0: e.tensor_tensor(
                        out=qd.t[:, :, :].rearrange("p (h a) t -> p h a t", a=2),
                        in0=qT_t.t[:, :, c0:c0 + L].rearrange("p (h a) t -> p h a t", a=2),
                        in1=dq.t[:, :, :].unsqueeze(2).broadcast_to([128, H, 2, L]), op=ALU.mult), reads=(qT_t, dq), writes=(qd,))
                    for hh in range(2):
                        pS = psb[hh]
                        for hl in range(4):
                            h = hh * 4 + hl
                            for a in range(2):
                                c.op("pe", lambda e, h=h, hl=hl, a=a, pS=pS, c0=c0: e.matmul(
                                    pS.t[:L, hl * L:(hl + 1) * L], lhsT=kT_t.t[:, 2 * h + a, c0:c0 + L],
                                    rhs=qT_t.t[:, 2 * h + a, c0:c0 + L], start=(a == 0), stop=(a == 1)),
                                    reads=(kT_t, qT_t), writes=(pS,))
                        c.op("dve", lambda e, hh=hh, pS=pS: e.tensor_tensor(
                            out=Sp.t[:, hh * 4:(hh + 1) * 4, :], in0=pS.t[:L, :4 * L].rearrange("p (h t) -> p h t", h=4),
                            in1=dint.t[:, hh * 4:(hh + 1) * 4, :], op=ALU.mult), reads=(pS, dint), writes=(Sp,))
                    for h in range(H):
                        pO = psb[2 + h % 2]
                        c.op("pe", lambda e, h=h, pO=pO, vc=vc: e.matmul(pO.t[:L, :512], lhsT=Sp.t[:, h, :],
                                                                        rhs=vc.t[:, h * 512:(h + 1) * 512], start=True, stop=False),
                             reads=(Sp, vc), writes=(pO,))
                        for a in range(2):
                            c.op("pe", lambda e, h=h, a=a, pO=pO: e.matmul(pO.t[:L, :512], lhsT=qd.t[:, 2 * h + a, :],
                                                                          rhs=Sb.t[:, h, a, :], start=False, stop=(a == 1)),
                                 reads=(qd, Sb), writes=(pO,))
                        c.op("dve", lambda e, h=h, pO=pO: e.bn_stats(out=st.t[:, h, :], in_=pO.t[:L, :512]), reads=(pO,),
                             writes=(st,))
                        c.op("dve", lambda e, h=h: e.bn_aggr(out=mv.t[:, h, :], in_=st.t[:, h, :]), reads=(st,), writes=(mv,))
                        c.op("act", lambda e, h=h: e.activation(out=rs.t[:, h:h + 1], in_=mv.t[:, h, 1:2], func=AF.Sqrt,
                                                                bias=self.eps_t.t[:L, :], scale=1.0), reads=(mv, self.eps_t),
                             writes=(rs,))
                        c.op("dve", lambda e, h=h: e.reciprocal(out=rs.t[:, h:h + 1], in_=rs.t[:, h:h + 1]), reads=(rs,),
                             writes=(rs,))
                        c.op("dve", lambda e, h=h: e.scalar_tensor_tensor(out=nmr.t[:, h:h + 1], in0=mv.t[:, h, 0:1], scalar=-1.0,
                                                                          in1=rs.t[:, h:h + 1], op0=ALU.mult, op1=ALU.mult),
                             reads=(mv, rs), writes=(nmr,))
                        c.op("act", lambda e, h=h, pO=pO: e.activation(out=ob.t[:, h * 512:(h + 1) * 512], in_=pO.t[:L, :512],
                                                                       func=AF.Identity, bias=nmr.t[:, h:h + 1],
                                                                       scale=rs.t[:, h:h + 1]), reads=(pO, rs, nmr), writes=(ob,))
                    for h in range(H):
                        for a in range(2):
                            pU = psb[4 + ucnt % 2]
                            ucnt += 1
                            c.op("pe", lambda e, h=h, a=a, pU=pU, kc_=kc_, vc=vc: e.matmul(
                                pU.t[:, :512], lhsT=kc_.t[:, h * 256 + a * 128:h * 256 + (a + 1) * 128],
                                rhs=vc.t[:, h * 512:(h + 1) * 512], start=True, stop=True), reads=(kc_, vc), writes=(pU,))
                            c.op("dve", lambda e, h=h, a=a, pU=pU: e.scalar_tensor_tensor(
                                out=Sst.t[:, h, a, :], in0=Sst.t[:, h, a, :], scalar=float(ds[h]), in1=pU.t[:, :512],
                                op0=ALU.mult, op1=ALU.add), reads=(Sst, pU), writes=(Sst,))
                    c.op("act", lambda e: e.copy(out=Sb.t[:, :, :, :], in_=Sst.t[:, :, :, :]), reads=(Sst,), writes=(Sb,))
                    c.op("pool", lambda e: e.tensor_tensor(out=ob.t[:, :], in0=ob.t[:, :], in1=gng.t[:, :], op=ALU.mult),
                         reads=(ob, gng), writes=(ob,))
                    c.op("pool", lambda e: e.tensor_tensor(out=ob.t[:, :], in0=ob.t[:, :], in1=gnb.t[:, :], op=ALU.add),
                         reads=(ob, gnb), writes=(ob,))
                    c.op("pool", lambda e: e.tensor_tensor(out=ob.t[:, :], in0=ob.t[:, :], in1=g_c.t[:, :], op=ALU.mult),
                         reads=(ob, g_c), writes=(ob,))
                    hT_c = hT_cs[cc % 2]
                    for g4 in range(8):
                        pT = psb[6 + g4 % 2]
                        for j4 in range(4):
                            kc = g4 * 4 + j4
                            c.op("pe", lambda e, kc=kc, j4=j4, pT=pT: e.transpose(
                                out=pT.t[:, j4 * L:(j4 + 1) * L], in_=ob.t[:L, kc * 128:(kc + 1) * 128],
                                identity=ident.t[:L, :L]), reads=(ob, ident), writes=(pT,))
                        c.op("act", lambda e, g4=g4, pT=pT, hT_c=hT_c: e.copy(
                            out=hT_c.t[:, g4 * 4:(g4 + 1) * 4, :],
                            in_=pT.t[:, :4 * L].rearrange("p (a b) -> p a b", b=L)), reads=(pT,), writes=(hT_c,))
                    c.dma("sp", hT_d.ap().rearrange("(kc p) t -> p kc t", p=128)[:, :, r0:r0 + L], hT_c.t[:, :, :], reads=(hT_c,))
            c.dma("sp", S_out.ap().rearrange("h (a dl) e -> dl h a e", dl=128), Sst.t[:, :, :, :], reads=(Sst,))
            c.barrier()
        self.outproj_stage(c, S, li, hT_d, 32, wout_b, wout_r, x_in, x_out, xT_out, W, ident, psb, "c")

    def build(self):
        nc = self.nc
        TP, TS = self.TP, self.TS
        SP = Seg("p", TP, min(512, TP))
        SS = Seg("s", TS, TS)
        self.SP, self.SS = SP, SS
        W = {}
        self.W = W
        L = self.layers
        xp = self.din("x_prompt", [TP, D])
        xs = self.din("x_sample", [TS, D])
        ident_d = self.din("c_ident", [128, 128])
        for nm in ("ln1_g", "ln1_b", "ln2_g", "ln2_b"):
            W[nm] = self.din(nm, [DEPTH, D])
        yp = self.dout("y_prompt", [TP, D])
        ys = self.dout("y_sample", [TS, D])
        wlist = []
        if self.do_ffn:
            ffn_w_in = self.din("ffn_w_in", [DEPTH, D, 2 * FFN_H])
            ffn_w_out = self.din("ffn_w_out", [DEPTH, FFN_H, D])
            for li in L:
                wlist.append(("ffn_w_in_b", li, _sub(ffn_w_in, li), D, 2 * FFN_H))
                wlist.append(("ffn_w_out_b", li, _sub(ffn_w_out, li), FFN_H, D))
        mix = self.cfg.get("mixers", True)
        if mix and 0 in L:
            a_w_in = self.din("a_w_in", [1, D, 2 * D])
            a_w_out = self.din("a_w_out", [1, D, D])
            W["a_b_in"] = self.din("a_b_in", [1, 2 * D])
            W["a_vn_g"] = self.din("a_vn_g", [1, D])
            W["a_vn_b"] = self.din("a_vn_b", [1, D])
            W["a_w_s"] = self.din("a_w_s", [1, 8, 128, 128])
            W["a_b_s"] = self.din("a_b_s", [1, 8, 128])
            wlist.append(("a_w_in_b", 0, _sub(a_w_in, 0), D, 2 * D))
            wlist.append(("a_w_out_b", 0, _sub(a_w_out, 0), D, D))
            self.av_out = self.dout("new_a_v_sample", [TS, D])
        self.declare_more(W, wlist, mix, L)
        xtm = {"p": [self.dscr("xp_tm%d" % i, [TP, D], F32) for i in range(2)],
               "s": [self.dscr("xs_tm%d" % i, [TS, D], F32) for i in range(2)]}
        xT = {"p": [self.dscr("xp_T%d" % i, [D, TP], BF16) for i in range(2)],
              "s": [self.dscr("xs_T%d" % i, [D, TS], BF16) for i in range(2)]}
        with ExitStack() as es:
            c = Ctx(nc, es)
            self.c = c
            psb = [Buf(es.enter_context(nc.psum_tensor("ps%d" % i, [128, 512], F32))) for i in range(8)]
            ident = c.sb("ident", [128, 128], F32)
            c.dma("sp", ident.t[:, :], ident_d.ap(), writes=(ident,))
            self.eps_t = c.sb("eps", [128, 1], F32)
            c.op("pool", lambda e: e.memset(self.eps_t.t[:, :], LN_EPS), writes=(self.eps_t,))
            self.load_consts(c, es)
            def _first_use(item):
                key = item[0]
                if key.startswith("ffn"):
                    return item[1] * 2 + 1
                return {"a": 0, "b": 2, "c": 4, "d": 6}[key[0]]
            wlist.sort(key=_first_use)
            for key, li, src, K_, N_ in wlist:
                wb = self.dscr("%s%d" % (key, li), [K_, N_], BF16)
                r = self.convert_weight(c, src, wb, K_, N_)
                W.setdefault(key, {})[li] = (wb, r)
            self.make_xT(c, SP, xp, xT["p"][0], ident, psb)
            self.make_xT(c, SS, xs, xT["s"][0], ident, psb)
            cur = {"p": (xp, xT["p"][0]), "s": (xs, xT["s"][0])}
            flip = {"p": 1, "s": 1}
            nstage = len(L) * ((1 if mix else 0) + (1 if self.do_ffn else 0))
            done = 0
            for idx, li in enumerate(L):
                for S, key, yfin in ((SP, "p", yp), (SS, "s", ys)):
                    kinds = (["m"] if mix else []) + (["f"] if self.do_ffn else [])
                    for kind in kinds:
                        x_in, xT_in = cur[key]
                        f = flip[key]
                        last = (idx == len(L) - 1) and kind == kinds[-1]
                        x_out = yfin if last else xtm[key][f]
                        xT_out = None if last else xT[key][f]
                        if kind == "f":
                            self.ffn_stage(c, S, li, x_in, xT_in, x_out, xT_out, W, ident, psb)
                        else:
                            self.mixer_stage(c, S, key, li, x_in, xT_in, x_out, xT_out, W, ident, psb)
                        cur[key] = (x_out, xT_out)
                        flip[key] = 1 - f
            c.finish()
        return nc

    def declare_more(self, W, wlist, mix, L):
        TP, TS = self.TP, self.TS
        self.cin = {}
        for nm, shp in (("c_tri", [128, 128]), ("c_negm", [128, 4, 512])):
            self.cin[nm] = self.din(nm, shp)
        if mix and 1 in L:
            b_w_in = self.din("b_w_in", [1, D, 6160])
            b_w_out = self.din("b_w_out", [1, D, D])
            W["b_b_gates"] = self.din("b_b_gates", [1, 16])
            W["b_norm_g"] = self.din("b_norm_g", [1, D])
            W["state_b_C"] = self.din("state_b_C", [8, 256, 128])
            W["state_b_n"] = self.din("state_b_n", [8, 128])
            W["state_b_m"] = self.din("state_b_m", [8])
            wlist.append(("b_w_in_b", 0, _sub(b_w_in, 0), D, 6160))
            wlist.append(("b_w_out_b", 0, _sub(b_w_out, 0), D, D))
            self.b_outs = {"p": (self.dout("new_b_C_prompt", [8, 256, 128]), self.dout("new_b_n_prompt", [8, 128]),
                                 self.dout("new_b_m_prompt", [8])),
                           "s": (self.dout("new_b_C_sample", [8, 256, 128]), self.dout("new_b_n_sample", [8, 128]),
                                 self.dout("new_b_m_sample", [8]))}
        if mix and 2 in L:
            c_w_in = self.din("c_w_in", [1, D, 6 * D])
            c_w_out = self.din("c_w_out", [1, 2 * D, D])
            W["c_gn_g"] = self.din("c_gn_g", [1, 2 * D])
            W["c_gn_b"] = self.din("c_gn_b", [1, 2 * D])
            W["state_c_S"] = self.din("state_c_S", [8, 256, 512])
            wlist.append(("c_w_in_b", 0, _sub(c_w_in, 0), D, 6 * D))
            wlist.append(("c_w_out_b", 0, _sub(c_w_out, 0), 2 * D, D))
            self.c_outs = {"p": self.dout("new_c_S_prompt", [8, 256, 512]), "s": self.dout("new_c_S_sample", [8, 256, 512])}
            self.c_decay_s = {}
            for key, T_, L_ in (("p", TP, 64), ("s", TS, 32)):
                P_ = min(128, T_)
                self.cin["c_cosT_" + key] = self.din("c_cosT_" + key, [128, T_])
                self.cin["c_sinT_" + key] = self.din("c_sinT_" + key, [128, T_])
                self.cin["c_cos_" + key] = self.din("c_cos_" + key, [T_, 128])
                self.cin["c_sin_" + key] = self.din("c_sin_" + key, [T_, 128])
                self.cin["c_dk_" + key] = self.din("c_dk_" + key, [P_, 8])
                self.cin["c_dint_" + key] = self.din("c_dint_" + key, [L_, 8, L_])
                self.cin["c_dq_" + key] = self.din("c_dq_" + key, [128, 8, L_])
                self.c_decay_s[key] = ret_tables(T_, L_, 0)["decay_s"]
        if mix and 3 in L:
            d_w_in = self.din("d_w_in", [1, D, 3 * D + 16])
            d_w_out = self.din("d_w_out", [1, D, D])
            W["d_b_f"] = self.din("d_b_f", [1, 16])
            W["cache_d_k"] = self.din("cache_d_k", [PAST, D])
            W["cache_d_v"] = self.din("cache_d_v", [PAST, D])
            W["cache_d_logf"] = self.din("cache_d_logf", [PAST, 16])
            wlist.append(("d_w_in_b", 0, _sub(d_w_in, 0), D, 3 * D + 16))
            wlist.append(("d_w_out_b", 0, _sub(d_w_out, 0), D, D))
            self.d_outs = {"p": (self.dout("new_d_k_prompt", [TP, D]), self.dout("new_d_v_prompt", [TP, D]),
                                 self.dout("new_d_logf_prompt", [TP, 16])),
                           "s": (self.dout("new_d_k_sample", [TS, D]), self.dout("new_d_v_sample", [TS, D]),
                                 self.dout("new_d_logf_sample", [TS, 16]))}

    def load_consts(self, c, es):
        C = {}
        self.consts = C
        C["tri"] = c.sb("c_tri", [128, 128], F32)
        c.dma("sp", C["tri"].t[:, :], self.cin["c_tri"].ap(), writes=(C["tri"],))
        C["ones_f"] = c.sb("c_ones_f", [128, 128], F32)
        c.op("pool", lambda e: e.memset(C["ones_f"].t[:, :], 1.0), writes=(C["ones_f"],))
        C["ones_b"] = c.sb("c_ones_b", [128, 128], BF16)
        c.op("pool", lambda e: e.memset(C["ones_b"].t[:, :], 1.0), writes=(C["ones_b"],))
        C["one"] = c.sb("c_one", [128, 1], F32)
        c.op("pool", lambda e: e.memset(C["one"].t[:, :], 1.0), writes=(C["one"],))

    def mixer_stage(self, c, S, key, li, x_in, xT_in, x_out, xT_out, W, ident, psb):
        kind, j = li % 4, li // 4
        if kind == 0:
            self.a_stage(c, S, li, j, x_in, xT_in, x_out, xT_out, W, ident, psb,
                         v_out=self.av_out if key == "s" else None)
        elif kind == 1:
            self.b_stage(c, S, key, li, j, x_in, xT_in, x_out, xT_out, W, ident, psb)
        elif kind == 2:
            self.c_stage(c, S, key, li, j, x_in, xT_in, x_out, xT_out, W, ident, psb)
        elif kind == 3:
            self.d_stage(c, S, key, li, j, x_in, xT_in, x_out, xT_out, W, ident, psb)
        else:
            raise NotImplementedError

    def make_xT(self, c, S, x_tm, xT_out, ident, psb):
        P, NS, TT = S.P, S.NS, S.TT
        with ExitStack() as es:
            xin = [c.sb("m_x%d" % i, [P, D], F32, es) for i in range(2)]
            xTo = [c.sb("m_xT%d" % i, [128, KC, P], BF16, es) for i in range(2)]
            n = 0
            pcnt = 0
            for r0 in range(0, S.T, P):
                xi = xin[n % 2]
                xo = xTo[n % 2]
                n += 1
                c.dma("sp", xi.t[:, :], x_tm.ap()[r0:r0 + P, :], writes=(xi,))
                for g4 in range(4):
                    pb = psb[pcnt % 8]
                    pcnt += 1
                    for j in range(4):
                        kc = g4 * 4 + j
                        c.op("pe", lambda e, kc=kc, j=j, pb=pb, xi=xi: e.transpose(
                            out=pb.t[:, j * P:(j + 1) * P], in_=xi.t[:P, kc * 128:(kc + 1) * 128],
                            identity=ident.t[:P, :P]), reads=(xi, ident), writes=(pb,))
                    if g4 % 2 == 0:
                        c.op("act", lambda e, g4=g4, pb=pb, xo=xo: e.copy(
                            out=xo.t[:, g4 * 4:(g4 + 1) * 4, :P], in_=pb.t[:, :4 * P].rearrange("p (a b) -> p a b", b=P)),
                            reads=(pb,), writes=(xo,))
                    else:
                        c.op("dve", lambda e, g4=g4, pb=pb, xo=xo: e.tensor_copy(
                            out=xo.t[:, g4 * 4:(g4 + 1) * 4, :P], in_=pb.t[:, :4 * P].rearrange("p (a b) -> p a b", b=P)),
                            reads=(pb,), writes=(xo,))
                c.dma("sp", xT_out.ap().rearrange("(kc p) t -> p kc t", p=128)[:, :, r0:r0 + P], xo.t[:, :, :P],
                      reads=(xo,))
            c.barrier()


class _Sub:
    def __init__(self, t, li):
        self.t = t
        self.li = li

    def ap(self):
        return self.t.ap()[self.li]


def _sub(t, li):
    return _Sub(t, li)


def ret_tables(T, L, pos0):
    f = np.float32
    half = 128
    inv = (f(10000.0) ** (-np.arange(half, dtype=f) / f(half))).astype(f)
    pos = (pos0 + np.arange(T)).astype(f)
    ang = (pos[:, None] * inv[None, :]).astype(f)
    cos = np.cos(ang).astype(f)
    sin = np.sin(ang).astype(f)
    lg = np.log1p(-(f(2.0) ** (f(-5.0) - np.arange(8, dtype=f)))).astype(f)
    idx = np.arange(L, dtype=f)
    causal = idx[:, None] >= idx[None, :]
    dintra = np.where(causal[:, :, None], np.exp((idx[:, None] - idx[None, :])[:, :, None] * lg), 0.0).astype(f)
    dq = np.exp((idx + f(1.0))[:, None] * lg).astype(f)
    dk = np.exp((f(L) - f(1.0) - idx)[:, None] * lg).astype(f)
    dsv = np.exp(f(L) * lg).astype(f)
    P_ = min(128, T)
    return {"cosT": np.ascontiguousarray(cos.T), "sinT": np.ascontiguousarray(sin.T), "cos": cos, "sin": sin,
            "dk": np.ascontiguousarray(np.tile(dk, (P_ // L, 1))),
            "dint": np.ascontiguousarray(np.transpose(dintra, (1, 2, 0))),
            "dq": np.ascontiguousarray(np.broadcast_to(np.transpose(dq, (1, 0))[None], (128, 8, L))),
            "decay_s": dsv}


def const_inputs_c(TP, TS):
    out = {}
    for key, T_, L_, p0 in (("p", TP, 64, 0), ("s", TS, 32, PAST)):
        t = ret_tables(T_, L_, p0)
        for nm in ("cosT", "sinT", "cos", "sin", "dk", "dint", "dq"):
            out["c_%s_%s" % (nm, key)] = t[nm]
    return out


def const_inputs():
    k = np.arange(128)[:, None, None]
    jj = np.arange(4)[None, :, None]
    q = np.arange(512)[None, None, :]
    negm = np.where(jj * 128 + k <= q, 0.0, -30000.0).astype(np.float32)
    tri = (np.arange(128)[:, None] <= np.arange(128)[None, :]).astype(np.float32)
    return {"c_ident": np.eye(128, dtype=np.float32), "c_tri": tri, "c_negm": negm}


_CACHE = {}


def kernel(**inputs):
    TP, TS, NCORE = 8192, 32, 8
    inp = {k: np.asarray(v) for k, v in inputs.items()}
    if "nc" not in _CACHE:
        prog = Prog({"TP": TP, "TS": TS, "layers": [0, 1, 2, 3], "ffn": True, "mixers": True})
        _CACHE["nc"] = prog.build()
    nc = _CACHE["nc"]
    consts = {**const_inputs(), **const_inputs_c(TP, TS)}
    shared = {}
    for nm in ("a_w_in", "a_b_in", "a_vn_g", "a_vn_b", "a_w_s", "a_b_s", "a_w_out", "b_w_in", "b_b_gates", "b_norm_g",
               "b_w_out", "c_w_in", "c_gn_g", "c_gn_b", "c_w_out", "d_w_in", "d_b_f", "d_w_out", "ffn_w_in", "ffn_w_out",
               "ln1_g", "ln1_b", "ln2_g", "ln2_b"):
        shared[nm] = np.ascontiguousarray(inp[nm], dtype=np.float32)
    in_maps = []
    for core in range(NCORE):
        b = core // 2
        m = dict(consts)
        m.update(shared)
        m["x_prompt"] = np.ascontiguousarray(inp["x_prompt"][b])
        m["x_sample"] = np.ascontiguousarray(inp["x_sample"][core])
        m["state_b_C"] = np.ascontiguousarray(inp["state_b_C"][0, core])
        m["state_b_n"] = np.ascontiguousarray(inp["state_b_n"][0, core])
        m["state_b_m"] = np.ascontiguousarray(inp["state_b_m"][0, core])
        m["state_c_S"] = np.ascontiguousarray(inp["state_c_S"][0, core])
        m["cache_d_k"] = np.ascontiguousarray(inp["cache_d_k"][0, core]).reshape(PAST, D)
        m["cache_d_v"] = np.ascontiguousarray(inp["cache_d_v"][0, core]).reshape(PAST, D)
        m["cache_d_logf"] = np.ascontiguousarray(inp["cache_d_logf"][0, core])
        in_maps.append(m)
    res = run_bass_kernel_spmd(nc, in_maps, core_ids=list(range(NCORE)))
    R = res.results
    ev = [0, 2, 4, 6]

    def gp(name, shape):
        return np.stack([R[cidx][name].reshape(shape) for cidx in ev])

    def gs(name, shape):
        return np.stack([R[cidx][name].reshape(shape) for cidx in range(NCORE)])

    outs = (
        gp("y_prompt", (TP, D)),
        gs("y_sample", (TS, D)),
        gs("new_a_v_sample", (TS, D))[None],
        gp("new_b_C_prompt", (8, 256, 128))[None],
        gp("new_b_n_prompt", (8, 128))[None],
        gp("new_b_m_prompt", (8,))[None],
        gs("new_b_C_sample", (8, 256, 128))[None],
        gs("new_b_n_sample", (8, 128))[None],
        gs("new_b_m_sample", (8,))[None],
        gp("new_c_S_prompt", (8, 256, 512))[None],
        gs("new_c_S_sample", (8, 256, 512))[None],
        gp("new_d_k_prompt", (TP, 16, 128))[None],
        gp("new_d_v_prompt", (TP, 16, 128))[None],
        gp("new_d_logf_prompt", (TP, 16))[None],
        gs("new_d_k_sample", (TS, 16, 128))[None],
        gs("new_d_v_sample", (TS, 16, 128))[None],
        gs("new_d_logf_sample", (TS, 16))[None],
    )
    return tuple(np.ascontiguousarray(o, dtype=np.float32) for o in outs)
```
